# Optimizing a Trainium2 kernel written in Bass

```python
import math
import jax, jax.numpy as jnp
from jax import lax
import numpy as np

D_MODEL = 1024
BATCH = 4
SEQ = 8192
DEPTH = 1
DEC_BATCH = 16
DEC_SEQ = 64
PAST_LEN = 2048

CHUNK = 64
Q_BLOCK = 128
MIX_WIDTH = D_MODEL
ATTN_WIDTH = MIX_WIDTH // 2
CONV_WIDTH = MIX_WIDTH - ATTN_WIDTH
N_HEADS = 4
HEAD_V = ATTN_WIDTH // N_HEADS
HEAD_QK = HEAD_V // 2
CONV_K = 3
D_FF = 4 * D_MODEL
N_BUCKETS = 32
MAX_DISTANCE = 128
LN_EPS = 1e-5
SUBLN_EPS = 1e-5
DN_ALPHA = (2.0 * DEPTH) ** 0.25
DN_BETA = (8.0 * DEPTH) ** -0.25
IN_WIDTH = 3 * ATTN_WIDTH + 3 * CONV_WIDTH

kernel_name = "hybrid_diffattn_shortconv_stream_step"


def _lambda_init(layer):
    return 0.8 - 0.6 * math.exp(-0.3 * layer)


def layer_norm(x, g, b):
    xf = x.astype(jnp.float32)
    mu = jnp.mean(xf, axis=-1, keepdims=True)
    var = jnp.mean(jnp.square(xf - mu), axis=-1, keepdims=True)
    return ((xf - mu) * lax.rsqrt(var + LN_EPS) * g + b).astype(x.dtype)


def rms_norm(x, g):
    xf = x.astype(jnp.float32)
    return xf * lax.rsqrt(jnp.mean(jnp.square(xf), axis=-1, keepdims=True) + SUBLN_EPS) * g


def rel_bucket(rel):
    half = N_BUCKETS // 2
    max_exact = half // 2
    ret = jnp.where(rel > 0, half, 0)
    n = jnp.abs(rel)
    nf = jnp.maximum(n, 1).astype(jnp.float32)
    large = max_exact + (jnp.log(nf / max_exact) / math.log(MAX_DISTANCE / max_exact)
                         * (half - max_exact)).astype(jnp.int32)
    large = jnp.minimum(large, half - 1)
    return ret + jnp.where(n < max_exact, n, large)


def diff_attention(q, k, v, q_pos, k_pos, rel_bias, lam):
    scale = HEAD_QK ** -0.5
    bias = jnp.transpose(rel_bias[rel_bucket(k_pos[None, :] - q_pos[:, None])], (2, 0, 1)).astype(jnp.float32)
    visible = (k_pos[None, :] // CHUNK) <= (q_pos[:, None] // CHUNK)

    def probs(qi, ki):
        s = jnp.einsum('bqhd,bkhd->bhqk', qi, ki, preferred_element_type=jnp.float32) * scale + bias
        return jax.nn.softmax(jnp.where(visible, s, -jnp.inf), axis=-1)

    p = probs(q[..., :HEAD_QK], k[..., :HEAD_QK]) - lam * probs(q[..., HEAD_QK:], k[..., HEAD_QK:])
    return jnp.einsum('bhqk,bkhd->bqhd', p.astype(v.dtype), v, preferred_element_type=jnp.float32)


def prompt_attention(q, k, v, rel_bias, lam):
    b, s = q.shape[0], q.shape[1]
    nb = s // Q_BLOCK
    qb = jnp.moveaxis(q.reshape(b, nb, Q_BLOCK, N_HEADS, HEAD_V), 1, 0)
    k_pos = jnp.arange(s, dtype=jnp.int32)

    def one_block(args):
        q_blk, i = args
        q_pos = i * Q_BLOCK + jnp.arange(Q_BLOCK, dtype=jnp.int32)
        return diff_attention(q_blk, k, v, q_pos, k_pos, rel_bias, lam)

    out = lax.map(one_block, (qb, jnp.arange(nb, dtype=jnp.int32)))
    return jnp.moveaxis(out, 0, 1).reshape(b, s, N_HEADS, HEAD_V)


def project_in(x, w_in):
    z = jnp.einsum('btd,de->bte', x, w_in)
    a, c = ATTN_WIDTH, CONV_WIDTH
    q, k, v, gb, gc, h = jnp.split(z, [a, 2 * a, 3 * a, 3 * a + c, 3 * a + 2 * c], axis=-1)
    bsz, t = x.shape[0], x.shape[1]
    heads = lambda y: y.reshape(bsz, t, N_HEADS, HEAD_V)
    return heads(q), heads(k), heads(v), gb, gc, h


def causal_conv(u_ext, w):
    t = u_ext.shape[1] - (CONV_K - 1)
    return sum(u_ext[:, j:j + t] * w[j] for j in range(CONV_K))


def finish(x, attn, conv, subln_g, lam_init, w_out, ln1_g, ln1_b, w_ff1, w_ff2, ln2_g, ln2_b):
    bsz, t = x.shape[0], x.shape[1]
    a = (rms_norm(attn, subln_g) * (1.0 - lam_init)).reshape(bsz, t, ATTN_WIDTH).astype(x.dtype)
    y = jnp.einsum('bte,ed->btd', jnp.concatenate([a, conv.astype(x.dtype)], axis=-1), w_out)
    x = layer_norm(DN_ALPHA * x + y, ln1_g, ln1_b)
    hdn = jnp.square(jax.nn.relu(jnp.einsum('btd,df->btf', x, w_ff1)))
    return layer_norm(DN_ALPHA * x + jnp.einsum('btf,fd->btd', hdn, w_ff2), ln2_g, ln2_b)


def setup_inputs(seed: int = 0) -> dict:
    key = jax.random.key(seed)
    ks = jax.random.split(key, 32)
    f32 = jnp.float32
    nrm = lambda k, shape, s: jax.random.normal(k, shape, f32) * s
    sd = D_MODEL ** -0.5
    w_in = jnp.concatenate([
        nrm(ks[0], (DEPTH, D_MODEL, ATTN_WIDTH), sd),
        nrm(ks[1], (DEPTH, D_MODEL, ATTN_WIDTH), sd),
        nrm(ks[2], (DEPTH, D_MODEL, ATTN_WIDTH), sd * DN_BETA),
        nrm(ks[3], (DEPTH, D_MODEL, CONV_WIDTH), sd),
        nrm(ks[4], (DEPTH, D_MODEL, CONV_WIDTH), sd),
        nrm(ks[5], (DEPTH, D_MODEL, CONV_WIDTH), sd * DN_BETA),
    ], axis=-1)
    return {
        "x_prompt": nrm(ks[6], (BATCH, SEQ, D_MODEL), 1.0),
        "x_sample": nrm(ks[7], (DEC_BATCH, DEC_SEQ, D_MODEL), 1.0),
        "cache_k": nrm(ks[8], (DEPTH, DEC_BATCH, PAST_LEN, N_HEADS, HEAD_V), 1.0),
        "cache_v": nrm(ks[9], (DEPTH, DEC_BATCH, PAST_LEN, N_HEADS, HEAD_V), DN_BETA),
        "cache_conv": nrm(ks[10], (DEPTH, DEC_BATCH, CONV_K - 1, CONV_WIDTH), DN_BETA),
        "ln0_g": 1.0 + nrm(ks[11], (D_MODEL,), 0.02),
        "ln0_b": nrm(ks[12], (D_MODEL,), 0.02),
        "rel_bias": nrm(ks[13], (N_BUCKETS, N_HEADS), 0.5),
        "w_in": w_in,
        "conv_w": nrm(ks[14], (DEPTH, CONV_K, CONV_WIDTH), CONV_K ** -0.5),
        "lambda_q1": nrm(ks[15], (DEPTH, HEAD_QK), 0.1),
        "lambda_k1": nrm(ks[16], (DEPTH, HEAD_QK), 0.1),
        "lambda_q2": nrm(ks[17], (DEPTH, HEAD_QK), 0.1),
        "lambda_k2": nrm(ks[18], (DEPTH, HEAD_QK), 0.1),
        "subln_g": 1.0 + nrm(ks[19], (DEPTH, HEAD_V), 0.02),
        "w_out": nrm(ks[20], (DEPTH, MIX_WIDTH, D_MODEL), MIX_WIDTH ** -0.5 * DN_BETA),
        "ln1_g": 1.0 + nrm(ks[21], (DEPTH, D_MODEL), 0.02),
        "ln1_b": nrm(ks[22], (DEPTH, D_MODEL), 0.02),
        "w_ff1": nrm(ks[23], (DEPTH, D_MODEL, D_FF), sd * DN_BETA),
        "w_ff2": nrm(ks[24], (DEPTH, D_FF, D_MODEL), D_FF ** -0.5 * DN_BETA),
        "ln2_g": 1.0 + nrm(ks[25], (DEPTH, D_MODEL), 0.02),
        "ln2_b": nrm(ks[26], (DEPTH, D_MODEL), 0.02),
    }


def reference(x_prompt, x_sample, cache_k, cache_v, cache_conv, ln0_g, ln0_b, rel_bias, w_in, conv_w,
              lambda_q1, lambda_k1, lambda_q2, lambda_k2, subln_g, w_out, ln1_g, ln1_b,
              w_ff1, w_ff2, ln2_g, ln2_b):
    f32 = jnp.float32
    xp = layer_norm(x_prompt, ln0_g, ln0_b)
    xs = layer_norm(x_sample, ln0_g, ln0_b)
    past = cache_k.shape[2]
    t_s = x_sample.shape[1]
    q_pos_s = past + jnp.arange(t_s, dtype=jnp.int32)
    k_pos_s = jnp.arange(past + t_s, dtype=jnp.int32)
    kp_l, vp_l, cp_l, ks_l, vs_l, cs_l = [], [], [], [], [], []
    for l in range(DEPTH):
        lam_init = _lambda_init(l)
        lam = (jnp.exp(jnp.sum(lambda_q1[l].astype(f32) * lambda_k1[l].astype(f32)))
               - jnp.exp(jnp.sum(lambda_q2[l].astype(f32) * lambda_k2[l].astype(f32))) + lam_init)
        ffn_args = (subln_g[l], lam_init, w_out[l], ln1_g[l], ln1_b[l], w_ff1[l], w_ff2[l], ln2_g[l], ln2_b[l])

        q, k, v, gb, gc, h = project_in(xp, w_in[l])
        attn = prompt_attention(q, k, v, rel_bias, lam)
        u_ext = jnp.pad(gc * h, ((0, 0), (CONV_K - 1, 0), (0, 0)))
        conv = gb * causal_conv(u_ext, conv_w[l])
        kp_l.append(k)
        vp_l.append(v)
        cp_l.append(u_ext[:, -(CONV_K - 1):])
        xp = finish(xp, attn, conv, *ffn_args)

        q, k, v, gb, gc, h = project_in(xs, w_in[l])
        k_all = jnp.concatenate([cache_k[l].astype(k.dtype), k], axis=1)
        v_all = jnp.concatenate([cache_v[l].astype(v.dtype), v], axis=1)
        attn = diff_attention(q, k_all, v_all, q_pos_s, k_pos_s, rel_bias, lam)
        u_ext = jnp.concatenate([cache_conv[l].astype(h.dtype), gc * h], axis=1)
        conv = gb * causal_conv(u_ext, conv_w[l])
        ks_l.append(k)
        vs_l.append(v)
        cs_l.append(u_ext[:, -(CONV_K - 1):])
        xs = finish(xs, attn, conv, *ffn_args)

    return (xp, xs, jnp.stack(kp_l), jnp.stack(vp_l), jnp.stack(cp_l),
            jnp.stack(ks_l), jnp.stack(vs_l), jnp.stack(cs_l))
```

```python
import math
import os
from contextlib import ExitStack

import numpy as np
import concourse.bass as bass
import concourse.mybir as mybir
from concourse.bass_utils import run_bass_kernel_spmd

F32 = mybir.dt.float32
BF16 = mybir.dt.bfloat16
AF = mybir.ActivationFunctionType
ALU = mybir.AluOpType
AX = mybir.AxisListType

D = 1024
GT = 512
DEC = 64
LAM_INIT = 0.8 - 0.6 * math.exp(-0.3 * 0)
ALPHA = (2.0 * 1) ** 0.25
EPS = 1e-5
NEG = -30000.0


class Sched:
    def __init__(self):
        self.ops = []
        self.lastw = {}
        self.readers = {}
        self.dma_cnt = {}
        self.marks = {}

    def mark(self, n):
        self.marks[n] = len(self.ops)

    def add(self, eng, fn, r=(), w=(), dma=False, key=None):
        idx = len(self.ops)
        deps = set()
        for b in r:
            if b in self.lastw:
                deps.add(self.lastw[b])
        for b in w:
            if b in self.lastw:
                deps.add(self.lastw[b])
            deps.update(self.readers.get(b, ()))
        op = dict(eng=eng, fn=fn, deps=deps, dma=dma)
        if dma:
            self.dma_cnt[key] = self.dma_cnt.get(key, 0) + 1
            op["key"] = key
            op["val"] = 16 * self.dma_cnt[key]
        self.ops.append(op)
        for b in r:
            lst = self.readers.setdefault(b, [])
            if not dma:
                for i in range(len(lst) - 1, -1, -1):
                    o = self.ops[lst[i]]
                    if (not o["dma"]) and o["eng"] == eng:
                        del lst[i]
            lst.append(idx)
        for b in w:
            self.lastw[b] = idx
            self.readers[b] = []
        return idx

    def barrier(self):
        allb = list(set(self.lastw) | set(self.readers))
        for eng in ("pe", "act", "dve", "pool", "sp"):
            self.add(eng, None, r=[], w=allb)
        self.bar_ids = []

    def emit(self, nc, es):
        lim = int(os.environ.get("KSTOP", "99"))
        ops = self.ops
        if lim >= 1000:
            self.marks[lim] = lim - 1000
        if os.environ.get("KVERB"):
            print("marks", self.marks, "nops", len(ops))
            for i in range(self.marks.get(2, 0), self.marks.get(3, 0)):
                print(i, ops[i]["eng"], ops[i]["dma"], ops[i].get("key"))
        if lim in self.marks:
            ops = ops[:self.marks[lim]]
            self.dma_cnt = {}
            for o in ops:
                if o["dma"]:
                    self.dma_cnt[o["key"]] = self.dma_cnt.get(o["key"], 0) + 1
        engs = ("pe", "act", "dve", "pool", "sp")
        for i, op in enumerate(ops):
            cmax = {}
            dm = {}
            for d in op["deps"]:
                o = ops[d]
                if o["dma"]:
                    k = o["key"]
                    dm[k] = max(dm.get(k, 0), o["val"])
                else:
                    if o["eng"] == "pe" and op["eng"] == "pe" and not op["dma"]:
                        continue
                    cmax[o["eng"]] = max(cmax.get(o["eng"], -1), d)
            op["cdeps"] = cmax
            op["ddeps"] = dm
        mile = set()
        for op in ops:
            mile.update(op["cdeps"].values())
        cnt = {e: 0 for e in engs}
        for i, op in enumerate(ops):
            if i in mile:
                cnt[op["eng"]] += 1
                op["ms"] = cnt[op["eng"]]
        esem = {e: es.enter_context(nc.semaphore("s_" + e)) for e in engs}
        dsem = {k: es.enter_context(nc.semaphore("d_%d" % n)) for n, k in enumerate(self.dma_cnt)}
        block = es.enter_context(nc.Block())

        def run(engname, engobj):
            known = {}
            for i, op in enumerate(ops):
                if op["eng"] != engname:
                    continue
                for e2, d in op["cdeps"].items():
                    v = ops[d]["ms"]
                    if known.get(("c", e2), 0) < v:
                        engobj.wait_ge(esem[e2], v)
                        known[("c", e2)] = v
                for k, v in op["ddeps"].items():
                    if known.get(("d", k), 0) < v:
                        engobj.wait_ge(dsem[k], v)
                        known[("d", k)] = v
                if op["fn"] is None:
                    ins = engobj.nop() if ("ms" in op) else None
                else:
                    ins = op["fn"](engobj)
                if op["dma"]:
                    ins.then_inc(dsem[op["key"]], 16)
                elif "ms" in op:
                    ins.then_inc(esem[engname], 1)
            if engname == "sp":
                for k, n in self.dma_cnt.items():
                    engobj.wait_ge(dsem[k], 16 * n)

        @block.tensor
        def _(e):
            run("pe", e)

        @block.scalar
        def _(e):
            run("act", e)

        @block.vector
        def _(e):
            run("dve", e)

        @block.gpsimd
        def _(e):
            run("pool", e)

        @block.sync
        def _(e):
            run("sp", e)


def _bucket_table():
    rel = (127 - np.arange(384)).astype(np.int32)
    half = 16
    max_exact = 8
    ret = np.where(rel > 0, half, 0)
    n = np.abs(rel)
    nf = np.maximum(n, 1).astype(np.float32)
    large = max_exact + (np.log(nf / np.float32(max_exact)) / np.float32(math.log(128 / max_exact))
                         * np.float32(half - max_exact)).astype(np.int32)
    large = np.minimum(large, half - 1)
    return np.asarray(ret + np.where(n < max_exact, n, large))


def _order(half, ng=16):
    o = []
    for m in range(ng // 2):
        own_first = ((m % 2) == 0) == (half == 0)
        if own_first:
            o += [2 * m + 1, 2 * m]
        else:
            o += [2 * m, 2 * m + 1]
    return o


class _Stop(Exception):
    pass


def build_nc(NG=16, PAST=2048):
    nc = bass.Bass("TRN2", target_bir_lowering=False)
    S = Sched()
    SEQ = NG * GT
    NSLOT = NG // 2
    NCB = PAST // 128
    SOFF = PAST + 256
    SBLK = NCB + 2
    SK0 = 2 * SOFF + 128
    VNEW = SK0 // 128
    SST = NSLOT * 4
    assert SK0 + 128 <= SEQ

    def din(name, shape):
        return nc.dram_tensor(name, list(shape), F32, kind="ExternalInput").ap()

    def dout(name, shape):
        return nc.dram_tensor(name, list(shape), F32, kind="ExternalOutput").ap()

    xp = din("xp", [SEQ, D])
    xh = din("xh", [NSLOT, 2, D])
    xs = din("xs", [128, D])
    ck = din("ck", [2, PAST, 512])
    cv = din("cv", [2, PAST, 512])
    cc = din("cc", [128, 4, 2, 2])
    w_in = din("w_in", [D, 3072])
    w_out = din("w_out", [D, D])
    w1 = din("w1", [D, 4096])
    w2 = din("w2", [4096, D])
    lnT_d = din("lnT", [128, 6, 8])
    lnB_d = din("lnB", [128, 6, D])
    convw_d = din("convw", [128, 4, 3])
    subg_d = din("subg", [128, 1])
    lamv_d = din("lamv", [128, 4, 64])
    rb_d = din("rb", [32, 4])
    oh_d = din("oh", [32, 384])
    flags_d = din("flags", [128, 16])

    yp = dout("yp", [NSLOT * GT, D])
    ys = dout("ys", [128, D])
    kp = dout("kp", [NSLOT * GT, 512])
    vp = dout("vp", [NSLOT * GT, 512])
    cp = dout("cp", [128, 4, 2])
    ksn = dout("ksn", [128, 512])
    vsn = dout("vsn", [128, 512])
    csn = dout("csn", [128, 4, 2, 2])

    wsc = nc.dram_tensor("wsc", [4, 384], F32, kind="Internal")
    dbg_cat = nc.dram_tensor("dbg_cat", [NSLOT, 128, 8, 512], BF16, kind="ExternalOutput").ap() if os.environ.get("KDBG") else None
    xsc = nc.dram_tensor("xsc", [NSLOT * GT + 128, D], F32, kind=("ExternalOutput" if os.environ.get("KDBG") else "Internal")).ap()

    es = ExitStack()
    with es:
        def sb(name, shape, dt):
            return es.enter_context(nc.sbuf_tensor("sb_" + name, list(shape), dt))

        big = sb("big", [128, 65536], BF16)
        KTW = max(SEQ, 8192)
        KT = big[:, 0:32768].rearrange("p (h t) -> p h t", h=4)
        Vt = big[:, 32768:65536].rearrange("p (b c) -> p b c", c=512)
        W1s = big[:, 0:32768].rearrange("p (k f) -> p k f", k=8)
        W2s = big[:, 32768:65536].rearrange("p (k c) -> p k c", c=1024)
        ident = sb("ident", [128, 128], BF16)
        Jm = sb("Jm", [128, 128], BF16)
        ones = sb("ones", [128, 128], BF16)
        sel = sb("sel", [64, 2, 128], F32)
        lnT = sb("lnT", [128, 6, 8], F32)
        convw = sb("convw", [128, 4, 3], F32)
        gsc = sb("gsc", [128, 1], F32)
        nlam = sb("nlam", [128, 1], F32)
        epsc = sb("epsc", [128, 1], F32)
        flags = sb("flags", [128, 16], F32)
        st0 = sb("st0", [128, SST + 1, 2], F32)
        pp = [es.enter_context(nc.psum_tensor("psum%d" % i, [128, 1024], F32)) for i in range(4)]
        ps = [pp[i // 2][:, (i % 2) * 512:(i % 2 + 1) * 512] for i in range(8)]
        PS = ["ps%d" % i for i in range(8)]

        def ps_bf(i):
            return ps[i][:, :].bitcast(BF16)

        pe = lambda fn, r=(), w=(): S.add("pe", fn, r, w)
        act = lambda fn, r=(), w=(): S.add("act", fn, r, w)
        dve = lambda fn, r=(), w=(): S.add("dve", fn, r, w)
        pool = lambda fn, r=(), w=(): S.add("pool", fn, r, w)

        def dma(q, out, in_, r, w, key, **kw):
            S.add(q, lambda e: e.dma_start(out=out, in_=in_, **kw), r, w, dma=True, key=key)

        esA = ExitStack()
        esA.__enter__()

        def sbA(name, shape, dt):
            return esA.enter_context(nc.sbuf_tensor("sb_" + name, list(shape), dt))

        GB0 = sbA("GB0", [128, 2, D], F32)
        H01 = sbA("H01", [128, 4, 256], BF16)
        HS = sbA("HS", [128, 2, 4, 128], BF16)
        Mf = sbA("Mf", [128, 2, 512], BF16)
        xst = [sbA("xst%d" % i, [128, D], F32) for i in range(2)]
        xnb = [sbA("xnb0", [128, D], BF16)] * 2
        x0T = sbA("x0T", [128, 8, 512], BF16)
        xhT = sbA("xhT", [128, 8, 2], BF16)
        QT = sbA("QT", [128, 4, 512], BF16)
        catT = sbA("catT", [128, 8, 512], BF16)
        wk = [sbA("wk%d" % i, [128, 520], F32) for i in range(4)]
        PT = [sbA("PT%d" % i, [128, 1024], BF16) for i in range(3)]
        sqb = sbA("sqb", [128, 512], BF16)
        NWS = 4
        wsl = [sbA("wsl%d" % i, [128, 8, 256], BF16) for i in range(NWS)]
        ost = [sbA("ost%d" % i, [128, 256], F32) for i in range(2)]
        stt = sbA("stt", [128, 20], F32)
        uh = sbA("uh", [128, 4, 2], F32)
        uhh = sbA("uhh", [128, 4, 2], F32)
        setup = xst[1]
        ws_state = [0]
        ost_state = [0]
        x_state = [0]

        pool(lambda e: e.memset(setup[:, 0:128], 0.0), w=["setup"])
        pool(lambda e: e.affine_select(out=setup[:, 0:128], in_=setup[:, 0:128], compare_op=ALU.not_equal,
                                       fill=1.0, base=0, pattern=[[-1, 128]], channel_multiplier=1),
             r=["setup"], w=["setup"])
        dve(lambda e: e.tensor_copy(out=ident[:, :], in_=setup[:, 0:128]), r=["setup"], w=["ident"])
        pool(lambda e: e.memset(setup[:, 128:256], 0.0), r=["setup"], w=["setup"])
        pool(lambda e: e.affine_select(out=setup[:, 128:256], in_=setup[:, 128:256], compare_op=ALU.not_equal,
                                       fill=1.0, base=-127, pattern=[[1, 128]], channel_multiplier=1),
             r=["setup"], w=["setup"])
        dve(lambda e: e.tensor_copy(out=Jm[:, :], in_=setup[:, 128:256]), r=["setup"], w=["Jm"])
        dve(lambda e: e.memset(ones[:, :], 1.0), w=["ones"])
        dve(lambda e: e.memset(sel[:, :, :], 0.0), w=["sel"])
        dve(lambda e: e.memset(sel[0:32, 0, :], 1.0 / 32.0), r=["sel"], w=["sel"])
        dve(lambda e: e.memset(sel[32:64, 1, :], 1.0 / 32.0), r=["sel"], w=["sel"])
        dve(lambda e: e.memset(epsc[:, :], EPS), w=["epsc"])
        dve(lambda e: e.memset(xst[0][:, :], 0.0), w=["xst0"])
        dma("sp", lnT[:, :, :], lnT_d, [], ["lnT"], "c_lnT")
        dma("sp", convw[:, :, :], convw_d, [], ["convw"], "c_convw")
        dma("sp", flags[:, :], flags_d, [], ["flags"], "c_flags")
        dma("sp", GB0[:, :, :], lnB_d[:, 0:2, :], [], ["GB0"], "c_GB0")
        dma("sp", setup[:, 256:512], lamv_d.rearrange("p a b -> p (a b)"), ["setup"], ["lamv"], "c_lamv")
        dma("sp", setup[:, 512:513], subg_d, ["setup"], ["subg"], "c_subg")
        dve(lambda e: e.tensor_mul(out=setup[:, 520:584], in0=setup[:, 256:320], in1=setup[:, 320:384]), r=["lamv"], w=["lp1"])
        dve(lambda e: e.tensor_mul(out=setup[:, 584:648], in0=setup[:, 384:448], in1=setup[:, 448:512]), r=["lamv"], w=["lp2"])
        dve(lambda e: e.reduce_sum(out=setup[:, 650:651], in_=setup[:, 520:584], axis=AX.X), r=["lp1"], w=["ls1"])
        dve(lambda e: e.reduce_sum(out=setup[:, 651:652], in_=setup[:, 584:648], axis=AX.X), r=["lp2"], w=["ls2"])
        act(lambda e: e.activation(out=setup[:, 652:654], in_=setup[:, 650:652], func=AF.Exp), r=["ls1", "ls2"], w=["le"])
        dve(lambda e: e.tensor_sub(out=setup[:, 654:655], in0=setup[:, 653:654], in1=setup[:, 652:653]), r=["le"], w=["ld"])
        dve(lambda e: e.tensor_scalar_add(out=nlam[:, :], in0=setup[:, 654:655], scalar1=-LAM_INIT), r=["ld"], w=["nlam"])
        dve(lambda e: e.tensor_scalar_mul(out=gsc[:, :], in0=setup[:, 512:513], scalar1=1.0 - LAM_INIT), r=["subg"], w=["gsc"])
        dma("sp", setup[0:32, 660:664], rb_d, ["setup"], ["rb"], "c_rb")
        dma("sp", wk[0][0:32, 0:384], oh_d, [], ["wk0"], "wk0")
        pe(lambda e: e.matmul(ps[0][0:4, 0:384], lhsT=setup[0:32, 660:664], rhs=wk[0][0:32, 0:384], start=True, stop=True),
           r=["rb", "wk0"], w=[PS[0]])
        dve(lambda e: e.tensor_scalar_mul(out=wk[1][0:4, 0:384], in0=ps[0][0:4, 0:384], scalar1=8.0), r=[PS[0]], w=["wk1"])
        dma("sp", wsc.ap(), wk[1][0:4, 0:384], ["wk1"], ["wsc"], "wk1")
        for h in range(4):
            src = wk[2 + (h % 2)]
            sn = "wk%d" % (2 + (h % 2))
            dma("sp", src[:, 0:256], bass.AP(wsc, h * 384, [[1, 128], [1, 256]]), ["wsc"], [sn], sn)
            for s in range(2):
                dve(lambda e, src=src, h=h, s=s: e.tensor_scalar_mul(out=HS[:, s, h, :], in0=src[:, 128:256], scalar1=flags[:, 8 + s:9 + s]),
                    r=[sn, "flags"], w=["HS"])
            dve(lambda e, src=src: e.memset(src[0:64, 0:64], NEG), r=[], w=[sn])
            dve(lambda e, src=src, h=h: e.tensor_copy(out=H01[:, h, :], in_=src[:, 0:256]), r=[sn], w=["H01"])
        dve(lambda e: e.memset(setup[:, 0:512], NEG), r=["setup", "ident", "Jm"], w=["setup"])
        for s in range(2):
            dve(lambda e, s=s: e.tensor_scalar_mul(out=Mf[:, s, :], in0=setup[:, 0:512], scalar1=flags[:, 10 + s:11 + s]),
                r=["setup", "flags"], w=["Mf"])

        if os.environ.get("KDBG"):
            dbg_h = nc.dram_tensor("dbg_h", [128, 4, 256], BF16, kind="ExternalOutput").ap()
            dbg_hs = nc.dram_tensor("dbg_hs", [128, 2, 4, 128], BF16, kind="ExternalOutput").ap()
            dbg_mf = nc.dram_tensor("dbg_mf", [128, 2, 512], BF16, kind="ExternalOutput").ap()
            dma("sp", dbg_h, H01[:, :, :], ["H01"], [], "dbgh")
            dma("sp", dbg_hs, HS[:, :, :, :], ["HS"], [], "dbgh")
            dma("sp", dbg_mf, Mf[:, :, :], ["Mf"], [], "dbgh")
            dbg_j = nc.dram_tensor("dbg_j", [128, 128], BF16, kind="ExternalOutput").ap()
            dma("sp", dbg_j, Jm[:, :], ["Jm"], [], "dbgh")
        S.mark(0)
        S.barrier()

        def wload(W, col0, nk=8, ncols=256):
            i = ws_state[0] % NWS
            ws_state[0] += 1
            name = "wsl%d" % i
            dma("pool", wsl[i][:, 0:nk, 0:ncols], W[:, col0:col0 + ncols].rearrange("(k p) c -> p k c", p=128),
                [], [name], name)
            return wsl[i], name

        def rstd_from_var(var_ap, out_ap, rbuf, wbuf):
            act(lambda e: e.activation(out=out_ap, in_=var_ap, func=AF.Ln, bias=epsc[:, :], scale=1.0), r=rbuf, w=wbuf)
            act(lambda e: e.activation(out=out_ap, in_=out_ap, func=AF.Exp, scale=-0.5), r=wbuf, w=wbuf)

        def exhaust(g):
            for _ in g:
                pass

        def ln_tile(*a, **k):
            exhaust(ln_tile_g(*a, **k))

        def ln_tile_g(src_dram, stat_ap, statbuf, gi, dstT, dstname, col0, ncolsT=128, psb=7):
            i = x_state[0] % 2
            x_state[0] += 1
            xn_ = "xst%d" % i
            xb_ = "xnb0"
            dma("sp", xst[i][:, :], src_dram, [], [xn_], xn_)
            dve(lambda e: e.bn_stats(stt[:, 0:6], xst[i][:, 0:512]), r=[xn_], w=["stt"])
            dve(lambda e: e.bn_stats(stt[:, 6:12], xst[i][:, 512:1024]), r=[xn_], w=["stt"])
            dve(lambda e: e.bn_aggr(stat_ap, stt[:, 0:12]), r=["stt"], w=[statbuf])
            rstd_from_var(stat_ap[:, 1:2], stat_ap[:, 1:2], [statbuf], [statbuf])
            dve(lambda e: e.tensor_scalar(out=xnb[i][:, :], in0=xst[i][:, :], scalar1=stat_ap[:, 0:1], scalar2=stat_ap[:, 1:2],
                                          op0=ALU.subtract, op1=ALU.mult), r=[xn_, statbuf], w=[xb_])
            yield
            yield
            yield
            transpose_affine(xnb[i], xb_, gi, dstT, dstname, col0, ncolsT, psb)
            yield

        def transpose_affine(xb, xb_, gi, dstT, dstname, col0, ncolsT=128, psb=7):
            pb = ps_bf(psb)

            def tr(e):
                ins = None
                for c in range(8):
                    ins = e.transpose(pb[:, c * 128:(c + 1) * 128], xb[:, c * 128:(c + 1) * 128], ident[:, :])
                return ins
            pe(tr, r=[xb_, "ident"], w=[PS[psb]])

            def ev(e):
                ins = None
                for c in range(8):
                    ins = e.activation(out=dstT[:, c, col0:col0 + ncolsT], in_=pb[:, c * 128:c * 128 + ncolsT], func=AF.Identity,
                                       scale=lnT[:, gi, c:c + 1], bias=lnT[:, gi + 1, c:c + 1])
                return ins
            act(ev, r=[PS[psb], "lnT"], w=[dstname])

        def kv_group(*a, **k):
            exhaust(kv_group_g(*a, **k))

        def kv_group_g(ntile, tok0, blk0, kname, vname, own_rows=None, kout=None, vout=None, onebank=None):
            n = ntile * 128
            for hp in range(2):
                wsb, wn = wload(w_in, 512 + hp * 256)
                for hh in range(2):
                    h = 2 * hp + hh
                    bank = hh if onebank is None else onebank

                    def mm(e, wsb=wsb, hh=hh, bank=bank):
                        ins = None
                        for k in range(8):
                            ins = e.matmul(ps[bank][:, 0:n], lhsT=wsb[:, k, hh * 128:(hh + 1) * 128], rhs=x0T[:, k, 0:n],
                                           start=(k == 0), stop=(k == 7))
                        return ins
                    pe(mm, r=[wn, "x0T"], w=[PS[bank]])
                    dve(lambda e, h=h, bank=bank: e.tensor_copy(out=KT[:, h, tok0:tok0 + n], in_=ps[bank][:, 0:n]),
                        r=[PS[bank]], w=[kname])
                    yield
                if kout is not None:
                    for t in range(ntile):
                        bank = 2 + (t % 2) if onebank is None else onebank

                        def mm(e, wsb=wsb, t=t, bank=bank):
                            ins = None
                            for k in range(8):
                                ins = e.matmul(ps[bank][:, 0:256], lhsT=x0T[:, k, t * 128:(t + 1) * 128], rhs=wsb[:, k, :],
                                               start=(k == 0), stop=(k == 7))
                            return ins
                        pe(mm, r=[wn, "x0T"], w=[PS[bank]])
                        oi = ost_state[0] % 2
                        ost_state[0] += 1
                        dve(lambda e, oi=oi, bank=bank: e.tensor_copy(out=ost[oi][:, :], in_=ps[bank][:, 0:256]),
                            r=[PS[bank]], w=["ost%d" % oi])
                        dma("sp", kout[own_rows + t * 128:own_rows + (t + 1) * 128, hp * 256:(hp + 1) * 256], ost[oi][:, :],
                            ["ost%d" % oi], [], "ost%d" % oi)
                        yield
            for hp in range(2):
                wsb, wn = wload(w_in, 1024 + hp * 256)
                for t in range(ntile):
                    bank = 4 + (t % 2) if onebank is None else onebank

                    def mm(e, wsb=wsb, t=t, bank=bank):
                        ins = None
                        for k in range(8):
                            ins = e.matmul(ps[bank][:, 0:256], lhsT=x0T[:, k, t * 128:(t + 1) * 128], rhs=wsb[:, k, :],
                                           start=(k == 0), stop=(k == 7))
                        return ins
                    pe(mm, r=[wn, "x0T"], w=[PS[bank]])
                    if vout is None:
                        act(lambda e, t=t, bank=bank, hp=hp: e.copy(out=Vt[:, blk0 + t, hp * 256:(hp + 1) * 256], in_=ps[bank][:, 0:256]),
                            r=[PS[bank]], w=[vname])
                        yield
                    else:
                        oi = ost_state[0] % 2
                        ost_state[0] += 1
                        dve(lambda e, oi=oi, bank=bank: e.tensor_copy(out=ost[oi][:, :], in_=ps[bank][:, 0:256]),
                            r=[PS[bank]], w=["ost%d" % oi])
                        act(lambda e, t=t, oi=oi, hp=hp: e.copy(out=Vt[:, blk0 + t, hp * 256:(hp + 1) * 256], in_=ost[oi][:, :]),
                            r=["ost%d" % oi], w=[vname])
                        dma("sp", vout[own_rows + t * 128:own_rows + (t + 1) * 128, hp * 256:(hp + 1) * 256], ost[oi][:, :],
                            ["ost%d" % oi], [], "ost%d" % oi)
                        yield

        def q_conv(*a, **k):
            exhaust(q_conv_g(*a, **k))

        def q_conv_g(ntile, nseg, hist_fn, conv_out_fn, prehist_fn=None):
            n = ntile * 128
            L = n // nseg
            for hp in range(2):
                wsb, wn = wload(w_in, hp * 256)
                for hh in range(2):
                    h = 2 * hp + hh
                    bank = hh

                    def mm(e, wsb=wsb, hh=hh, bank=bank):
                        ins = None
                        for k in range(8):
                            ins = e.matmul(ps[bank][:, 0:n], lhsT=wsb[:, k, hh * 128:(hh + 1) * 128], rhs=x0T[:, k, 0:n],
                                           start=(k == 0), stop=(k == 7))
                        return ins
                    pe(mm, r=[wn, "x0T"], w=[PS[bank]])
                    act(lambda e, h=h, bank=bank: e.copy(out=QT[:, h, 0:n], in_=ps[bank][:, 0:n]), r=[PS[bank]], w=["QT"])
                    yield
            for cpair in range(2):
                wh, whn = wload(w_in, 2560 + cpair * 256)
                wc, wcn = wload(w_in, 2048 + cpair * 256)
                wb, wbn = wload(w_in, 1536 + cpair * 256)
                for ci in range(2):
                    cch = 2 * cpair + ci

                    def proj(bank, wsb, wn, rhs_ap, nn, rname, ci=ci):
                        def mm(e):
                            ins = None
                            for k in range(8):
                                ins = e.matmul(ps[bank][:, 0:nn], lhsT=wsb[:, k, ci * 128:(ci + 1) * 128], rhs=rhs_ap(k),
                                               start=(k == 0), stop=(k == 7))
                            return ins
                        pe(mm, r=[wn, rname], w=[PS[bank]])
                    u3 = wk[1][:, 0:nseg * (L + 2)].rearrange("p (s l) -> p s l", s=nseg)
                    a3 = wk[2][:, 0:n].rearrange("p (s l) -> p s l", s=nseg)
                    hb = 2 if cch % 2 == 0 else 0
                    cb = hb + 1
                    proj(hb, wh, whn, lambda k: x0T[:, k, 0:n], n, "x0T")
                    act(lambda e, hb=hb: e.copy(out=wk[0][:, 0:n], in_=ps[hb][:, 0:n]), r=[PS[hb]], w=["wk0"])
                    proj(cb, wc, wcn, lambda k: x0T[:, k, 0:n], n, "x0T")
                    dve(lambda e, u3=u3, cb=cb: e.tensor_mul(out=u3[:, :, 2:L + 2], in0=ps[cb][:, 0:n].rearrange("p (s l) -> p s l", s=nseg),
                                                             in1=wk[0][:, 0:n].rearrange("p (s l) -> p s l", s=nseg)),
                        r=[PS[cb], "wk0"], w=["wk1"])
                    if prehist_fn is not None and ci == 0:
                        prehist_fn(cpair, wh, whn, wc, wcn)
                    hist_fn(cch, ci, u3, wh, whn, wc, wcn, proj, hb, cb)
                    dve(lambda e, u3=u3, a3=a3, cch=cch: e.tensor_scalar_mul(out=a3, in0=u3[:, :, 0:L], scalar1=convw[:, cch, 0:1]),
                        r=["wk1", "convw"], w=["wk2"])
                    dve(lambda e, u3=u3, a3=a3, cch=cch: e.scalar_tensor_tensor(out=a3, in0=u3[:, :, 1:L + 1], scalar=convw[:, cch, 1:2], in1=a3,
                                                                                  op0=ALU.mult, op1=ALU.add), r=["wk1", "wk2"], w=["wk2"])
                    dve(lambda e, u3=u3, a3=a3, cch=cch: e.scalar_tensor_tensor(out=a3, in0=u3[:, :, 2:L + 2], scalar=convw[:, cch, 2:3], in1=a3,
                                                                                  op0=ALU.mult, op1=ALU.add), r=["wk1", "wk2"], w=["wk2"])
                    proj(hb, wb, wbn, lambda k: x0T[:, k, 0:n], n, "x0T")
                    dve(lambda e, cch=cch, hb=hb: e.tensor_mul(out=catT[:, 4 + cch, 0:n], in0=ps[hb][:, 0:n], in1=wk[2][:, 0:n]),
                        r=[PS[hb], "wk2"], w=["catT"])
                    conv_out_fn(cch, u3, L)
                    yield

        def attn_finish1(n):
            dve(lambda e: e.tensor_copy(out=wk[0][:, 0:n], in_=ps[4][:, 0:n]), r=[PS[4]], w=["wk0"])
            dve(lambda e: e.tensor_copy(out=wk[1][:, 0:n], in_=ps[5][:, 0:n]), r=[PS[5]], w=["wk1"])
            dve(lambda e: e.tensor_copy(out=wk[2][0:64, 0:n], in_=ps[6][0:64, 0:n]), r=[PS[6]], w=["wk2"])

        def attn_finish2(h, n, cc0):
            for m in range(2):
                pe(lambda e, m=m: e.matmul(ps[7][:, 0:n], lhsT=sel[:, m, :], rhs=wk[2][0:64, 0:n], start=True, stop=True),
                   r=["wk2", "sel"], w=[PS[7]])
                act(lambda e: e.activation(out=wk[3][:, 0:n], in_=ps[7][:, 0:n], func=AF.Ln), r=[PS[7]], w=["wk3"])
                act(lambda e: e.activation(out=wk[3][:, 0:n], in_=wk[3][:, 0:n], func=AF.Exp, scale=-1.0), r=["wk3"], w=["wk3"])
                yield
                dve(lambda e, m=m: e.tensor_mul(out=wk[m][:, 0:n], in0=wk[m][:, 0:n], in1=wk[3][:, 0:n]), r=["wk%d" % m, "wk3"], w=["wk%d" % m])
                yield
            dve(lambda e: e.scalar_tensor_tensor(out=wk[0][:, 0:n], in0=wk[1][:, 0:n], scalar=nlam[:, 0:1], in1=wk[0][:, 0:n],
                                                 op0=ALU.mult, op1=ALU.add), r=["wk0", "wk1", "nlam"], w=["wk0"])
            dve(lambda e: e.tensor_mul(out=sqb[:, 0:n], in0=wk[0][:, 0:n], in1=wk[0][:, 0:n]), r=["wk0"], w=["sqb"])
            yield
            pe(lambda e: e.matmul(ps[7][:, 0:n], lhsT=ones[:, :], rhs=sqb[:, 0:n], start=True, stop=True), r=["sqb", "ones"], w=[PS[7]])
            act(lambda e: e.activation(out=wk[3][:, 0:n], in_=ps[7][:, 0:n], func=AF.Ln, bias=epsc[:, :], scale=1.0 / 128.0),
                r=[PS[7]], w=["wk3"])
            act(lambda e: e.activation(out=wk[3][:, 0:n], in_=wk[3][:, 0:n], func=AF.Exp, scale=-0.5), r=["wk3"], w=["wk3"])
            yield
            dve(lambda e: e.scalar_tensor_tensor(out=catT[:, h, cc0:cc0 + n], in0=wk[0][:, 0:n], scalar=gsc[:, 0:1], in1=wk[3][:, 0:n],
                                                 op0=ALU.mult, op1=ALU.mult), r=["wk0", "wk3", "gsc"], w=["catT"])

        def attention(blocks_for_head, ncol, qcol0=0, background=None):
            pt_i = [0]
            pending_fin = None
            for h in range(4):
                blks = blocks_for_head(h)
                nb = len(blks)
                pend = None
                first = [True, True]

                def issue_pv(b, pt, ptn):
                    p0, p1 = b["prange"]
                    c0, c1 = b["c0"], b["c1"]
                    st = first[0]
                    first[0] = False
                    last = b["last"]
                    for m, ob in enumerate((4, 5)):
                        pe(lambda e, b=b, ob=ob, m=m: e.matmul(ps[ob][:, c0:c1], lhsT=b["v"], rhs=pt[p0:p1, m * 512 + c0:m * 512 + c1],
                                                                 start=st, stop=last),
                           r=["%s_%d" % (ptn, m), b["vname"]], w=[PS[ob]])
                    for m in range(2):
                        pe(lambda e, m=m: e.matmul(ps[6][32 * m:32 * m + 32, c0:c1], lhsT=ones[p0:p1, 0:32],
                                                   rhs=pt[p0:p1, m * 512 + c0:m * 512 + c1], start=st, stop=last),
                           r=["%s_%d" % (ptn, m), "ones"], w=[PS[6]])

                for bi, b in enumerate(blks):
                    b["last"] = (bi == nb - 1)
                    sp_ = (bi % 2)
                    c0, c1 = b["c0"], b["c1"]
                    p0, p1 = b["prange"]
                    nbias = len(b["bias"])
                    for m in range(2):
                        kap = b["kt"][m]
                        qap = QT[m * 64:(m + 1) * 64, h, qcol0 + c0:qcol0 + c1]
                        bank = 2 * sp_ + m
                        pe(lambda e, kap=kap, qap=qap, bank=bank, nbias=nbias, p0=p0, p1=p1, c0=c0, c1=c1:
                           e.matmul(ps[bank][p0:p1, c0:c1], lhsT=kap, rhs=qap, start=True, stop=(nbias == 0)),
                           r=[b["kname"], "QT"], w=[PS[bank]])
                        for bj, (bl, br, oc0, oc1) in enumerate(b["bias"]):
                            pe(lambda e, bl=bl, br=br, oc0=oc0, oc1=oc1, bank=bank, lastb=(bj == nbias - 1), p0=p0, p1=p1:
                               e.matmul(ps[bank][p0:p1, oc0:oc1], lhsT=bl, rhs=br, start=False, stop=lastb),
                               r=["Jm", "H01", "HS", "Mf"], w=[PS[bank]])
                    i = pt_i[0] % 3
                    pt_i[0] += 1
                    for m in range(2):
                        act(lambda e, m=m, i=i, bank=2 * sp_ + m, p0=p0, p1=p1, c0=c0, c1=c1: e.activation(out=PT[i][p0:p1, m * 512 + c0:m * 512 + c1], in_=ps[bank][p0:p1, c0:c1],
                                                                                func=AF.Exp, scale=0.125),
                            r=[PS[2 * sp_ + m]], w=["PT%d_%d" % (i, m)])
                    if pend is not None:
                        issue_pv(*pend)
                    pend = (b, PT[i], "PT%d" % i)
                    if bi >= 2 and pending_fin is not None:
                        next(pending_fin, None)
                    if background is not None:
                        background()
                issue_pv(*pend)
                if pending_fin is not None:
                    exhaust(pending_fin)
                attn_finish1(ncol)
                pending_fin = attn_finish2(h, ncol, qcol0)
            exhaust(pending_fin)

        def out_proj_mm(ntile):
            for dq in range(4):
                wsb, wn = wload(w_out, dq * 256)
                for t in range(ntile):
                    bank = 2 * t + dq // 2

                    def mm(e, wsb=wsb, t=t, bank=bank, dq=dq):
                        ins = None
                        for k in range(8):
                            ins = e.matmul(ps[bank][:, (dq % 2) * 256:(dq % 2) * 256 + 256], lhsT=catT[:, k, t * 128:(t + 1) * 128],
                                           rhs=wsb[:, k, :], start=(k == 0), stop=(k == 7))
                        return ins
                    pe(mm, r=[wn, "catT"], w=[PS[bank]])

        def ln1_load(t, src_rows):
            i = x_state[0] % 2
            x_state[0] += 1
            dma("sp", xst[i][:, :], src_rows(t), [], ["xst%d" % i], "xst%d" % i)
            return i

        def ln1_chain(t, src_rows, stat_idx0, xsc_row0, i=None):
            if i is None:
                i = ln1_load(t, src_rows)
            xn_ = "xst%d" % i
            x_ = xst[i]
            sa = st0[:, stat_idx0 + t, :]
            dve(lambda e, sa=sa: e.tensor_scalar(out=stt[:, 16:17], in0=sa[:, 0:1], scalar1=sa[:, 1:2], scalar2=-1.0,
                                                 op0=ALU.mult, op1=ALU.mult), r=["st0"], w=["stt4"])
            act(lambda e, x_=x_, sa=sa: e.activation(out=x_[:, :], in_=x_[:, :], func=AF.Identity, scale=sa[:, 1:2], bias=stt[:, 16:17]),
                r=[xn_, "st0", "stt4"], w=[xn_])
            dve(lambda e, x_=x_: e.tensor_mul(out=x_[:, :], in0=x_[:, :], in1=GB0[:, 0, :]), r=[xn_, "GB0"], w=[xn_])
            dve(lambda e, x_=x_: e.tensor_add(out=x_[:, :], in0=x_[:, :], in1=GB0[:, 1, :]), r=[xn_, "GB0"], w=[xn_])
            for hf in range(2):
                dve(lambda e, x_=x_, hf=hf, t=t: e.scalar_tensor_tensor(out=x_[:, hf * 512:(hf + 1) * 512], in0=x_[:, hf * 512:(hf + 1) * 512],
                                                                        scalar=ALPHA, in1=ps[2 * t + hf][:, :], op0=ALU.mult, op1=ALU.add),
                    r=[xn_, PS[2 * t + hf]], w=[xn_])
            dve(lambda e, x_=x_: e.bn_stats(stt[:, 0:6], x_[:, 0:512]), r=[xn_], w=["stt"])
            dve(lambda e, x_=x_: e.bn_stats(stt[:, 6:12], x_[:, 512:1024]), r=[xn_], w=["stt"])
            dve(lambda e: e.bn_aggr(stt[:, 12:14], stt[:, 0:12]), r=["stt"], w=["stt2"])
            rstd_from_var(stt[:, 13:14], stt[:, 13:14], ["stt2"], ["stt2"])
            dve(lambda e: e.tensor_scalar(out=stt[:, 17:18], in0=stt[:, 12:13], scalar1=stt[:, 13:14], scalar2=-1.0,
                                          op0=ALU.mult, op1=ALU.mult), r=["stt2"], w=["stt5"])
            act(lambda e, x_=x_: e.activation(out=x_[:, :], in_=x_[:, :], func=AF.Identity, scale=stt[:, 13:14], bias=stt[:, 17:18]),
                r=[xn_, "stt2", "stt5"], w=[xn_])
            dma("sp", xsc[xsc_row0 + t * 128:xsc_row0 + (t + 1) * 128, :], x_[:, :], [xn_], ["xsc"], xn_)

        def out_proj_ln1(ntile, src_rows, stat_idx0, xsc_row0):
            out_proj_mm(ntile)
            for t in range(ntile):
                ln1_chain(t, src_rows, stat_idx0, xsc_row0)

        KTALL = ["KT%d" % g for g in range(NG)]
        VALL = ["V%d" % g for g in range(NG)]
        CG = min(4, NCB)
        for s in range(2):
            VG = min(8, NCB)
            for v8 in range(NCB // VG):
                dma("pool", Vt[:, s * SBLK + v8 * VG:s * SBLK + (v8 + 1) * VG, :],
                    cv[s, v8 * VG * 128:(v8 + 1) * VG * 128, :].rearrange("(b p) c -> p b c", p=128), [], ["Vc%d_%d" % (s, v8)], "vc%d_%d" % (s, v8))
            for b4 in range(NCB // CG):
                i = ws_state[0] % NWS
                ws_state[0] += 1
                ck4 = wsl[i][:, :, :].rearrange("p a b -> p (a b)").rearrange("p (g c) -> p g c", c=512)
                dma("pool", ck4[:, 0:CG, :], ck[s, b4 * CG * 128:(b4 + 1) * CG * 128, :].rearrange("(b p) c -> p b c", p=128),
                    [], ["wsl%d" % i], "wsl%d" % i)
                for bb in range(CG):
                    blk = b4 * CG + bb
                    pb = ps_bf(blk % 2)

                    def tr(e, ck4=ck4, pb=pb, bb=bb):
                        ins = None
                        for h in range(4):
                            ins = e.transpose(pb[:, h * 128:(h + 1) * 128], ck4[:, bb, h * 128:(h + 1) * 128], ident[:, :])
                        return ins
                    pe(tr, r=["wsl%d" % i, "ident"], w=[PS[blk % 2]])
                    t0 = s * SOFF + blk * 128
                    dve(lambda e, pb=pb, t0=t0: e.tensor_copy(out=KT[:, :, t0:t0 + 128], in_=pb[:, 0:512].rearrange("p (h t) -> p h t", h=4)),
                        r=[PS[blk % 2]], w=["KTc%d_%d" % (s, blk)])
        S.add("pe", None, r=["Vc%d_%d" % (s, v8) for s in range(2) for v8 in range(NCB // min(8, NCB))] +
              ["KTc%d_%d" % (s, blk) for s in range(2) for blk in range(NCB)], w=["V0", "KT0"])
        S.mark(1)
        ln_tile(xs[:, :], st0[:, SST, :], "st0", 0, x0T, "x0T", 0)
        S.mark(2)
        kv_group(1, SK0, VNEW, "KT0", "V0", own_rows=0, kout=ksn, vout=vsn)

        def hist_sample(cch, ci, u3, wh, whn, wc, wcn, proj, hb, cb):
            dma("sp", uh[:, 0:2, :], cc[:, cch, :, :], [], ["uh"], "uh")
            dve(lambda e, u3=u3: e.tensor_copy(out=u3[:, :, 0:2], in_=uh[:, 0:2, :]), r=["uh"], w=["wk1"])

        def cout_sample(cch, u3, L):
            dve(lambda e, u3=u3: e.tensor_copy(out=uh[:, 2:4, :], in_=u3[:, :, L:L + 2]), r=["wk1"], w=["uh2"])
            dma("sp", csn[:, cch, :, :], uh[:, 2:4, :], ["uh2"], [], "uh2")

        S.mark(3)
        q_conv(1, 2, hist_sample, cout_sample)
        S.mark(4)

        for s in range(2):
            def blocks_sample(h, s=s):
                out = []
                for blk in range(NCB):
                    t0 = s * SOFF + blk * 128
                    bias = []
                    if blk == NCB - 1:
                        bias = [(Jm[:, :], H01[:, h, 128:192], 0, 64)]
                    out.append(dict(kt=(KT[0:64, h, t0:t0 + 128], KT[64:128, h, t0:t0 + 128]), kname="KT0",
                                    v=Vt[:, s * SBLK + blk, h * 128:(h + 1) * 128], vname="V0", c0=0, c1=64, prange=(0, 128), bias=bias))
                t0 = SK0 + s * 64
                p0 = s * 64
                out.append(dict(kt=(KT[0:64, h, t0:t0 + 64], KT[64:128, h, t0:t0 + 64]), kname="KT0",
                                v=Vt[p0:p0 + 64, VNEW, h * 128:(h + 1) * 128], vname="V0", c0=0, c1=64, prange=(p0, p0 + 64),
                                bias=[(Jm[:, 0:64], H01[:, h, 0:64], 0, 64)]))
                return out
            attention(blocks_sample, 64, qcol0=s * 64)
        S.mark(5)
        out_proj_ln1(1, lambda t: xs[:, :], SST, NSLOT * GT)
        S.mark(6)
        S.barrier()

        def prep_steps(pos, onebank):
            own = (pos % 2 == 1)
            j = pos // 2
            for t in range(4):
                sa = st0[:, j * 4 + t, :] if own else stt[:, 14:16]
                yield from ln_tile_g(xp[pos * GT + t * 128:pos * GT + (t + 1) * 128, :], sa, "st0" if own else "stt3", 0, x0T, "x0T", t * 128)
            yield from kv_group_g(4, pos * GT, pos * 4, "KT%d" % pos, "V%d" % pos,
                                  own_rows=j * GT if own else None, kout=kp if own else None, vout=vp if own else None, onebank=onebank)
            if not own:
                return
            i = x_state[0] % 2
            x_state[0] += 1
            xn_, xb_ = "xst%d" % i, "xnb0"
            dma("sp", xst[i][0:2, :], xh[j, :, :], [], [xn_], xn_)
            dve(lambda e, i=i: e.bn_stats(stt[:, 0:6], xst[i][:, 0:512]), r=[xn_], w=["stt"])
            dve(lambda e, i=i: e.bn_stats(stt[:, 6:12], xst[i][:, 512:1024]), r=[xn_], w=["stt"])
            dve(lambda e: e.bn_aggr(stt[:, 14:16], stt[:, 0:12]), r=["stt"], w=["stt3"])
            rstd_from_var(stt[:, 15:16], stt[:, 15:16], ["stt3"], ["stt3"])
            dve(lambda e, i=i: e.tensor_scalar(out=xnb[i][:, :], in0=xst[i][:, :], scalar1=stt[:, 14:15], scalar2=stt[:, 15:16],
                                               op0=ALU.subtract, op1=ALU.mult), r=[xn_, "stt3"], w=[xb_])
            yield
            yield
            yield
            transpose_affine(xnb[i], xb_, 0, xhT, "xhT", 0, 2)
            yield

        def chain2(a, b):
            yield from a
            yield from b

        def make_bg(gen, ncalls, nsteps):
            state = dict(calls=ncalls, steps=nsteps)

            def bg():
                if state["steps"] <= 0:
                    return
                k = -(-state["steps"] // max(state["calls"], 1))
                state["calls"] -= 1
                for _ in range(k):
                    if next(gen, "END") == "END":
                        state["steps"] = 0
                        return
                    state["steps"] -= 1
            return bg

        exhaust(prep_steps(0, None))
        exhaust(prep_steps(1, None))
        def make_hist(j):
            def prehist(cpair, wh, whn, wc, wcn, j=j):
                for ci in range(2):
                    for ty, (wsb, wn) in enumerate(((wh, whn), (wc, wcn))):
                        col = (ci * 2 + ty) * 2

                        def mm(e, wsb=wsb, ci=ci, col=col):
                            ins = None
                            for k in range(8):
                                ins = e.matmul(ps[1][:, col:col + 2], lhsT=wsb[:, k, ci * 128:(ci + 1) * 128], rhs=xhT[:, k, 0:2],
                                               start=(k == 0), stop=(k == 7))
                            return ins
                        pe(mm, r=[wn, "xhT"], w=[PS[1]])
                v = ps[1][:, 0:8].rearrange("p (c t r) -> p c t r", c=2, t=2)
                dst = uhh[:, 2 * cpair:2 * cpair + 2, :]
                dve(lambda e: e.tensor_copy(out=dst, in_=v[:, :, 0, :]), r=[PS[1]], w=["uhh"])
                dve(lambda e: e.tensor_mul(out=dst, in0=v[:, :, 1, :], in1=dst), r=[PS[1], "uhh"], w=["uhh"])
                dve(lambda e: e.tensor_scalar_mul(out=dst, in0=dst, scalar1=flags[:, j:j + 1]), r=["uhh", "flags"], w=["uhh"])

            def hist_prompt(cch, ci, u3, wh, whn, wc, wcn, proj, hb, cb, j=j):
                dve(lambda e, u3=u3: e.tensor_copy(out=u3[:, 0, 0:2], in_=uhh[:, cch, :]), r=["uhh"], w=["wk1"])

            def cout_prompt(cch, u3, L, j=j):
                if j == NSLOT - 1:
                    dve(lambda e, u3=u3: e.tensor_copy(out=uh[:, 2, :], in_=u3[:, 0, L:L + 2]), r=["wk1"], w=["uh2"])
                    dma("sp", cp[:, cch, :], uh[:, 2, :], ["uh2"], [], "uh2")
            return hist_prompt, cout_prompt, prehist

        def steps(g, k):
            for _ in range(k):
                if next(g, "END") == "END":
                    return

        q_conv(4, 1, *make_hist(0))
        for pos in range(1, NG, 2):
            j = pos // 2
            sidx = j % 2

            def blocks_prompt(h, pos=pos, j=j, sidx=sidx):
                out = []
                for kpos in range(pos + 1):
                    for b in range(4):
                        t0 = kpos * GT + b * 128
                        c0 = 0
                        bias = []
                        if kpos == pos:
                            c0 = b * 128
                            if b < 3:
                                bias = [(Jm[:, :], H01[:, h, :], c0, c0 + 256)]
                            else:
                                bias = [(Jm[:, :], H01[:, h, 0:128], c0, c0 + 128)]
                        elif kpos == pos - 1:
                            bias = [(ident[:, :], Mf[:, sidx, :], 0, 512)]
                            if b == 3:
                                bias.append((Jm[:, :], HS[:, 1 - sidx, h, :], 0, 128))
                        elif kpos == pos - 2 and b == 3:
                            bias = [(Jm[:, :], HS[:, sidx, h, :], 0, 128)]
                        out.append(dict(kt=(KT[0:64, h, t0:t0 + 128], KT[64:128, h, t0:t0 + 128]), kname="KT%d" % kpos,
                                        v=Vt[:, kpos * 4 + b, h * 128:(h + 1) * 128], vname="V%d" % kpos, c0=c0, c1=512,
                                        prange=(0, 128), bias=bias))
                return out
            if j < NSLOT - 1:
                bgen = chain2(prep_steps(pos + 1, 7), prep_steps(pos + 2, 7))
                bgf = make_bg(bgen, 4 * (pos + 1) * 4, 86)
            else:
                bgen, bgf = iter(()), None
            attention(blocks_prompt, 512, background=bgf)
            if dbg_cat is not None:
                dma("sp", dbg_cat[j], catT[:, :, :], ["catT"], [], "dbgc")
                if j == 0:
                    dbg_q = nc.dram_tensor("dbg_q", [128, 4, 512], BF16, kind="ExternalOutput").ap()
                    dbg_k = nc.dram_tensor("dbg_k", [128, 4, 1024], BF16, kind="ExternalOutput").ap()
                    dma("sp", dbg_q, QT[:, :, :], ["QT"], [], "dbgc")
                    dma("sp", dbg_k, KT[:, :, 0:1024], ["KT0", "KT1"], [], "dbgc")
                    dbg_v = nc.dram_tensor("dbg_v", [128, 8, 512], BF16, kind="ExternalOutput").ap()
                    dma("sp", dbg_v, Vt[:, 0:8, :], ["V0", "V1"], [], "dbgc")
            exhaust(bgen)
            srcf = lambda t, pos=pos: xp[pos * GT + t * 128:pos * GT + (t + 1) * 128, :]
            xi0 = ln1_load(0, srcf)
            xi1 = ln1_load(1, srcf)
            out_proj_mm(4)
            if j == NSLOT - 1:
                for c in range(8):
                    dma("pool", W1s[:, :, c * 512:(c + 1) * 512], w1[:, c * 512:(c + 1) * 512].rearrange("(k p) c -> p k c", p=128),
                        [], ["W1_%d" % c] + (KTALL + VALL if c == 0 else []), "W1_%d" % c)
                for c in range(8):
                    dma("pool", W2s[:, c * 4:(c + 1) * 4, :], w2[c * 512:(c + 1) * 512, :].rearrange("(k p) c -> p k c", p=128),
                        [], ["W2_%d" % c], "W2_%d" % c)
            qg = q_conv_g(4, 1, *make_hist(j + 1)) if j + 1 < NSLOT else iter(())
            ln1_chain(0, srcf, j * 4, j * GT, xi0)
            xi2 = ln1_load(2, srcf)
            ln1_chain(1, srcf, j * 4, j * GT, xi1)
            xi3 = ln1_load(3, srcf)
            steps(qg, 3)
            ln1_chain(2, srcf, j * 4, j * GT, xi2)
            steps(qg, 3)
            ln1_chain(3, srcf, j * 4, j * GT, xi3)
            exhaust(qg)

        S.mark(7)
        S.barrier()
        esA.__exit__(None, None, None)
        BAR = S.bar_ids
        GB = sb("GB12", [128, 4, D], F32)
        xres = [sb("xres%d" % i, [128, D], F32) for i in range(2)]
        xb2 = [sb("xb2_0", [128, D], BF16)] * 2
        x1T = sb("x1T", [128, 8, 512], BF16)
        hT = sb("hT", [128, 32, 512], BF16)
        rl = [sb("rl%d" % i, [128, 512], F32) for i in range(2)]
        stb = sb("stb", [128, 16], F32)
        dma("sp", GB[:, :, :], lnB_d[:, 2:6, :], BAR, ["GB12"], "c_GB12")
        W1N = ["W1_%d" % c for c in range(8)]
        W2N = ["W2_%d" % c for c in range(8)]
        xr_state = [0]
        xld = sb("xld", [128, D], F32)

        def grp(g):
            ntile = 4 if g < NSLOT else 1
            return ntile, g * GT

        def prep_load(g, t):
            _, row0 = grp(g)
            dma("sp", xld[:, :], xsc[row0 + t * 128:row0 + (t + 1) * 128, :], ["xsc"], ["xld"], "xld")
            dve(lambda e: e.tensor_copy(out=xb2[0][:, :], in_=xld[:, :]), r=["xld"], w=["xb2_0"])

        def prep_tr(g, t):
            pb = ps_bf(t % 2)

            def tr(e, pb=pb):
                ins = None
                for c in range(8):
                    ins = e.transpose(pb[:, c * 128:(c + 1) * 128], xb2[0][:, c * 128:(c + 1) * 128], ident[:, :])
                return ins
            pe(tr, r=["xb2_0", "ident"], w=[PS[t % 2]])

            def ev(e, pb=pb, t=t):
                ins = None
                for c in range(8):
                    ins = e.activation(out=x1T[:, c, t * 128:(t + 1) * 128], in_=pb[:, c * 128:(c + 1) * 128], func=AF.Identity,
                                       scale=lnT[:, 2, c:c + 1], bias=lnT[:, 3, c:c + 1])
                return ins
            act(ev, r=[PS[t % 2], "lnT"], w=["x1T"])

        for t in range(4):
            prep_load(0, t)
            prep_tr(0, t)
        for g in range(NSLOT + 1):
            ntile, row0 = grp(g)
            n = ntile * 128
            outd = yp if g < NSLOT else ys
            orow0 = row0 if g < NSLOT else 0
            nnext = grp(g + 1)[0] if g < NSLOT else 0
            for fc in range(32):
                bank = 2 + fc % 2

                def mm(e, fc=fc, bank=bank, n=n):
                    ins = None
                    for k in range(8):
                        ins = e.matmul(ps[bank][:, 0:n], lhsT=W1s[:, k, fc * 128:(fc + 1) * 128], rhs=x1T[:, k, 0:n],
                                       start=(k == 0), stop=(k == 7))
                    return ins
                pe(mm, r=[W1N[fc // 4], "x1T"], w=[PS[bank]])
                ri = fc % 2
                act(lambda e, ri=ri, bank=bank, n=n: e.activation(out=rl[ri][:, 0:n], in_=ps[bank][:, 0:n], func=AF.Relu),
                    r=[PS[bank]], w=["rl%d" % ri])
                dve(lambda e, ri=ri, fc=fc, n=n: e.tensor_mul(out=hT[:, fc, 0:n], in0=rl[ri][:, 0:n], in1=rl[ri][:, 0:n]),
                    r=["rl%d" % ri], w=["hT"])
            for t in range(max(ntile, nnext)):
                if t < nnext:
                    prep_load(g + 1, t)
                if t < ntile:
                    i = xr_state[0] % 2
                    xr_state[0] += 1
                    xn_ = "xres%d" % i
                    x_ = xres[i]
                    dma("sp", x_[:, :], xsc[row0 + t * 128:row0 + (t + 1) * 128, :], ["xsc"], [xn_], xn_)
                    dve(lambda e, x_=x_: e.tensor_mul(out=x_[:, :], in0=x_[:, :], in1=GB[:, 0, :]), r=[xn_, "GB12"], w=[xn_])
                    dve(lambda e, x_=x_: e.tensor_add(out=x_[:, :], in0=x_[:, :], in1=GB[:, 1, :]), r=[xn_, "GB12"], w=[xn_])
                    for hf in range(2):
                        bank = 4 + (t % 2) * 2 + hf

                        def mm(e, t=t, hf=hf, bank=bank):
                            ins = None
                            for fc in range(32):
                                ins = e.matmul(ps[bank][:, :], lhsT=hT[:, fc, t * 128:(t + 1) * 128], rhs=W2s[:, fc, hf * 512:(hf + 1) * 512],
                                               start=(fc == 0), stop=(fc == 31))
                            return ins
                        pe(mm, r=W2N + ["hT"], w=[PS[bank]])
                if t < nnext:
                    prep_tr(g + 1, t)
                if t < ntile:
                    for hf in range(2):
                        bank = 4 + (t % 2) * 2 + hf
                        dve(lambda e, x_=x_, hf=hf, bank=bank: e.scalar_tensor_tensor(out=x_[:, hf * 512:(hf + 1) * 512], in0=x_[:, hf * 512:(hf + 1) * 512],
                                                                                      scalar=ALPHA, in1=ps[bank][:, :], op0=ALU.mult, op1=ALU.add),
                            r=[xn_, PS[bank]], w=[xn_])
                    dve(lambda e, x_=x_: e.bn_stats(stb[:, 0:6], x_[:, 0:512]), r=[xn_], w=["stb"])
                    dve(lambda e, x_=x_: e.bn_stats(stb[:, 6:12], x_[:, 512:1024]), r=[xn_], w=["stb"])
                    dve(lambda e: e.bn_aggr(stb[:, 12:14], stb[:, 0:12]), r=["stb"], w=["stb2"])
                    rstd_from_var(stb[:, 13:14], stb[:, 13:14], ["stb2"], ["stb2"])
                    dve(lambda e, x_=x_: e.tensor_scalar(out=x_[:, :], in0=x_[:, :], scalar1=stb[:, 12:13], scalar2=stb[:, 13:14],
                                                         op0=ALU.subtract, op1=ALU.mult), r=[xn_, "stb2"], w=[xn_])
                    dve(lambda e, x_=x_: e.tensor_mul(out=x_[:, :], in0=x_[:, :], in1=GB[:, 2, :]), r=[xn_, "GB12"], w=[xn_])
                    dve(lambda e, x_=x_: e.tensor_add(out=x_[:, :], in0=x_[:, :], in1=GB[:, 3, :]), r=[xn_, "GB12"], w=[xn_])
                    dma("sp", outd[orow0 + t * 128:orow0 + (t + 1) * 128, :], x_[:, :], [xn_], [], xn_)

        S.emit(nc, es)
    return nc


_NC_CACHE = {}
_RUNNER = [None]


def _get_nc(NG, PAST):
    if (NG, PAST) not in _NC_CACHE:
        _NC_CACHE[(NG, PAST)] = build_nc(NG, PAST)
    return _NC_CACHE[(NG, PAST)]


def kernel(x_prompt, x_sample, cache_k, cache_v, cache_conv, ln0_g, ln0_b, rel_bias, w_in, conv_w,
           lambda_q1, lambda_k1, lambda_q2, lambda_k2, subln_g, w_out, ln1_g, ln1_b,
           w_ff1, w_ff2, ln2_g, ln2_b):
    f = lambda a: np.ascontiguousarray(np.asarray(a, dtype=np.float32))
    x_prompt, x_sample, cache_k, cache_v, cache_conv = map(f, (x_prompt, x_sample, cache_k, cache_v, cache_conv))
    NB, SEQ = x_prompt.shape[0], x_prompt.shape[1]
    NG = SEQ // GT
    NSLOT = NG // 2
    PAST = cache_k.shape[2]
    NCORES = 2 * NB
    assert x_sample.shape[0] == 2 * NCORES and x_sample.shape[1] == DEC
    vecs = [f(v).reshape(-1) for v in (ln0_g, ln0_b, ln1_g, ln1_b, ln2_g, ln2_b)]
    lnT = np.ascontiguousarray(np.stack([v.reshape(8, 128).T for v in vecs], axis=1))
    lnB = np.ascontiguousarray(np.broadcast_to(np.stack(vecs, 0)[None], (128, 6, D)))
    cw = f(conv_w)[0]
    convw = np.ascontiguousarray(cw.reshape(3, 4, 128).transpose(2, 1, 0))
    subg = f(subln_g).reshape(128, 1)
    lam = np.stack([f(lambda_q1)[0], f(lambda_k1)[0], f(lambda_q2)[0], f(lambda_k2)[0]], 0)
    lamv = np.ascontiguousarray(np.broadcast_to(lam[None], (128, 4, 64)))
    rb = f(rel_bias)
    bt = _bucket_table()
    oh = np.zeros((32, 384), np.float32)
    oh[bt, np.arange(384)] = 1.0
    oh[15, :] -= 1.0
    w_in_, w_out_, w1_, w2_ = f(w_in)[0], f(w_out)[0], f(w_ff1)[0], f(w_ff2)[0]

    in_maps = []
    orders = [_order(0, NG), _order(1, NG)]
    for c in range(NCORES):
        b, half = c // 2, c % 2
        order = orders[half]
        xb = x_prompt[b].reshape(NG, GT, D)
        xp = np.ascontiguousarray(xb[order].reshape(SEQ, D))
        xh = np.zeros((NSLOT, 2, D), np.float32)
        flags = np.zeros((128, 16), np.float32)
        for j in range(NSLOT):
            g = order[2 * j + 1]
            if g > 0:
                xh[j] = x_prompt[b, g * GT - 2:g * GT]
                flags[:, j] = 1.0
        for s in range(2):
            sa = 1.0 if (s + half) % 2 == 0 else 0.0
            flags[:, 8 + s] = sa
            flags[:, 10 + s] = sa
        xs = np.ascontiguousarray(x_sample[2 * c:2 * c + 2].reshape(128, D))
        ck = np.ascontiguousarray(cache_k[0, 2 * c:2 * c + 2].reshape(2, PAST, 512))
        cv = np.ascontiguousarray(cache_v[0, 2 * c:2 * c + 2].reshape(2, PAST, 512))
        ccv = cache_conv[0, 2 * c:2 * c + 2]
        cc = np.ascontiguousarray(ccv.reshape(2, 2, 4, 128).transpose(3, 2, 0, 1))
        in_maps.append(dict(xp=xp, xh=xh, xs=xs, ck=ck, cv=cv, cc=cc, w_in=w_in_, w_out=w_out_, w1=w1_, w2=w2_,
                            lnT=lnT, lnB=lnB, convw=convw, subg=subg, lamv=lamv, rb=rb, oh=oh, flags=flags))

    nc = _get_nc(NG, PAST)
    if _RUNNER[0] is not None:
        R = _RUNNER[0](nc, in_maps)
    else:
        R = run_bass_kernel_spmd(nc, in_maps, core_ids=list(range(NCORES))).results
    if os.environ.get("KDBG"):
        _RUNNER.append(R)

    y_prompt = np.zeros((NB, SEQ, D), np.float32)
    y_sample = np.zeros((2 * NCORES, DEC, D), np.float32)
    nk_p = np.zeros((1, NB, SEQ, 4, 128), np.float32)
    nv_p = np.zeros((1, NB, SEQ, 4, 128), np.float32)
    nc_p = np.zeros((1, NB, 2, 512), np.float32)
    nk_s = np.zeros((1, 2 * NCORES, DEC, 4, 128), np.float32)
    nv_s = np.zeros((1, 2 * NCORES, DEC, 4, 128), np.float32)
    nc_s = np.zeros((1, 2 * NCORES, 2, 512), np.float32)
    for c in range(NCORES):
        b, half = c // 2, c % 2
        order = orders[half]
        r = R[c]
        for j in range(NSLOT):
            g = order[2 * j + 1]
            y_prompt[b, g * GT:(g + 1) * GT] = r["yp"][j * GT:(j + 1) * GT]
            nk_p[0, b, g * GT:(g + 1) * GT] = r["kp"][j * GT:(j + 1) * GT].reshape(GT, 4, 128)
            nv_p[0, b, g * GT:(g + 1) * GT] = r["vp"][j * GT:(j + 1) * GT].reshape(GT, 4, 128)
        if order[NG - 1] == NG - 1:
            nc_p[0, b] = r["cp"].transpose(2, 1, 0).reshape(2, 512)
        y_sample[2 * c:2 * c + 2] = r["ys"].reshape(2, DEC, D)
        nk_s[0, 2 * c:2 * c + 2] = r["ksn"].reshape(2, DEC, 4, 128)
        nv_s[0, 2 * c:2 * c + 2] = r["vsn"].reshape(2, DEC, 4, 128)
        nc_s[0, 2 * c:2 * c + 2] = r["csn"].transpose(2, 3, 1, 0).reshape(2, 2, 512)
    return (y_prompt, y_sample, nk_p, nv_p, nc_p, nk_s, nv_s, nc_s)
```

```python
import math
import os
from contextlib import ExitStack

import numpy as np
import concourse.bass as bass
import concourse.mybir as mybir
from concourse.bass_utils import run_bass_kernel_spmd

F32 = mybir.dt.float32
BF16 = mybir.dt.bfloat16
AF = mybir.ActivationFunctionType
ALU = mybir.AluOpType
AX = mybir.AxisListType

D = 1024
GT = 512
DEC = 64
LAM_INIT = 0.8 - 0.6 * math.exp(-0.3 * 0)
ALPHA = (2.0 * 1) ** 0.25
EPS = 1e-5
NEG = -30000.0


class Sched:
    def __init__(self):
        self.ops = []
        self.lastw = {}
        self.readers = {}
        self.dma_cnt = {}
        self.marks = {}

    def mark(self, n):
        self.marks[n] = len(self.ops)

    def add(self, eng, fn, r=(), w=(), dma=False, key=None):
        idx = len(self.ops)
        deps = set()
        for b in r:
            if b in self.lastw:
                deps.add(self.lastw[b])
        for b in w:
            if b in self.lastw:
                deps.add(self.lastw[b])
            deps.update(self.readers.get(b, ()))
        op = dict(eng=eng, fn=fn, deps=deps, dma=dma)
        if dma:
            self.dma_cnt[key] = self.dma_cnt.get(key, 0) + 1
            op["key"] = key
            op["val"] = 16 * self.dma_cnt[key]
        self.ops.append(op)
        for b in r:
            lst = self.readers.setdefault(b, [])
            if not dma:
                for i in range(len(lst) - 1, -1, -1):
                    o = self.ops[lst[i]]
                    if (not o["dma"]) and o["eng"] == eng:
                        del lst[i]
            lst.append(idx)
        for b in w:
            self.lastw[b] = idx
            self.readers[b] = []
        return idx

    def barrier(self):
        allb = list(set(self.lastw) | set(self.readers))
        for eng in ("pe", "act", "dve", "pool", "sp"):
            self.add(eng, None, r=[], w=allb)
        self.bar_ids = []

    def emit(self, nc, es):
        lim = int(os.environ.get("KSTOP", "99"))
        ops = self.ops
        if lim >= 1000:
            self.marks[lim] = lim - 1000
        if os.environ.get("KVERB"):
            print("marks", self.marks, "nops", len(ops))
            for i in range(self.marks.get(2, 0), self.marks.get(3, 0)):
                print(i, ops[i]["eng"], ops[i]["dma"], ops[i].get("key"))
        if lim in self.marks:
            ops = ops[:self.marks[lim]]
            self.dma_cnt = {}
            for o in ops:
                if o["dma"]:
                    self.dma_cnt[o["key"]] = self.dma_cnt.get(o["key"], 0) + 1
        engs = ("pe", "act", "dve", "pool", "sp")
        for i, op in enumerate(ops):
            cmax = {}
            dm = {}
            for d in op["deps"]:
                o = ops[d]
                if o["dma"]:
                    k = o["key"]
                    dm[k] = max(dm.get(k, 0), o["val"])
                else:
                    if o["eng"] == "pe" and op["eng"] == "pe" and not op["dma"]:
                        continue
                    cmax[o["eng"]] = max(cmax.get(o["eng"], -1), d)
            op["cdeps"] = cmax
            op["ddeps"] = dm
        mile = set()
        for op in ops:
            mile.update(op["cdeps"].values())
        cnt = {e: 0 for e in engs}
        for i, op in enumerate(ops):
            if i in mile:
                cnt[op["eng"]] += 1
                op["ms"] = cnt[op["eng"]]
        esem = {e: es.enter_context(nc.semaphore("s_" + e)) for e in engs}
        dsem = {k: es.enter_context(nc.semaphore("d_%d" % n)) for n, k in enumerate(self.dma_cnt)}
        block = es.enter_context(nc.Block())

        def run(engname, engobj):
            known = {}
            for i, op in enumerate(ops):
                if op["eng"] != engname:
                    continue
                for e2, d in op["cdeps"].items():
                    v = ops[d]["ms"]
                    if known.get(("c", e2), 0) < v:
                        engobj.wait_ge(esem[e2], v)
                        known[("c", e2)] = v
                for k, v in op["ddeps"].items():
                    if known.get(("d", k), 0) < v:
                        engobj.wait_ge(dsem[k], v)
                        known[("d", k)] = v
                if op["fn"] is None:
                    ins = engobj.nop() if ("ms" in op) else None
                else:
                    ins = op["fn"](engobj)
                if op["dma"]:
                    ins.then_inc(dsem[op["key"]], 16)
                elif "ms" in op:
                    ins.then_inc(esem[engname], 1)
            if engname == "sp":
                for k, n in self.dma_cnt.items():
                    engobj.wait_ge(dsem[k], 16 * n)

        @block.tensor
        def _(e):
            run("pe", e)

        @block.scalar
        def _(e):
            run("act", e)

        @block.vector
        def _(e):
            run("dve", e)

        @block.gpsimd
        def _(e):
            run("pool", e)

        @block.sync
        def _(e):
            run("sp", e)


def _bucket_table():
    rel = (127 - np.arange(384)).astype(np.int32)
    half = 16
    max_exact = 8
    ret = np.where(rel > 0, half, 0)
    n = np.abs(rel)
    nf = np.maximum(n, 1).astype(np.float32)
    large = max_exact + (np.log(nf / np.float32(max_exact)) / np.float32(math.log(128 / max_exact))
                         * np.float32(half - max_exact)).astype(np.int32)
    large = np.minimum(large, half - 1)
    return np.asarray(ret + np.where(n < max_exact, n, large))


def _order(half, ng=16):
    o = []
    for m in range(ng // 2):
        own_first = ((m % 2) == 0) == (half == 0)
        if own_first:
            o += [2 * m + 1, 2 * m]
        else:
            o += [2 * m, 2 * m + 1]
    return o


class _Stop(Exception):
    pass


def build_nc(NG=16, PAST=2048):
    nc = bass.Bass("TRN2", target_bir_lowering=False)
    S = Sched()
    SEQ = NG * GT
    NSLOT = NG // 2
    NCB = PAST // 128
    SOFF = PAST + 256
    SBLK = NCB + 2
    SK0 = 2 * SOFF + 128
    VNEW = SK0 // 128
    SST = NSLOT * 4
    assert SK0 + 128 <= SEQ

    def din(name, shape):
        return nc.dram_tensor(name, list(shape), F32, kind="ExternalInput").ap()

    def dout(name, shape):
        return nc.dram_tensor(name, list(shape), F32, kind="ExternalOutput").ap()

    xp = din("xp", [SEQ, D])
    xh = din("xh", [NSLOT, 2, D])
    xs = din("xs", [128, D])
    ck = din("ck", [2, PAST, 512])
    cv = din("cv", [2, PAST, 512])
    cc = din("cc", [128, 4, 2, 2])
    w_in = din("w_in", [D, 3072])
    w_out = din("w_out", [D, D])
    w1 = din("w1", [D, 4096])
    w2 = din("w2", [4096, D])
    lnT_d = din("lnT", [128, 6, 8])
    lnB_d = din("lnB", [128, 6, D])
    convw_d = din("convw", [128, 4, 3])
    subg_d = din("subg", [128, 1])
    lamv_d = din("lamv", [128, 4, 64])
    rb_d = din("rb", [32, 4])
    oh_d = din("oh", [32, 384])
    flags_d = din("flags", [128, 16])

    yp = dout("yp", [NSLOT * GT, D])
    ys = dout("ys", [128, D])
    kp = dout("kp", [NSLOT * GT, 512])
    vp = dout("vp", [NSLOT * GT, 512])
    cp = dout("cp", [128, 4, 2])
    ksn = dout("ksn", [128, 512])
    vsn = dout("vsn", [128, 512])
    csn = dout("csn", [128, 4, 2, 2])

    wsc = nc.dram_tensor("wsc", [4, 384], F32, kind="Internal")
    dbg_cat = nc.dram_tensor("dbg_cat", [NSLOT, 128, 8, 512], BF16, kind="ExternalOutput").ap() if os.environ.get("KDBG") else None
    xsc = nc.dram_tensor("xsc", [NSLOT * GT + 128, D], F32, kind=("ExternalOutput" if os.environ.get("KDBG") else "Internal")).ap()

    es = ExitStack()
    with es:
        def sb(name, shape, dt):
            return es.enter_context(nc.sbuf_tensor("sb_" + name, list(shape), dt))

        big = sb("big", [128, 65536], BF16)
        KTW = max(SEQ, 8192)
        KT = big[:, 0:32768].rearrange("p (h t) -> p h t", h=4)
        Vt = big[:, 32768:65536].rearrange("p (b c) -> p b c", c=512)
        W1s = big[:, 0:32768].rearrange("p (k f) -> p k f", k=8)
        W2s = big[:, 32768:65536].rearrange("p (k c) -> p k c", c=1024)
        ident = sb("ident", [128, 128], BF16)
        Jm = sb("Jm", [128, 128], BF16)
        ones = sb("ones", [128, 128], BF16)
        sel = sb("sel", [64, 2, 128], F32)
        lnT = sb("lnT", [128, 6, 8], F32)
        convw = sb("convw", [128, 4, 3], F32)
        gsc = sb("gsc", [128, 1], F32)
        nlam = sb("nlam", [128, 1], F32)
        epsc = sb("epsc", [128, 1], F32)
        flags = sb("flags", [128, 16], F32)
        st0 = sb("st0", [128, SST + 1, 2], F32)
        pp = [es.enter_context(nc.psum_tensor("psum%d" % i, [128, 1024], F32)) for i in range(4)]
        ps = [pp[i // 2][:, (i % 2) * 512:(i % 2 + 1) * 512] for i in range(8)]
        PS = ["ps%d" % i for i in range(8)]

        def ps_bf(i):
            return ps[i][:, :].bitcast(BF16)

        pe = lambda fn, r=(), w=(): S.add("pe", fn, r, w)
        act = lambda fn, r=(), w=(): S.add("act", fn, r, w)
        dve = lambda fn, r=(), w=(): S.add("dve", fn, r, w)
        pool = lambda fn, r=(), w=(): S.add("pool", fn, r, w)

        def dma(q, out, in_, r, w, key, **kw):
            S.add(q, lambda e: e.dma_start(out=out, in_=in_, **kw), r, w, dma=True, key=key)

        esA = ExitStack()
        esA.__enter__()

        def sbA(name, shape, dt):
            return esA.enter_context(nc.sbuf_tensor("sb_" + name, list(shape), dt))

        GB0 = sbA("GB0", [128, 2, D], F32)
        H01 = sbA("H01", [128, 4, 256], BF16)
        HS = sbA("HS", [128, 2, 4, 128], BF16)
        Mf = sbA("Mf", [128, 2, 512], BF16)
        xst = [sbA("xst%d" % i, [128, D], F32) for i in range(2)]
        xnb = [sbA("xnb0", [128, D], BF16)] * 2
        x0T = sbA("x0T", [128, 8, 512], BF16)
        xhT = sbA("xhT", [128, 8, 2], BF16)
        QT = sbA("QT", [128, 4, 512], BF16)
        catT = sbA("catT", [128, 8, 512], BF16)
        wk = [sbA("wk%d" % i, [128, 520], F32) for i in range(4)]
        PT = [sbA("PT%d" % i, [128, 1024], BF16) for i in range(3)]
        sqb = sbA("sqb", [128, 512], BF16)
        NWS = 4
        wsl = [sbA("wsl%d" % i, [128, 8, 256], BF16) for i in range(NWS)]
        ost = [sbA("ost%d" % i, [128, 256], F32) for i in range(2)]
        stt = sbA("stt", [128, 20], F32)
        uh = sbA("uh", [128, 4, 2], F32)
        uhh = sbA("uhh", [128, 4, 2], F32)
        setup = xst[1]
        ws_state = [0]
        ost_state = [0]
        x_state = [0]

        pool(lambda e: e.memset(setup[:, 0:128], 0.0), w=["setup"])
        pool(lambda e: e.affine_select(out=setup[:, 0:128], in_=setup[:, 0:128], compare_op=ALU.not_equal,
                                       fill=1.0, base=0, pattern=[[-1, 128]], channel_multiplier=1),
             r=["setup"], w=["setup"])
        dve(lambda e: e.tensor_copy(out=ident[:, :], in_=setup[:, 0:128]), r=["setup"], w=["ident"])
        pool(lambda e: e.memset(setup[:, 128:256], 0.0), r=["setup"], w=["setup"])
        pool(lambda e: e.affine_select(out=setup[:, 128:256], in_=setup[:, 128:256], compare_op=ALU.not_equal,
                                       fill=1.0, base=-127, pattern=[[1, 128]], channel_multiplier=1),
             r=["setup"], w=["setup"])
        dve(lambda e: e.tensor_copy(out=Jm[:, :], in_=setup[:, 128:256]), r=["setup"], w=["Jm"])
        dve(lambda e: e.memset(ones[:, :], 1.0), w=["ones"])
        dve(lambda e: e.memset(sel[:, :, :], 0.0), w=["sel"])
        dve(lambda e: e.memset(sel[0:32, 0, :], 1.0 / 32.0), r=["sel"], w=["sel"])
        dve(lambda e: e.memset(sel[32:64, 1, :], 1.0 / 32.0), r=["sel"], w=["sel"])
        dve(lambda e: e.memset(epsc[:, :], EPS), w=["epsc"])
        dve(lambda e: e.memset(xst[0][:, :], 0.0), w=["xst0"])
        dma("sp", lnT[:, :, :], lnT_d, [], ["lnT"], "c_lnT")
        dma("sp", convw[:, :, :], convw_d, [], ["convw"], "c_convw")
        dma("sp", flags[:, :], flags_d, [], ["flags"], "c_flags")
        dma("sp", GB0[:, :, :], lnB_d[:, 0:2, :], [], ["GB0"], "c_GB0")
        dma("sp", setup[:, 256:512], lamv_d.rearrange("p a b -> p (a b)"), ["setup"], ["lamv"], "c_lamv")
        dma("sp", setup[:, 512:513], subg_d, ["setup"], ["subg"], "c_subg")
        dve(lambda e: e.tensor_mul(out=setup[:, 520:584], in0=setup[:, 256:320], in1=setup[:, 320:384]), r=["lamv"], w=["lp1"])
        dve(lambda e: e.tensor_mul(out=setup[:, 584:648], in0=setup[:, 384:448], in1=setup[:, 448:512]), r=["lamv"], w=["lp2"])
        dve(lambda e: e.reduce_sum(out=setup[:, 650:651], in_=setup[:, 520:584], axis=AX.X), r=["lp1"], w=["ls1"])
        dve(lambda e: e.reduce_sum(out=setup[:, 651:652], in_=setup[:, 584:648], axis=AX.X), r=["lp2"], w=["ls2"])
        act(lambda e: e.activation(out=setup[:, 652:654], in_=setup[:, 650:652], func=AF.Exp), r=["ls1", "ls2"], w=["le"])
        dve(lambda e: e.tensor_sub(out=setup[:, 654:655], in0=setup[:, 653:654], in1=setup[:, 652:653]), r=["le"], w=["ld"])
        dve(lambda e: e.tensor_scalar_add(out=nlam[:, :], in0=setup[:, 654:655], scalar1=-LAM_INIT), r=["ld"], w=["nlam"])
        dve(lambda e: e.tensor_scalar_mul(out=gsc[:, :], in0=setup[:, 512:513], scalar1=1.0 - LAM_INIT), r=["subg"], w=["gsc"])
        dma("sp", setup[0:32, 660:664], rb_d, ["setup"], ["rb"], "c_rb")
        dma("sp", wk[0][0:32, 0:384], oh_d, [], ["wk0"], "wk0")
        pe(lambda e: e.matmul(ps[0][0:4, 0:384], lhsT=setup[0:32, 660:664], rhs=wk[0][0:32, 0:384], start=True, stop=True),
           r=["rb", "wk0"], w=[PS[0]])
        dve(lambda e: e.tensor_scalar_mul(out=wk[1][0:4, 0:384], in0=ps[0][0:4, 0:384], scalar1=8.0), r=[PS[0]], w=["wk1"])
        dma("sp", wsc.ap(), wk[1][0:4, 0:384], ["wk1"], ["wsc"], "wk1")
        for h in range(4):
            src = wk[2 + (h % 2)]
            sn = "wk%d" % (2 + (h % 2))
            dma("sp", src[:, 0:256], bass.AP(wsc, h * 384, [[1, 128], [1, 256]]), ["wsc"], [sn], sn)
            for s in range(2):
                dve(lambda e, src=src, h=h, s=s: e.tensor_scalar_mul(out=HS[:, s, h, :], in0=src[:, 128:256], scalar1=flags[:, 8 + s:9 + s]),
                    r=[sn, "flags"], w=["HS"])
            dve(lambda e, src=src: e.memset(src[0:64, 0:64], NEG), r=[], w=[sn])
            dve(lambda e, src=src, h=h: e.tensor_copy(out=H01[:, h, :], in_=src[:, 0:256]), r=[sn], w=["H01"])
        dve(lambda e: e.memset(setup[:, 0:512], NEG), r=["setup", "ident", "Jm"], w=["setup"])
        for s in range(2):
            dve(lambda e, s=s: e.tensor_scalar_mul(out=Mf[:, s, :], in0=setup[:, 0:512], scalar1=flags[:, 10 + s:11 + s]),
                r=["setup", "flags"], w=["Mf"])

        if os.environ.get("KDBG"):
            dbg_h = nc.dram_tensor("dbg_h", [128, 4, 256], BF16, kind="ExternalOutput").ap()
            dbg_hs = nc.dram_tensor("dbg_hs", [128, 2, 4, 128], BF16, kind="ExternalOutput").ap()
            dbg_mf = nc.dram_tensor("dbg_mf", [128, 2, 512], BF16, kind="ExternalOutput").ap()
            dma("sp", dbg_h, H01[:, :, :], ["H01"], [], "dbgh")
            dma("sp", dbg_hs, HS[:, :, :, :], ["HS"], [], "dbgh")
            dma("sp", dbg_mf, Mf[:, :, :], ["Mf"], [], "dbgh")
            dbg_j = nc.dram_tensor("dbg_j", [128, 128], BF16, kind="ExternalOutput").ap()
            dma("sp", dbg_j, Jm[:, :], ["Jm"], [], "dbgh")
        S.mark(0)
        S.barrier()

        def wload(W, col0, nk=8, ncols=256):
            i = ws_state[0] % NWS
            ws_state[0] += 1
            name = "wsl%d" % i
            dma("pool", wsl[i][:, 0:nk, 0:ncols], W[:, col0:col0 + ncols].rearrange("(k p) c -> p k c", p=128),
                [], [name], name)
            return wsl[i], name

        def rstd_from_var(var_ap, out_ap, rbuf, wbuf):
            act(lambda e: e.activation(out=out_ap, in_=var_ap, func=AF.Ln, bias=epsc[:, :], scale=1.0), r=rbuf, w=wbuf)
            act(lambda e: e.activation(out=out_ap, in_=out_ap, func=AF.Exp, scale=-0.5), r=wbuf, w=wbuf)

        def exhaust(g):
            for _ in g:
                pass

        def ln_tile(*a, **k):
            exhaust(ln_tile_g(*a, **k))

        def ln_tile_g(src_dram, stat_ap, statbuf, gi, dstT, dstname, col0, ncolsT=128, psb=7):
            i = x_state[0] % 2
            x_state[0] += 1
            xn_ = "xst%d" % i
            xb_ = "xnb0"
            dma("sp", xst[i][:, :], src_dram, [], [xn_], xn_)
            dve(lambda e: e.bn_stats(stt[:, 0:6], xst[i][:, 0:512]), r=[xn_], w=["stt"])
            dve(lambda e: e.bn_stats(stt[:, 6:12], xst[i][:, 512:1024]), r=[xn_], w=["stt"])
            dve(lambda e: e.bn_aggr(stat_ap, stt[:, 0:12]), r=["stt"], w=[statbuf])
            rstd_from_var(stat_ap[:, 1:2], stat_ap[:, 1:2], [statbuf], [statbuf])
            dve(lambda e: e.tensor_scalar(out=xnb[i][:, :], in0=xst[i][:, :], scalar1=stat_ap[:, 0:1], scalar2=stat_ap[:, 1:2],
                                          op0=ALU.subtract, op1=ALU.mult), r=[xn_, statbuf], w=[xb_])
            yield
            yield
            yield
            transpose_affine(xnb[i], xb_, gi, dstT, dstname, col0, ncolsT, psb)
            yield

        def transpose_affine(xb, xb_, gi, dstT, dstname, col0, ncolsT=128, psb=7):
            pb = ps_bf(psb)

            def tr(e):
                ins = None
                for c in range(8):
                    ins = e.transpose(pb[:, c * 128:(c + 1) * 128], xb[:, c * 128:(c + 1) * 128], ident[:, :])
                return ins
            pe(tr, r=[xb_, "ident"], w=[PS[psb]])

            def ev(e):
                ins = None
                for c in range(8):
                    ins = e.activation(out=dstT[:, c, col0:col0 + ncolsT], in_=pb[:, c * 128:c * 128 + ncolsT], func=AF.Identity,
                                       scale=lnT[:, gi, c:c + 1], bias=lnT[:, gi + 1, c:c + 1])
                return ins
            act(ev, r=[PS[psb], "lnT"], w=[dstname])

        def kv_group(*a, **k):
            exhaust(kv_group_g(*a, **k))

        def kv_group_g(ntile, tok0, blk0, kname, vname, own_rows=None, kout=None, vout=None, onebank=None, pre=None):
            n = ntile * 128
            for hp in range(2):
                wsb, wn = wload(w_in, 512 + hp * 256) if pre is None else pre[hp]
                for hh in range(2):
                    h = 2 * hp + hh
                    bank = hh if onebank is None else onebank

                    def mm(e, wsb=wsb, hh=hh, bank=bank):
                        ins = None
                        for k in range(8):
                            ins = e.matmul(ps[bank][:, 0:n], lhsT=wsb[:, k, hh * 128:(hh + 1) * 128], rhs=x0T[:, k, 0:n],
                                           start=(k == 0), stop=(k == 7))
                        return ins
                    pe(mm, r=[wn, "x0T"], w=[PS[bank]])
                    dve(lambda e, h=h, bank=bank: e.tensor_copy(out=KT[:, h, tok0:tok0 + n], in_=ps[bank][:, 0:n]),
                        r=[PS[bank]], w=[kname])
                    yield
                if kout is not None:
                    for t in range(ntile):
                        bank = 2 + (t % 2) if onebank is None else onebank

                        def mm(e, wsb=wsb, t=t, bank=bank):
                            ins = None
                            for k in range(8):
                                ins = e.matmul(ps[bank][:, 0:256], lhsT=x0T[:, k, t * 128:(t + 1) * 128], rhs=wsb[:, k, :],
                                               start=(k == 0), stop=(k == 7))
                            return ins
                        pe(mm, r=[wn, "x0T"], w=[PS[bank]])
                        oi = ost_state[0] % 2
                        ost_state[0] += 1
                        dve(lambda e, oi=oi, bank=bank: e.tensor_copy(out=ost[oi][:, :], in_=ps[bank][:, 0:256]),
                            r=[PS[bank]], w=["ost%d" % oi])
                        dma("sp", kout[own_rows + t * 128:own_rows + (t + 1) * 128, hp * 256:(hp + 1) * 256], ost[oi][:, :],
                            ["ost%d" % oi], [], "ost%d" % oi)
                        yield
            for hp in range(2):
                wsb, wn = wload(w_in, 1024 + hp * 256) if pre is None else pre[2 + hp]
                for t in range(ntile):
                    bank = 4 + (t % 2) if onebank is None else onebank

                    def mm(e, wsb=wsb, t=t, bank=bank):
                        ins = None
                        for k in range(8):
                            ins = e.matmul(ps[bank][:, 0:256], lhsT=x0T[:, k, t * 128:(t + 1) * 128], rhs=wsb[:, k, :],
                                           start=(k == 0), stop=(k == 7))
                        return ins
                    pe(mm, r=[wn, "x0T"], w=[PS[bank]])
                    if vout is None:
                        act(lambda e, t=t, bank=bank, hp=hp: e.copy(out=Vt[:, blk0 + t, hp * 256:(hp + 1) * 256], in_=ps[bank][:, 0:256]),
                            r=[PS[bank]], w=[vname])
                        yield
                    else:
                        oi = ost_state[0] % 2
                        ost_state[0] += 1
                        dve(lambda e, oi=oi, bank=bank: e.tensor_copy(out=ost[oi][:, :], in_=ps[bank][:, 0:256]),
                            r=[PS[bank]], w=["ost%d" % oi])
                        act(lambda e, t=t, oi=oi, hp=hp: e.copy(out=Vt[:, blk0 + t, hp * 256:(hp + 1) * 256], in_=ost[oi][:, :]),
                            r=["ost%d" % oi], w=[vname])
                        dma("sp", vout[own_rows + t * 128:own_rows + (t + 1) * 128, hp * 256:(hp + 1) * 256], ost[oi][:, :],
                            ["ost%d" % oi], [], "ost%d" % oi)
                        yield

        def q_conv(*a, **k):
            exhaust(q_conv_g(*a, **k))

        def q_conv_g(ntile, nseg, hist_fn, conv_out_fn, prehist_fn=None):
            n = ntile * 128
            L = n // nseg
            for hp in range(2):
                wsb, wn = wload(w_in, hp * 256)
                for hh in range(2):
                    h = 2 * hp + hh
                    bank = hh

                    def mm(e, wsb=wsb, hh=hh, bank=bank):
                        ins = None
                        for k in range(8):
                            ins = e.matmul(ps[bank][:, 0:n], lhsT=wsb[:, k, hh * 128:(hh + 1) * 128], rhs=x0T[:, k, 0:n],
                                           start=(k == 0), stop=(k == 7))
                        return ins
                    pe(mm, r=[wn, "x0T"], w=[PS[bank]])
                    act(lambda e, h=h, bank=bank: e.copy(out=QT[:, h, 0:n], in_=ps[bank][:, 0:n]), r=[PS[bank]], w=["QT"])
                    yield
            for cpair in range(2):
                wh, whn = wload(w_in, 2560 + cpair * 256)
                wc, wcn = wload(w_in, 2048 + cpair * 256)
                wb, wbn = wload(w_in, 1536 + cpair * 256)
                for ci in range(2):
                    cch = 2 * cpair + ci

                    def proj(bank, wsb, wn, rhs_ap, nn, rname, ci=ci):
                        def mm(e):
                            ins = None
                            for k in range(8):
                                ins = e.matmul(ps[bank][:, 0:nn], lhsT=wsb[:, k, ci * 128:(ci + 1) * 128], rhs=rhs_ap(k),
                                               start=(k == 0), stop=(k == 7))
                            return ins
                        pe(mm, r=[wn, rname], w=[PS[bank]])
                    u3 = wk[1][:, 0:nseg * (L + 2)].rearrange("p (s l) -> p s l", s=nseg)
                    a3 = wk[2][:, 0:n].rearrange("p (s l) -> p s l", s=nseg)
                    hb = 2 if cch % 2 == 0 else 0
                    cb = hb + 1
                    proj(hb, wh, whn, lambda k: x0T[:, k, 0:n], n, "x0T")
                    act(lambda e, hb=hb: e.copy(out=wk[0][:, 0:n], in_=ps[hb][:, 0:n]), r=[PS[hb]], w=["wk0"])
                    proj(cb, wc, wcn, lambda k: x0T[:, k, 0:n], n, "x0T")
                    dve(lambda e, u3=u3, cb=cb: e.tensor_mul(out=u3[:, :, 2:L + 2], in0=ps[cb][:, 0:n].rearrange("p (s l) -> p s l", s=nseg),
                                                             in1=wk[0][:, 0:n].rearrange("p (s l) -> p s l", s=nseg)),
                        r=[PS[cb], "wk0"], w=["wk1"])
                    if prehist_fn is not None and ci == 0:
                        prehist_fn(cpair, wh, whn, wc, wcn)
                    hist_fn(cch, ci, u3, wh, whn, wc, wcn, proj, hb, cb)
                    dve(lambda e, u3=u3, a3=a3, cch=cch: e.tensor_scalar_mul(out=a3, in0=u3[:, :, 0:L], scalar1=convw[:, cch, 0:1]),
                        r=["wk1", "convw"], w=["wk2"])
                    dve(lambda e, u3=u3, a3=a3, cch=cch: e.scalar_tensor_tensor(out=a3, in0=u3[:, :, 1:L + 1], scalar=convw[:, cch, 1:2], in1=a3,
                                                                                  op0=ALU.mult, op1=ALU.add), r=["wk1", "wk2"], w=["wk2"])
                    dve(lambda e, u3=u3, a3=a3, cch=cch: e.scalar_tensor_tensor(out=a3, in0=u3[:, :, 2:L + 2], scalar=convw[:, cch, 2:3], in1=a3,
                                                                                  op0=ALU.mult, op1=ALU.add), r=["wk1", "wk2"], w=["wk2"])
                    proj(hb, wb, wbn, lambda k: x0T[:, k, 0:n], n, "x0T")
                    dve(lambda e, cch=cch, hb=hb: e.tensor_mul(out=catT[:, 4 + cch, 0:n], in0=ps[hb][:, 0:n], in1=wk[2][:, 0:n]),
                        r=[PS[hb], "wk2"], w=["catT"])
                    conv_out_fn(cch, u3, L)
                    yield

        def attn_finish1(n):
            dve(lambda e: e.tensor_copy(out=wk[0][:, 0:n], in_=ps[4][:, 0:n]), r=[PS[4]], w=["wk0"])
            dve(lambda e: e.tensor_copy(out=wk[1][:, 0:n], in_=ps[5][:, 0:n]), r=[PS[5]], w=["wk1"])
            dve(lambda e: e.tensor_copy(out=wk[2][0:64, 0:n], in_=ps[6][0:64, 0:n]), r=[PS[6]], w=["wk2"])

        def attn_finish2(h, n, cc0):
            for m in range(2):
                pe(lambda e, m=m: e.matmul(ps[7][:, 0:n], lhsT=sel[:, m, :], rhs=wk[2][0:64, 0:n], start=True, stop=True),
                   r=["wk2", "sel"], w=[PS[7]])
                act(lambda e: e.activation(out=wk[3][:, 0:n], in_=ps[7][:, 0:n], func=AF.Ln), r=[PS[7]], w=["wk3"])
                act(lambda e: e.activation(out=wk[3][:, 0:n], in_=wk[3][:, 0:n], func=AF.Exp, scale=-1.0), r=["wk3"], w=["wk3"])
                yield
                dve(lambda e, m=m: e.tensor_mul(out=wk[m][:, 0:n], in0=wk[m][:, 0:n], in1=wk[3][:, 0:n]), r=["wk%d" % m, "wk3"], w=["wk%d" % m])
                yield
            dve(lambda e: e.scalar_tensor_tensor(out=wk[0][:, 0:n], in0=wk[1][:, 0:n], scalar=nlam[:, 0:1], in1=wk[0][:, 0:n],
                                                 op0=ALU.mult, op1=ALU.add), r=["wk0", "wk1", "nlam"], w=["wk0"])
            dve(lambda e: e.tensor_mul(out=sqb[:, 0:n], in0=wk[0][:, 0:n], in1=wk[0][:, 0:n]), r=["wk0"], w=["sqb"])
            yield
            pe(lambda e: e.matmul(ps[7][:, 0:n], lhsT=ones[:, :], rhs=sqb[:, 0:n], start=True, stop=True), r=["sqb", "ones"], w=[PS[7]])
            act(lambda e: e.activation(out=wk[3][:, 0:n], in_=ps[7][:, 0:n], func=AF.Ln, bias=epsc[:, :], scale=1.0 / 128.0),
                r=[PS[7]], w=["wk3"])
            act(lambda e: e.activation(out=wk[3][:, 0:n], in_=wk[3][:, 0:n], func=AF.Exp, scale=-0.5), r=["wk3"], w=["wk3"])
            yield
            dve(lambda e: e.scalar_tensor_tensor(out=catT[:, h, cc0:cc0 + n], in0=wk[0][:, 0:n], scalar=gsc[:, 0:1], in1=wk[3][:, 0:n],
                                                 op0=ALU.mult, op1=ALU.mult), r=["wk0", "wk3", "gsc"], w=["catT"])

        def attention(blocks_for_head, ncol, qcol0=0, background=None):
            pt_i = [0]
            pending_fin = None
            for h in range(4):
                blks = blocks_for_head(h)
                nb = len(blks)
                pend = None
                first = [True, True]

                def issue_pv(b, pt, ptn):
                    p0, p1 = b["prange"]
                    c0, c1 = b["c0"], b["c1"]
                    st = first[0]
                    first[0] = False
                    last = b["last"]
                    for m, ob in enumerate((4, 5)):
                        pe(lambda e, b=b, ob=ob, m=m: e.matmul(ps[ob][:, c0:c1], lhsT=b["v"], rhs=pt[p0:p1, m * 512 + c0:m * 512 + c1],
                                                                 start=st, stop=last),
                           r=["%s_%d" % (ptn, m), b["vname"]], w=[PS[ob]])
                    for m in range(2):
                        pe(lambda e, m=m: e.matmul(ps[6][32 * m:32 * m + 32, c0:c1], lhsT=ones[p0:p1, 0:32],
                                                   rhs=pt[p0:p1, m * 512 + c0:m * 512 + c1], start=st, stop=last),
                           r=["%s_%d" % (ptn, m), "ones"], w=[PS[6]])

                for bi, b in enumerate(blks):
                    b["last"] = (bi == nb - 1)
                    sp_ = (bi % 2)
                    c0, c1 = b["c0"], b["c1"]
                    p0, p1 = b["prange"]
                    nbias = len(b["bias"])
                    for m in range(2):
                        kap = b["kt"][m]
                        qap = QT[m * 64:(m + 1) * 64, h, qcol0 + c0:qcol0 + c1]
                        bank = 2 * sp_ + m
                        pe(lambda e, kap=kap, qap=qap, bank=bank, nbias=nbias, p0=p0, p1=p1, c0=c0, c1=c1:
                           e.matmul(ps[bank][p0:p1, c0:c1], lhsT=kap, rhs=qap, start=True, stop=(nbias == 0)),
                           r=[b["kname"], "QT"], w=[PS[bank]])
                        for bj, (bl, br, oc0, oc1) in enumerate(b["bias"]):
                            pe(lambda e, bl=bl, br=br, oc0=oc0, oc1=oc1, bank=bank, lastb=(bj == nbias - 1), p0=p0, p1=p1:
                               e.matmul(ps[bank][p0:p1, oc0:oc1], lhsT=bl, rhs=br, start=False, stop=lastb),
                               r=["Jm", "H01", "HS", "Mf"], w=[PS[bank]])
                    i = pt_i[0] % 3
                    pt_i[0] += 1
                    for m in range(2):
                        act(lambda e, m=m, i=i, bank=2 * sp_ + m, p0=p0, p1=p1, c0=c0, c1=c1: e.activation(out=PT[i][p0:p1, m * 512 + c0:m * 512 + c1], in_=ps[bank][p0:p1, c0:c1],
                                                                                func=AF.Exp, scale=0.125),
                            r=[PS[2 * sp_ + m]], w=["PT%d_%d" % (i, m)])
                    if pend is not None:
                        issue_pv(*pend)
                    pend = (b, PT[i], "PT%d" % i)
                    if bi >= 2 and pending_fin is not None:
                        next(pending_fin, None)
                    if background is not None:
                        background()
                issue_pv(*pend)
                if pending_fin is not None:
                    exhaust(pending_fin)
                attn_finish1(ncol)
                pending_fin = attn_finish2(h, ncol, qcol0)
            exhaust(pending_fin)

        def out_proj_mm(ntile):
            for dq in range(4):
                wsb, wn = wload(w_out, dq * 256)
                for t in range(ntile):
                    bank = 2 * t + dq // 2

                    def mm(e, wsb=wsb, t=t, bank=bank, dq=dq):
                        ins = None
                        for k in range(8):
                            ins = e.matmul(ps[bank][:, (dq % 2) * 256:(dq % 2) * 256 + 256], lhsT=catT[:, k, t * 128:(t + 1) * 128],
                                           rhs=wsb[:, k, :], start=(k == 0), stop=(k == 7))
                        return ins
                    pe(mm, r=[wn, "catT"], w=[PS[bank]])

        def ln1_load(t, src_rows):
            i = x_state[0] % 2
            x_state[0] += 1
            dma("sp", xst[i][:, :], src_rows(t), [], ["xst%d" % i], "xst%d" % i)
            return i

        def ln1_chain(t, src_rows, stat_idx0, xsc_row0, i=None):
            if i is None:
                i = ln1_load(t, src_rows)
            xn_ = "xst%d" % i
            x_ = xst[i]
            sa = st0[:, stat_idx0 + t, :]
            dve(lambda e, sa=sa: e.tensor_scalar(out=stt[:, 16:17], in0=sa[:, 0:1], scalar1=sa[:, 1:2], scalar2=-1.0,
                                                 op0=ALU.mult, op1=ALU.mult), r=["st0"], w=["stt4"])
            act(lambda e, x_=x_, sa=sa: e.activation(out=x_[:, :], in_=x_[:, :], func=AF.Identity, scale=sa[:, 1:2], bias=stt[:, 16:17]),
                r=[xn_, "st0", "stt4"], w=[xn_])
            dve(lambda e, x_=x_: e.tensor_mul(out=x_[:, :], in0=x_[:, :], in1=GB0[:, 0, :]), r=[xn_, "GB0"], w=[xn_])
            dve(lambda e, x_=x_: e.tensor_add(out=x_[:, :], in0=x_[:, :], in1=GB0[:, 1, :]), r=[xn_, "GB0"], w=[xn_])
            for hf in range(2):
                dve(lambda e, x_=x_, hf=hf, t=t: e.scalar_tensor_tensor(out=x_[:, hf * 512:(hf + 1) * 512], in0=x_[:, hf * 512:(hf + 1) * 512],
                                                                        scalar=ALPHA, in1=ps[2 * t + hf][:, :], op0=ALU.mult, op1=ALU.add),
                    r=[xn_, PS[2 * t + hf]], w=[xn_])
            dve(lambda e, x_=x_: e.bn_stats(stt[:, 0:6], x_[:, 0:512]), r=[xn_], w=["stt"])
            dve(lambda e, x_=x_: e.bn_stats(stt[:, 6:12], x_[:, 512:1024]), r=[xn_], w=["stt"])
            dve(lambda e: e.bn_aggr(stt[:, 12:14], stt[:, 0:12]), r=["stt"], w=["stt2"])
            rstd_from_var(stt[:, 13:14], stt[:, 13:14], ["stt2"], ["stt2"])
            dve(lambda e: e.tensor_scalar(out=stt[:, 17:18], in0=stt[:, 12:13], scalar1=stt[:, 13:14], scalar2=-1.0,
                                          op0=ALU.mult, op1=ALU.mult), r=["stt2"], w=["stt5"])
            act(lambda e, x_=x_: e.activation(out=x_[:, :], in_=x_[:, :], func=AF.Identity, scale=stt[:, 13:14], bias=stt[:, 17:18]),
                r=[xn_, "stt2", "stt5"], w=[xn_])
            dma("sp", xsc[xsc_row0 + t * 128:xsc_row0 + (t + 1) * 128, :], x_[:, :], [xn_], ["xsc"], xn_)

        def out_proj_ln1(ntile, src_rows, stat_idx0, xsc_row0):
            out_proj_mm(ntile)
            for t in range(ntile):
                ln1_chain(t, src_rows, stat_idx0, xsc_row0)

        KTALL = ["KT%d" % g for g in range(NG)]
        VALL = ["V%d" % g for g in range(NG)]
        CG = min(4, NCB)
        for s in range(2):
            VG = min(8, NCB)
            for v8 in range(NCB // VG):
                dma("pool", Vt[:, s * SBLK + v8 * VG:s * SBLK + (v8 + 1) * VG, :],
                    cv[s, v8 * VG * 128:(v8 + 1) * VG * 128, :].rearrange("(b p) c -> p b c", p=128), [], ["Vc%d_%d" % (s, v8)], "vc%d_%d" % (s, v8))
            for b4 in range(NCB // CG):
                i = ws_state[0] % NWS
                ws_state[0] += 1
                ck4 = wsl[i][:, :, :].rearrange("p a b -> p (a b)").rearrange("p (g c) -> p g c", c=512)
                dma("pool", ck4[:, 0:CG, :], ck[s, b4 * CG * 128:(b4 + 1) * CG * 128, :].rearrange("(b p) c -> p b c", p=128),
                    [], ["wsl%d" % i], "wsl%d" % i)
                for bb in range(CG):
                    blk = b4 * CG + bb
                    pb = ps_bf(blk % 2)

                    def tr(e, ck4=ck4, pb=pb, bb=bb):
                        ins = None
                        for h in range(4):
                            ins = e.transpose(pb[:, h * 128:(h + 1) * 128], ck4[:, bb, h * 128:(h + 1) * 128], ident[:, :])
                        return ins
                    pe(tr, r=["wsl%d" % i, "ident"], w=[PS[blk % 2]])
                    t0 = s * SOFF + blk * 128
                    dve(lambda e, pb=pb, t0=t0: e.tensor_copy(out=KT[:, :, t0:t0 + 128], in_=pb[:, 0:512].rearrange("p (h t) -> p h t", h=4)),
                        r=[PS[blk % 2]], w=["KTc%d_%d" % (s, blk)])
        S.add("pe", None, r=["Vc%d_%d" % (s, v8) for s in range(2) for v8 in range(NCB // min(8, NCB))] +
              ["KTc%d_%d" % (s, blk) for s in range(2) for blk in range(NCB)], w=["V0", "KT0"])
        S.mark(1)
        ln_tile(xs[:, :], st0[:, SST, :], "st0", 0, x0T, "x0T", 0)
        S.mark(2)
        kv_group(1, SK0, VNEW, "KT0", "V0", own_rows=0, kout=ksn, vout=vsn)

        def hist_sample(cch, ci, u3, wh, whn, wc, wcn, proj, hb, cb):
            dma("sp", uh[:, 0:2, :], cc[:, cch, :, :], [], ["uh"], "uh")
            dve(lambda e, u3=u3: e.tensor_copy(out=u3[:, :, 0:2], in_=uh[:, 0:2, :]), r=["uh"], w=["wk1"])

        def cout_sample(cch, u3, L):
            dve(lambda e, u3=u3: e.tensor_copy(out=uh[:, 2:4, :], in_=u3[:, :, L:L + 2]), r=["wk1"], w=["uh2"])
            dma("sp", csn[:, cch, :, :], uh[:, 2:4, :], ["uh2"], [], "uh2")

        S.mark(3)
        q_conv(1, 2, hist_sample, cout_sample)
        S.mark(4)

        for s in range(2):
            def blocks_sample(h, s=s):
                out = []
                for blk in range(NCB):
                    t0 = s * SOFF + blk * 128
                    bias = []
                    if blk == NCB - 1:
                        bias = [(Jm[:, :], H01[:, h, 128:192], 0, 64)]
                    out.append(dict(kt=(KT[0:64, h, t0:t0 + 128], KT[64:128, h, t0:t0 + 128]), kname="KT0",
                                    v=Vt[:, s * SBLK + blk, h * 128:(h + 1) * 128], vname="V0", c0=0, c1=64, prange=(0, 128), bias=bias))
                t0 = SK0 + s * 64
                p0 = s * 64
                out.append(dict(kt=(KT[0:64, h, t0:t0 + 64], KT[64:128, h, t0:t0 + 64]), kname="KT0",
                                v=Vt[p0:p0 + 64, VNEW, h * 128:(h + 1) * 128], vname="V0", c0=0, c1=64, prange=(p0, p0 + 64),
                                bias=[(Jm[:, 0:64], H01[:, h, 0:64], 0, 64)]))
                return out
            attention(blocks_sample, 64, qcol0=s * 64)
        S.mark(5)
        out_proj_ln1(1, lambda t: xs[:, :], SST, NSLOT * GT)
        S.mark(6)
        S.barrier()

        def prep_steps(pos, onebank):
            own = (pos % 2 == 1)
            j = pos // 2
            pre = [wload(w_in, 512), wload(w_in, 768), wload(w_in, 1024), wload(w_in, 1280)]
            for t in range(4):
                sa = st0[:, j * 4 + t, :] if own else stt[:, 14:16]
                yield from ln_tile_g(xp[pos * GT + t * 128:pos * GT + (t + 1) * 128, :], sa, "st0" if own else "stt3", 0, x0T, "x0T", t * 128)
            yield from kv_group_g(4, pos * GT, pos * 4, "KT%d" % pos, "V%d" % pos,
                                  own_rows=j * GT if own else None, kout=kp if own else None, vout=vp if own else None, onebank=onebank, pre=pre)
            if not own:
                return
            i = x_state[0] % 2
            x_state[0] += 1
            xn_, xb_ = "xst%d" % i, "xnb0"
            dma("sp", xst[i][0:2, :], xh[j, :, :], [], [xn_], xn_)
            dve(lambda e, i=i: e.bn_stats(stt[:, 0:6], xst[i][:, 0:512]), r=[xn_], w=["stt"])
            dve(lambda e, i=i: e.bn_stats(stt[:, 6:12], xst[i][:, 512:1024]), r=[xn_], w=["stt"])
            dve(lambda e: e.bn_aggr(stt[:, 14:16], stt[:, 0:12]), r=["stt"], w=["stt3"])
            rstd_from_var(stt[:, 15:16], stt[:, 15:16], ["stt3"], ["stt3"])
            dve(lambda e, i=i: e.tensor_scalar(out=xnb[i][:, :], in0=xst[i][:, :], scalar1=stt[:, 14:15], scalar2=stt[:, 15:16],
                                               op0=ALU.subtract, op1=ALU.mult), r=[xn_, "stt3"], w=[xb_])
            yield
            yield
            yield
            transpose_affine(xnb[i], xb_, 0, xhT, "xhT", 0, 2)
            yield

        def chain2(a, b):
            yield from a
            yield from b

        def make_bg(gen, ncalls, nsteps):
            state = dict(calls=ncalls, steps=nsteps)

            def bg():
                if state["steps"] <= 0:
                    return
                k = -(-state["steps"] // max(state["calls"], 1))
                state["calls"] -= 1
                for _ in range(k):
                    if next(gen, "END") == "END":
                        state["steps"] = 0
                        return
                    state["steps"] -= 1
            return bg

        exhaust(prep_steps(0, None))
        exhaust(prep_steps(1, None))
        def make_hist(j):
            def prehist(cpair, wh, whn, wc, wcn, j=j):
                for ci in range(2):
                    for ty, (wsb, wn) in enumerate(((wh, whn), (wc, wcn))):
                        col = (ci * 2 + ty) * 2

                        def mm(e, wsb=wsb, ci=ci, col=col):
                            ins = None
                            for k in range(8):
                                ins = e.matmul(ps[1][:, col:col + 2], lhsT=wsb[:, k, ci * 128:(ci + 1) * 128], rhs=xhT[:, k, 0:2],
                                               start=(k == 0), stop=(k == 7))
                            return ins
                        pe(mm, r=[wn, "xhT"], w=[PS[1]])
                v = ps[1][:, 0:8].rearrange("p (c t r) -> p c t r", c=2, t=2)
                dst = uhh[:, 2 * cpair:2 * cpair + 2, :]
                dve(lambda e: e.tensor_copy(out=dst, in_=v[:, :, 0, :]), r=[PS[1]], w=["uhh"])
                dve(lambda e: e.tensor_mul(out=dst, in0=v[:, :, 1, :], in1=dst), r=[PS[1], "uhh"], w=["uhh"])
                dve(lambda e: e.tensor_scalar_mul(out=dst, in0=dst, scalar1=flags[:, j:j + 1]), r=["uhh", "flags"], w=["uhh"])

            def hist_prompt(cch, ci, u3, wh, whn, wc, wcn, proj, hb, cb, j=j):
                dve(lambda e, u3=u3: e.tensor_copy(out=u3[:, 0, 0:2], in_=uhh[:, cch, :]), r=["uhh"], w=["wk1"])

            def cout_prompt(cch, u3, L, j=j):
                if j == NSLOT - 1:
                    dve(lambda e, u3=u3: e.tensor_copy(out=uh[:, 2, :], in_=u3[:, 0, L:L + 2]), r=["wk1"], w=["uh2"])
                    dma("sp", cp[:, cch, :], uh[:, 2, :], ["uh2"], [], "uh2")
            return hist_prompt, cout_prompt, prehist

        def steps(g, k):
            for _ in range(k):
                if next(g, "END") == "END":
                    return

        q_conv(4, 1, *make_hist(0))
        for pos in range(1, NG, 2):
            j = pos // 2
            sidx = j % 2

            def blocks_prompt(h, pos=pos, j=j, sidx=sidx):
                out = []
                for kpos in range(pos + 1):
                    for b in range(4):
                        t0 = kpos * GT + b * 128
                        c0 = 0
                        bias = []
                        if kpos == pos:
                            c0 = b * 128
                            if b < 3:
                                bias = [(Jm[:, :], H01[:, h, :], c0, c0 + 256)]
                            else:
                                bias = [(Jm[:, :], H01[:, h, 0:128], c0, c0 + 128)]
                        elif kpos == pos - 1:
                            bias = [(ident[:, :], Mf[:, sidx, :], 0, 512)]
                            if b == 3:
                                bias.append((Jm[:, :], HS[:, 1 - sidx, h, :], 0, 128))
                        elif kpos == pos - 2 and b == 3:
                            bias = [(Jm[:, :], HS[:, sidx, h, :], 0, 128)]
                        out.append(dict(kt=(KT[0:64, h, t0:t0 + 128], KT[64:128, h, t0:t0 + 128]), kname="KT%d" % kpos,
                                        v=Vt[:, kpos * 4 + b, h * 128:(h + 1) * 128], vname="V%d" % kpos, c0=c0, c1=512,
                                        prange=(0, 128), bias=bias))
                return out
            if j < NSLOT - 1:
                bgen = chain2(prep_steps(pos + 1, 7), prep_steps(pos + 2, 7))
                bgf = make_bg(bgen, 4 * (pos + 1) * 4, 86)
            else:
                bgen, bgf = iter(()), None
            attention(blocks_prompt, 512, background=bgf)
            if dbg_cat is not None:
                dma("sp", dbg_cat[j], catT[:, :, :], ["catT"], [], "dbgc")
                if j == 0:
                    dbg_q = nc.dram_tensor("dbg_q", [128, 4, 512], BF16, kind="ExternalOutput").ap()
                    dbg_k = nc.dram_tensor("dbg_k", [128, 4, 1024], BF16, kind="ExternalOutput").ap()
                    dma("sp", dbg_q, QT[:, :, :], ["QT"], [], "dbgc")
                    dma("sp", dbg_k, KT[:, :, 0:1024], ["KT0", "KT1"], [], "dbgc")
                    dbg_v = nc.dram_tensor("dbg_v", [128, 8, 512], BF16, kind="ExternalOutput").ap()
                    dma("sp", dbg_v, Vt[:, 0:8, :], ["V0", "V1"], [], "dbgc")
            exhaust(bgen)
            srcf = lambda t, pos=pos: xp[pos * GT + t * 128:pos * GT + (t + 1) * 128, :]
            xi0 = ln1_load(0, srcf)
            xi1 = ln1_load(1, srcf)
            out_proj_mm(4)
            if j == NSLOT - 1:
                for c in range(8):
                    dma("pool", W1s[:, :, c * 512:(c + 1) * 512], w1[:, c * 512:(c + 1) * 512].rearrange("(k p) c -> p k c", p=128),
                        [], ["W1_%d" % c] + (KTALL + VALL if c == 0 else []), "W1_%d" % c)
                for c in range(8):
                    dma("pool", W2s[:, c * 4:(c + 1) * 4, :], w2[c * 512:(c + 1) * 512, :].rearrange("(k p) c -> p k c", p=128),
                        [], ["W2_%d" % c], "W2_%d" % c)
            qg = q_conv_g(4, 1, *make_hist(j + 1)) if j + 1 < NSLOT else iter(())
            ln1_chain(0, srcf, j * 4, j * GT, xi0)
            xi2 = ln1_load(2, srcf)
            ln1_chain(1, srcf, j * 4, j * GT, xi1)
            xi3 = ln1_load(3, srcf)
            steps(qg, 3)
            ln1_chain(2, srcf, j * 4, j * GT, xi2)
            steps(qg, 3)
            ln1_chain(3, srcf, j * 4, j * GT, xi3)
            exhaust(qg)

        S.mark(7)
        S.barrier()
        esA.__exit__(None, None, None)
        BAR = S.bar_ids
        GB = sb("GB12", [128, 4, D], F32)
        xres = [sb("xres%d" % i, [128, D], F32) for i in range(2)]
        xb2 = [sb("xb2_0", [128, D], BF16)] * 2
        x1T = sb("x1T", [128, 8, 512], BF16)
        hT = sb("hT", [128, 32, 512], BF16)
        rl = [sb("rl%d" % i, [128, 512], F32) for i in range(2)]
        stb = sb("stb", [128, 16], F32)
        dma("sp", GB[:, :, :], lnB_d[:, 2:6, :], BAR, ["GB12"], "c_GB12")
        W1N = ["W1_%d" % c for c in range(8)]
        W2N = ["W2_%d" % c for c in range(8)]
        xr_state = [0]
        xld = sb("xld", [128, D], F32)

        def grp(g):
            ntile = 4 if g < NSLOT else 1
            return ntile, g * GT

        def prep_load(g, t):
            _, row0 = grp(g)
            dma("sp", xld[:, :], xsc[row0 + t * 128:row0 + (t + 1) * 128, :], ["xsc"], ["xld"], "xld")
            dve(lambda e: e.tensor_copy(out=xb2[0][:, :], in_=xld[:, :]), r=["xld"], w=["xb2_0"])

        def prep_tr(g, t):
            pb = ps_bf(t % 2)

            def tr(e, pb=pb):
                ins = None
                for c in range(8):
                    ins = e.transpose(pb[:, c * 128:(c + 1) * 128], xb2[0][:, c * 128:(c + 1) * 128], ident[:, :])
                return ins
            pe(tr, r=["xb2_0", "ident"], w=[PS[t % 2]])

            def ev(e, pb=pb, t=t):
                ins = None
                for c in range(8):
                    ins = e.activation(out=x1T[:, c, t * 128:(t + 1) * 128], in_=pb[:, c * 128:(c + 1) * 128], func=AF.Identity,
                                       scale=lnT[:, 2, c:c + 1], bias=lnT[:, 3, c:c + 1])
                return ins
            act(ev, r=[PS[t % 2], "lnT"], w=["x1T"])

        for t in range(4):
            prep_load(0, t)
            prep_tr(0, t)
        for g in range(NSLOT + 1):
            ntile, row0 = grp(g)
            n = ntile * 128
            outd = yp if g < NSLOT else ys
            orow0 = row0 if g < NSLOT else 0
            nnext = grp(g + 1)[0] if g < NSLOT else 0
            for fc in range(32):
                bank = 2 + fc % 2

                def mm(e, fc=fc, bank=bank, n=n):
                    ins = None
                    for k in range(8):
                        ins = e.matmul(ps[bank][:, 0:n], lhsT=W1s[:, k, fc * 128:(fc + 1) * 128], rhs=x1T[:, k, 0:n],
                                       start=(k == 0), stop=(k == 7))
                    return ins
                pe(mm, r=[W1N[fc // 4], "x1T"], w=[PS[bank]])
                ri = fc % 2
                act(lambda e, ri=ri, bank=bank, n=n: e.activation(out=rl[ri][:, 0:n], in_=ps[bank][:, 0:n], func=AF.Relu),
                    r=[PS[bank]], w=["rl%d" % ri])
                dve(lambda e, ri=ri, fc=fc, n=n: e.tensor_mul(out=hT[:, fc, 0:n], in0=rl[ri][:, 0:n], in1=rl[ri][:, 0:n]),
                    r=["rl%d" % ri], w=["hT"])
            for t in range(max(ntile, nnext)):
                if t < nnext:
                    prep_load(g + 1, t)
                if t < ntile:
                    i = xr_state[0] % 2
                    xr_state[0] += 1
                    xn_ = "xres%d" % i
                    x_ = xres[i]
                    dma("sp", x_[:, :], xsc[row0 + t * 128:row0 + (t + 1) * 128, :], ["xsc"], [xn_], xn_)
                    dve(lambda e, x_=x_: e.tensor_mul(out=x_[:, :], in0=x_[:, :], in1=GB[:, 0, :]), r=[xn_, "GB12"], w=[xn_])
                    dve(lambda e, x_=x_: e.tensor_add(out=x_[:, :], in0=x_[:, :], in1=GB[:, 1, :]), r=[xn_, "GB12"], w=[xn_])
                    for hf in range(2):
                        bank = 4 + (t % 2) * 2 + hf

                        def mm(e, t=t, hf=hf, bank=bank):
                            ins = None
                            for fc in range(32):
                                ins = e.matmul(ps[bank][:, :], lhsT=hT[:, fc, t * 128:(t + 1) * 128], rhs=W2s[:, fc, hf * 512:(hf + 1) * 512],
                                               start=(fc == 0), stop=(fc == 31))
                            return ins
                        pe(mm, r=W2N + ["hT"], w=[PS[bank]])
                if t < nnext:
                    prep_tr(g + 1, t)
                if t < ntile:
                    for hf in range(2):
                        bank = 4 + (t % 2) * 2 + hf
                        dve(lambda e, x_=x_, hf=hf, bank=bank: e.scalar_tensor_tensor(out=x_[:, hf * 512:(hf + 1) * 512], in0=x_[:, hf * 512:(hf + 1) * 512],
                                                                                      scalar=ALPHA, in1=ps[bank][:, :], op0=ALU.mult, op1=ALU.add),
                            r=[xn_, PS[bank]], w=[xn_])
                    dve(lambda e, x_=x_: e.bn_stats(stb[:, 0:6], x_[:, 0:512]), r=[xn_], w=["stb"])
                    dve(lambda e, x_=x_: e.bn_stats(stb[:, 6:12], x_[:, 512:1024]), r=[xn_], w=["stb"])
                    dve(lambda e: e.bn_aggr(stb[:, 12:14], stb[:, 0:12]), r=["stb"], w=["stb2"])
                    rstd_from_var(stb[:, 13:14], stb[:, 13:14], ["stb2"], ["stb2"])
                    dve(lambda e, x_=x_: e.tensor_scalar(out=x_[:, :], in0=x_[:, :], scalar1=stb[:, 12:13], scalar2=stb[:, 13:14],
                                                         op0=ALU.subtract, op1=ALU.mult), r=[xn_, "stb2"], w=[xn_])
                    dve(lambda e, x_=x_: e.tensor_mul(out=x_[:, :], in0=x_[:, :], in1=GB[:, 2, :]), r=[xn_, "GB12"], w=[xn_])
                    dve(lambda e, x_=x_: e.tensor_add(out=x_[:, :], in0=x_[:, :], in1=GB[:, 3, :]), r=[xn_, "GB12"], w=[xn_])
                    dma("sp", outd[orow0 + t * 128:orow0 + (t + 1) * 128, :], x_[:, :], [xn_], [], xn_)

        S.emit(nc, es)
    return nc


_NC_CACHE = {}
_RUNNER = [None]


def _get_nc(NG, PAST):
    if (NG, PAST) not in _NC_CACHE:
        _NC_CACHE[(NG, PAST)] = build_nc(NG, PAST)
    return _NC_CACHE[(NG, PAST)]


def kernel(x_prompt, x_sample, cache_k, cache_v, cache_conv, ln0_g, ln0_b, rel_bias, w_in, conv_w,
           lambda_q1, lambda_k1, lambda_q2, lambda_k2, subln_g, w_out, ln1_g, ln1_b,
           w_ff1, w_ff2, ln2_g, ln2_b):
    f = lambda a: np.ascontiguousarray(np.asarray(a, dtype=np.float32))
    x_prompt, x_sample, cache_k, cache_v, cache_conv = map(f, (x_prompt, x_sample, cache_k, cache_v, cache_conv))
    NB, SEQ = x_prompt.shape[0], x_prompt.shape[1]
    NG = SEQ // GT
    NSLOT = NG // 2
    PAST = cache_k.shape[2]
    NCORES = 2 * NB
    assert x_sample.shape[0] == 2 * NCORES and x_sample.shape[1] == DEC
    vecs = [f(v).reshape(-1) for v in (ln0_g, ln0_b, ln1_g, ln1_b, ln2_g, ln2_b)]
    lnT = np.ascontiguousarray(np.stack([v.reshape(8, 128).T for v in vecs], axis=1))
    lnB = np.ascontiguousarray(np.broadcast_to(np.stack(vecs, 0)[None], (128, 6, D)))
    cw = f(conv_w)[0]
    convw = np.ascontiguousarray(cw.reshape(3, 4, 128).transpose(2, 1, 0))
    subg = f(subln_g).reshape(128, 1)
    lam = np.stack([f(lambda_q1)[0], f(lambda_k1)[0], f(lambda_q2)[0], f(lambda_k2)[0]], 0)
    lamv = np.ascontiguousarray(np.broadcast_to(lam[None], (128, 4, 64)))
    rb = f(rel_bias)
    bt = _bucket_table()
    oh = np.zeros((32, 384), np.float32)
    oh[bt, np.arange(384)] = 1.0
    oh[15, :] -= 1.0
    w_in_, w_out_, w1_, w2_ = f(w_in)[0], f(w_out)[0], f(w_ff1)[0], f(w_ff2)[0]

    in_maps = []
    orders = [_order(0, NG), _order(1, NG)]
    for c in range(NCORES):
        b, half = c // 2, c % 2
        order = orders[half]
        xb = x_prompt[b].reshape(NG, GT, D)
        xp = np.ascontiguousarray(xb[order].reshape(SEQ, D))
        xh = np.zeros((NSLOT, 2, D), np.float32)
        flags = np.zeros((128, 16), np.float32)
        for j in range(NSLOT):
            g = order[2 * j + 1]
            if g > 0:
                xh[j] = x_prompt[b, g * GT - 2:g * GT]
                flags[:, j] = 1.0
        for s in range(2):
            sa = 1.0 if (s + half) % 2 == 0 else 0.0
            flags[:, 8 + s] = sa
            flags[:, 10 + s] = sa
        xs = np.ascontiguousarray(x_sample[2 * c:2 * c + 2].reshape(128, D))
        ck = np.ascontiguousarray(cache_k[0, 2 * c:2 * c + 2].reshape(2, PAST, 512))
        cv = np.ascontiguousarray(cache_v[0, 2 * c:2 * c + 2].reshape(2, PAST, 512))
        ccv = cache_conv[0, 2 * c:2 * c + 2]
        cc = np.ascontiguousarray(ccv.reshape(2, 2, 4, 128).transpose(3, 2, 0, 1))
        in_maps.append(dict(xp=xp, xh=xh, xs=xs, ck=ck, cv=cv, cc=cc, w_in=w_in_, w_out=w_out_, w1=w1_, w2=w2_,
                            lnT=lnT, lnB=lnB, convw=convw, subg=subg, lamv=lamv, rb=rb, oh=oh, flags=flags))

    nc = _get_nc(NG, PAST)
    if _RUNNER[0] is not None:
        R = _RUNNER[0](nc, in_maps)
    else:
        R = run_bass_kernel_spmd(nc, in_maps, core_ids=list(range(NCORES))).results
    if os.environ.get("KDBG"):
        _RUNNER.append(R)

    y_prompt = np.zeros((NB, SEQ, D), np.float32)
    y_sample = np.zeros((2 * NCORES, DEC, D), np.float32)
    nk_p = np.zeros((1, NB, SEQ, 4, 128), np.float32)
    nv_p = np.zeros((1, NB, SEQ, 4, 128), np.float32)
    nc_p = np.zeros((1, NB, 2, 512), np.float32)
    nk_s = np.zeros((1, 2 * NCORES, DEC, 4, 128), np.float32)
    nv_s = np.zeros((1, 2 * NCORES, DEC, 4, 128), np.float32)
    nc_s = np.zeros((1, 2 * NCORES, 2, 512), np.float32)
    for c in range(NCORES):
        b, half = c // 2, c % 2
        order = orders[half]
        r = R[c]
        for j in range(NSLOT):
            g = order[2 * j + 1]
            y_prompt[b, g * GT:(g + 1) * GT] = r["yp"][j * GT:(j + 1) * GT]
            nk_p[0, b, g * GT:(g + 1) * GT] = r["kp"][j * GT:(j + 1) * GT].reshape(GT, 4, 128)
            nv_p[0, b, g * GT:(g + 1) * GT] = r["vp"][j * GT:(j + 1) * GT].reshape(GT, 4, 128)
        if order[NG - 1] == NG - 1:
            nc_p[0, b] = r["cp"].transpose(2, 1, 0).reshape(2, 512)
        y_sample[2 * c:2 * c + 2] = r["ys"].reshape(2, DEC, D)
        nk_s[0, 2 * c:2 * c + 2] = r["ksn"].reshape(2, DEC, 4, 128)
        nv_s[0, 2 * c:2 * c + 2] = r["vsn"].reshape(2, DEC, 4, 128)
        nc_s[0, 2 * c:2 * c + 2] = r["csn"].transpose(2, 3, 1, 0).reshape(2, 2, 512)
    return (y_prompt, y_sample, nk_p, nv_p, nc_p, nk_s, nv_s, nc_s)
```

```python
import math
import os
from contextlib import ExitStack

import numpy as np
import concourse.bass as bass
import concourse.mybir as mybir
from concourse.bass_utils import run_bass_kernel_spmd

F32 = mybir.dt.float32
BF16 = mybir.dt.bfloat16
AF = mybir.ActivationFunctionType
ALU = mybir.AluOpType
AX = mybir.AxisListType

D = 1024
GT = 512
DEC = 64
LAM_INIT = 0.8 - 0.6 * math.exp(-0.3 * 0)
ALPHA = (2.0 * 1) ** 0.25
EPS = 1e-5
NEG = -30000.0


class Sched:
    def __init__(self):
        self.ops = []
        self.lastw = {}
        self.readers = {}
        self.dma_cnt = {}
        self.marks = {}

    def mark(self, n):
        self.marks[n] = len(self.ops)

    def add(self, eng, fn, r=(), w=(), dma=False, key=None):
        idx = len(self.ops)
        deps = set()
        for b in r:
            if b in self.lastw:
                deps.add(self.lastw[b])
        for b in w:
            if b in self.lastw:
                deps.add(self.lastw[b])
            deps.update(self.readers.get(b, ()))
        op = dict(eng=eng, fn=fn, deps=deps, dma=dma)
        if dma:
            self.dma_cnt[key] = self.dma_cnt.get(key, 0) + 1
            op["key"] = key
            op["val"] = 16 * self.dma_cnt[key]
        self.ops.append(op)
        for b in r:
            lst = self.readers.setdefault(b, [])
            if not dma:
                for i in range(len(lst) - 1, -1, -1):
                    o = self.ops[lst[i]]
                    if (not o["dma"]) and o["eng"] == eng:
                        del lst[i]
            lst.append(idx)
        for b in w:
            self.lastw[b] = idx
            self.readers[b] = []
        return idx

    def barrier(self):
        allb = list(set(self.lastw) | set(self.readers))
        for eng in ("pe", "act", "dve", "pool", "sp"):
            self.add(eng, None, r=[], w=allb)
        self.bar_ids = []

    def emit(self, nc, es):
        lim = int(os.environ.get("KSTOP", "99"))
        ops = self.ops
        if lim >= 1000:
            self.marks[lim] = lim - 1000
        if os.environ.get("KVERB"):
            print("marks", self.marks, "nops", len(ops))
            for i in range(self.marks.get(2, 0), self.marks.get(3, 0)):
                print(i, ops[i]["eng"], ops[i]["dma"], ops[i].get("key"))
        if lim in self.marks:
            ops = ops[:self.marks[lim]]
            self.dma_cnt = {}
            for o in ops:
                if o["dma"]:
                    self.dma_cnt[o["key"]] = self.dma_cnt.get(o["key"], 0) + 1
        engs = ("pe", "act", "dve", "pool", "sp")
        for i, op in enumerate(ops):
            cmax = {}
            dm = {}
            for d in op["deps"]:
                o = ops[d]
                if o["dma"]:
                    k = o["key"]
                    dm[k] = max(dm.get(k, 0), o["val"])
                else:
                    if o["eng"] == "pe" and op["eng"] == "pe" and not op["dma"]:
                        continue
                    cmax[o["eng"]] = max(cmax.get(o["eng"], -1), d)
            op["cdeps"] = cmax
            op["ddeps"] = dm
        mile = set()
        for op in ops:
            mile.update(op["cdeps"].values())
        cnt = {e: 0 for e in engs}
        for i, op in enumerate(ops):
            if i in mile:
                cnt[op["eng"]] += 1
                op["ms"] = cnt[op["eng"]]
        esem = {e: es.enter_context(nc.semaphore("s_" + e)) for e in engs}
        dsem = {k: es.enter_context(nc.semaphore("d_%d" % n)) for n, k in enumerate(self.dma_cnt)}
        block = es.enter_context(nc.Block())

        def run(engname, engobj):
            known = {}
            for i, op in enumerate(ops):
                if op["eng"] != engname:
                    continue
                for e2, d in op["cdeps"].items():
                    v = ops[d]["ms"]
                    if known.get(("c", e2), 0) < v:
                        engobj.wait_ge(esem[e2], v)
                        known[("c", e2)] = v
                for k, v in op["ddeps"].items():
                    if known.get(("d", k), 0) < v:
                        engobj.wait_ge(dsem[k], v)
                        known[("d", k)] = v
                if op["fn"] is None:
                    ins = engobj.nop() if ("ms" in op) else None
                else:
                    ins = op["fn"](engobj)
                if op["dma"]:
                    ins.then_inc(dsem[op["key"]], 16)
                elif "ms" in op:
                    ins.then_inc(esem[engname], 1)
            if engname == "sp":
                for k, n in self.dma_cnt.items():
                    engobj.wait_ge(dsem[k], 16 * n)

        @block.tensor
        def _(e):
            run("pe", e)

        @block.scalar
        def _(e):
            run("act", e)

        @block.vector
        def _(e):
            run("dve", e)

        @block.gpsimd
        def _(e):
            run("pool", e)

        @block.sync
        def _(e):
            run("sp", e)


def _bucket_table():
    rel = (127 - np.arange(384)).astype(np.int32)
    half = 16
    max_exact = 8
    ret = np.where(rel > 0, half, 0)
    n = np.abs(rel)
    nf = np.maximum(n, 1).astype(np.float32)
    large = max_exact + (np.log(nf / np.float32(max_exact)) / np.float32(math.log(128 / max_exact))
                         * np.float32(half - max_exact)).astype(np.int32)
    large = np.minimum(large, half - 1)
    return np.asarray(ret + np.where(n < max_exact, n, large))


def _order(half, ng=16):
    o = []
    for m in range(ng // 2):
        own_first = ((m % 2) == 0) == (half == 0)
        if own_first:
            o += [2 * m + 1, 2 * m]
        else:
            o += [2 * m, 2 * m + 1]
    return o


class _Stop(Exception):
    pass


def build_nc(NG=16, PAST=2048):
    nc = bass.Bass("TRN2", target_bir_lowering=False)
    S = Sched()
    SEQ = NG * GT
    NSLOT = NG // 2
    NCB = PAST // 128
    SOFF = PAST + 256
    SBLK = NCB + 2
    SK0 = 2 * SOFF + 128
    VNEW = SK0 // 128
    SST = NSLOT * 4
    assert SK0 + 128 <= SEQ

    def din(name, shape):
        return nc.dram_tensor(name, list(shape), F32, kind="ExternalInput").ap()

    def dout(name, shape):
        return nc.dram_tensor(name, list(shape), F32, kind="ExternalOutput").ap()

    xp = din("xp", [SEQ, D])
    xh = din("xh", [NSLOT, 2, D])
    xs = din("xs", [128, D])
    ck = din("ck", [2, PAST, 512])
    cv = din("cv", [2, PAST, 512])
    cc = din("cc", [128, 4, 2, 2])
    w_in = din("w_in", [D, 3072])
    w_out = din("w_out", [D, D])
    w1 = din("w1", [D, 4096])
    w2 = din("w2", [4096, D])
    lnT_d = din("lnT", [128, 6, 8])
    lnB_d = din("lnB", [128, 6, D])
    convw_d = din("convw", [128, 4, 3])
    subg_d = din("subg", [128, 1])
    lamv_d = din("lamv", [128, 4, 64])
    rb_d = din("rb", [32, 4])
    oh_d = din("oh", [32, 384])
    flags_d = din("flags", [128, 16])

    yp = dout("yp", [NSLOT * GT, D])
    ys = dout("ys", [128, D])
    kp = dout("kp", [NSLOT * GT, 512])
    vp = dout("vp", [NSLOT * GT, 512])
    cp = dout("cp", [128, 4, 2])
    ksn = dout("ksn", [128, 512])
    vsn = dout("vsn", [128, 512])
    csn = dout("csn", [128, 4, 2, 2])

    wsc = nc.dram_tensor("wsc", [4, 384], F32, kind="Internal")
    dbg_cat = nc.dram_tensor("dbg_cat", [NSLOT, 128, 8, 512], BF16, kind="ExternalOutput").ap() if os.environ.get("KDBG") else None
    xsc = nc.dram_tensor("xsc", [NSLOT * GT + 128, D], F32, kind=("ExternalOutput" if os.environ.get("KDBG") else "Internal")).ap()

    es = ExitStack()
    with es:
        def sb(name, shape, dt):
            return es.enter_context(nc.sbuf_tensor("sb_" + name, list(shape), dt))

        big = sb("big", [128, 65536], BF16)
        KTW = max(SEQ, 8192)
        KT = big[:, 0:32768].rearrange("p (h t) -> p h t", h=4)
        Vt = big[:, 32768:65536].rearrange("p (b c) -> p b c", c=512)
        W1s = big[:, 0:32768].rearrange("p (k f) -> p k f", k=8)
        W2s = big[:, 32768:65536].rearrange("p (k c) -> p k c", c=1024)
        ident = sb("ident", [128, 128], BF16)
        Jm = sb("Jm", [128, 128], BF16)
        ones = sb("ones", [128, 128], BF16)
        sel = sb("sel", [64, 2, 128], F32)
        lnT = sb("lnT", [128, 6, 8], F32)
        convw = sb("convw", [128, 4, 3], F32)
        gsc = sb("gsc", [128, 1], F32)
        nlam = sb("nlam", [128, 1], F32)
        epsc = sb("epsc", [128, 1], F32)
        flags = sb("flags", [128, 16], F32)
        st0 = sb("st0", [128, SST + 1, 2], F32)
        pp = [es.enter_context(nc.psum_tensor("psum%d" % i, [128, 1024], F32)) for i in range(4)]
        ps = [pp[i // 2][:, (i % 2) * 512:(i % 2 + 1) * 512] for i in range(8)]
        PS = ["ps%d" % i for i in range(8)]

        def ps_bf(i):
            return ps[i][:, :].bitcast(BF16)

        pe = lambda fn, r=(), w=(): S.add("pe", fn, r, w)
        act = lambda fn, r=(), w=(): S.add("act", fn, r, w)
        dve = lambda fn, r=(), w=(): S.add("dve", fn, r, w)
        pool = lambda fn, r=(), w=(): S.add("pool", fn, r, w)

        def dma(q, out, in_, r, w, key, **kw):
            S.add(q, lambda e: e.dma_start(out=out, in_=in_, **kw), r, w, dma=True, key=key)

        esA = ExitStack()
        esA.__enter__()

        def sbA(name, shape, dt):
            return esA.enter_context(nc.sbuf_tensor("sb_" + name, list(shape), dt))

        GB0 = sbA("GB0", [128, 2, D], F32)
        H01 = sbA("H01", [128, 4, 256], BF16)
        HS = sbA("HS", [128, 2, 4, 128], BF16)
        Mf = sbA("Mf", [128, 2, 512], BF16)
        xst = [sbA("xst%d" % i, [128, D], F32) for i in range(2)]
        xnb = [sbA("xnb0", [128, D], BF16)] * 2
        x0T = sbA("x0T", [128, 8, 512], BF16)
        xhT = sbA("xhT", [128, 8, 2], BF16)
        QT = sbA("QT", [128, 4, 512], BF16)
        catT = sbA("catT", [128, 8, 512], BF16)
        wk = [sbA("wk%d" % i, [128, 520], F32) for i in range(4)]
        PT = [sbA("PT%d" % i, [128, 1024], BF16) for i in range(3)]
        sqb = sbA("sqb", [128, 512], BF16)
        NWS = 4
        wsl = [sbA("wsl%d" % i, [128, 8, 256], BF16) for i in range(NWS)]
        ost = [sbA("ost%d" % i, [128, 256], F32) for i in range(2)]
        stt = sbA("stt", [128, 20], F32)
        uh = sbA("uh", [128, 4, 2], F32)
        uhh = sbA("uhh", [128, 4, 2], F32)
        setup = xst[1]
        ws_state = [0]
        ost_state = [0]
        x_state = [0]

        pool(lambda e: e.memset(setup[:, 0:128], 0.0), w=["setup"])
        pool(lambda e: e.affine_select(out=setup[:, 0:128], in_=setup[:, 0:128], compare_op=ALU.not_equal,
                                       fill=1.0, base=0, pattern=[[-1, 128]], channel_multiplier=1),
             r=["setup"], w=["setup"])
        dve(lambda e: e.tensor_copy(out=ident[:, :], in_=setup[:, 0:128]), r=["setup"], w=["ident"])
        pool(lambda e: e.memset(setup[:, 128:256], 0.0), r=["setup"], w=["setup"])
        pool(lambda e: e.affine_select(out=setup[:, 128:256], in_=setup[:, 128:256], compare_op=ALU.not_equal,
                                       fill=1.0, base=-127, pattern=[[1, 128]], channel_multiplier=1),
             r=["setup"], w=["setup"])
        dve(lambda e: e.tensor_copy(out=Jm[:, :], in_=setup[:, 128:256]), r=["setup"], w=["Jm"])
        dve(lambda e: e.memset(ones[:, :], 1.0), w=["ones"])
        dve(lambda e: e.memset(sel[:, :, :], 0.0), w=["sel"])
        dve(lambda e: e.memset(sel[0:32, 0, :], 1.0 / 32.0), r=["sel"], w=["sel"])
        dve(lambda e: e.memset(sel[32:64, 1, :], 1.0 / 32.0), r=["sel"], w=["sel"])
        dve(lambda e: e.memset(epsc[:, :], EPS), w=["epsc"])
        dve(lambda e: e.memset(xst[0][:, :], 0.0), w=["xst0"])
        dma("sp", lnT[:, :, :], lnT_d, [], ["lnT"], "c_lnT")
        dma("sp", convw[:, :, :], convw_d, [], ["convw"], "c_convw")
        dma("sp", flags[:, :], flags_d, [], ["flags"], "c_flags")
        dma("sp", GB0[:, :, :], lnB_d[:, 0:2, :], [], ["GB0"], "c_GB0")
        dma("sp", setup[:, 256:512], lamv_d.rearrange("p a b -> p (a b)"), ["setup"], ["lamv"], "c_lamv")
        dma("sp", setup[:, 512:513], subg_d, ["setup"], ["subg"], "c_subg")
        dve(lambda e: e.tensor_mul(out=setup[:, 520:584], in0=setup[:, 256:320], in1=setup[:, 320:384]), r=["lamv"], w=["lp1"])
        dve(lambda e: e.tensor_mul(out=setup[:, 584:648], in0=setup[:, 384:448], in1=setup[:, 448:512]), r=["lamv"], w=["lp2"])
        dve(lambda e: e.reduce_sum(out=setup[:, 650:651], in_=setup[:, 520:584], axis=AX.X), r=["lp1"], w=["ls1"])
        dve(lambda e: e.reduce_sum(out=setup[:, 651:652], in_=setup[:, 584:648], axis=AX.X), r=["lp2"], w=["ls2"])
        act(lambda e: e.activation(out=setup[:, 652:654], in_=setup[:, 650:652], func=AF.Exp), r=["ls1", "ls2"], w=["le"])
        dve(lambda e: e.tensor_sub(out=setup[:, 654:655], in0=setup[:, 653:654], in1=setup[:, 652:653]), r=["le"], w=["ld"])
        dve(lambda e: e.tensor_scalar_add(out=nlam[:, :], in0=setup[:, 654:655], scalar1=-LAM_INIT), r=["ld"], w=["nlam"])
        dve(lambda e: e.tensor_scalar_mul(out=gsc[:, :], in0=setup[:, 512:513], scalar1=1.0 - LAM_INIT), r=["subg"], w=["gsc"])
        dma("sp", setup[0:32, 660:664], rb_d, ["setup"], ["rb"], "c_rb")
        dma("sp", wk[0][0:32, 0:384], oh_d, [], ["wk0"], "wk0")
        pe(lambda e: e.matmul(ps[0][0:4, 0:384], lhsT=setup[0:32, 660:664], rhs=wk[0][0:32, 0:384], start=True, stop=True),
           r=["rb", "wk0"], w=[PS[0]])
        dve(lambda e: e.tensor_scalar_mul(out=wk[1][0:4, 0:384], in0=ps[0][0:4, 0:384], scalar1=8.0), r=[PS[0]], w=["wk1"])
        dma("sp", wsc.ap(), wk[1][0:4, 0:384], ["wk1"], ["wsc"], "wk1")
        for h in range(4):
            src = wk[2 + (h % 2)]
            sn = "wk%d" % (2 + (h % 2))
            dma("sp", src[:, 0:256], bass.AP(wsc, h * 384, [[1, 128], [1, 256]]), ["wsc"], [sn], sn)
            for s in range(2):
                dve(lambda e, src=src, h=h, s=s: e.tensor_scalar_mul(out=HS[:, s, h, :], in0=src[:, 128:256], scalar1=flags[:, 8 + s:9 + s]),
                    r=[sn, "flags"], w=["HS"])
            dve(lambda e, src=src: e.memset(src[0:64, 0:64], NEG), r=[], w=[sn])
            dve(lambda e, src=src, h=h: e.tensor_copy(out=H01[:, h, :], in_=src[:, 0:256]), r=[sn], w=["H01"])
        dve(lambda e: e.memset(setup[:, 0:512], NEG), r=["setup", "ident", "Jm"], w=["setup"])
        for s in range(2):
            dve(lambda e, s=s: e.tensor_scalar_mul(out=Mf[:, s, :], in0=setup[:, 0:512], scalar1=flags[:, 10 + s:11 + s]),
                r=["setup", "flags"], w=["Mf"])

        if os.environ.get("KDBG"):
            dbg_h = nc.dram_tensor("dbg_h", [128, 4, 256], BF16, kind="ExternalOutput").ap()
            dbg_hs = nc.dram_tensor("dbg_hs", [128, 2, 4, 128], BF16, kind="ExternalOutput").ap()
            dbg_mf = nc.dram_tensor("dbg_mf", [128, 2, 512], BF16, kind="ExternalOutput").ap()
            dma("sp", dbg_h, H01[:, :, :], ["H01"], [], "dbgh")
            dma("sp", dbg_hs, HS[:, :, :, :], ["HS"], [], "dbgh")
            dma("sp", dbg_mf, Mf[:, :, :], ["Mf"], [], "dbgh")
            dbg_j = nc.dram_tensor("dbg_j", [128, 128], BF16, kind="ExternalOutput").ap()
            dma("sp", dbg_j, Jm[:, :], ["Jm"], [], "dbgh")
        S.mark(0)
        S.barrier()

        def wload(W, col0, nk=8, ncols=256):
            i = ws_state[0] % NWS
            ws_state[0] += 1
            name = "wsl%d" % i
            dma("pool", wsl[i][:, 0:nk, 0:ncols], W[:, col0:col0 + ncols].rearrange("(k p) c -> p k c", p=128),
                [], [name], name)
            return wsl[i], name

        def rstd_from_var(var_ap, out_ap, rbuf, wbuf):
            act(lambda e: e.activation(out=out_ap, in_=var_ap, func=AF.Ln, bias=epsc[:, :], scale=1.0), r=rbuf, w=wbuf)
            act(lambda e: e.activation(out=out_ap, in_=out_ap, func=AF.Exp, scale=-0.5), r=wbuf, w=wbuf)

        def exhaust(g):
            for _ in g:
                pass

        def ln_tile(*a, **k):
            exhaust(ln_tile_g(*a, **k))

        def ln_tile_g(src_dram, stat_ap, statbuf, gi, dstT, dstname, col0, ncolsT=128, psb=7):
            i = x_state[0] % 2
            x_state[0] += 1
            xn_ = "xst%d" % i
            xb_ = "xnb0"
            dma("sp", xst[i][:, :], src_dram, [], [xn_], xn_)
            dve(lambda e: e.bn_stats(stt[:, 0:6], xst[i][:, 0:512]), r=[xn_], w=["stt"])
            dve(lambda e: e.bn_stats(stt[:, 6:12], xst[i][:, 512:1024]), r=[xn_], w=["stt"])
            dve(lambda e: e.bn_aggr(stat_ap, stt[:, 0:12]), r=["stt"], w=[statbuf])
            rstd_from_var(stat_ap[:, 1:2], stat_ap[:, 1:2], [statbuf], [statbuf])
            dve(lambda e: e.tensor_scalar(out=xnb[i][:, :], in0=xst[i][:, :], scalar1=stat_ap[:, 0:1], scalar2=stat_ap[:, 1:2],
                                          op0=ALU.subtract, op1=ALU.mult), r=[xn_, statbuf], w=[xb_])
            yield
            yield
            yield
            transpose_affine(xnb[i], xb_, gi, dstT, dstname, col0, ncolsT, psb)
            yield

        def transpose_affine(xb, xb_, gi, dstT, dstname, col0, ncolsT=128, psb=7):
            pb = ps_bf(psb)

            def tr(e):
                ins = None
                for c in range(8):
                    ins = e.transpose(pb[:, c * 128:(c + 1) * 128], xb[:, c * 128:(c + 1) * 128], ident[:, :])
                return ins
            pe(tr, r=[xb_, "ident"], w=[PS[psb]])

            def ev(e):
                ins = None
                for c in range(8):
                    ins = e.activation(out=dstT[:, c, col0:col0 + ncolsT], in_=pb[:, c * 128:c * 128 + ncolsT], func=AF.Identity,
                                       scale=lnT[:, gi, c:c + 1], bias=lnT[:, gi + 1, c:c + 1])
                return ins
            act(ev, r=[PS[psb], "lnT"], w=[dstname])

        def kv_group(*a, **k):
            exhaust(kv_group_g(*a, **k))

        def kv_group_g(ntile, tok0, blk0, kname, vname, own_rows=None, kout=None, vout=None, onebank=None, pre=None):
            n = ntile * 128
            for hp in range(2):
                wsb, wn = wload(w_in, 512 + hp * 256) if pre is None else pre[hp]
                for hh in range(2):
                    h = 2 * hp + hh
                    bank = hh if onebank is None else onebank

                    def mm(e, wsb=wsb, hh=hh, bank=bank):
                        ins = None
                        for k in range(8):
                            ins = e.matmul(ps[bank][:, 0:n], lhsT=wsb[:, k, hh * 128:(hh + 1) * 128], rhs=x0T[:, k, 0:n],
                                           start=(k == 0), stop=(k == 7))
                        return ins
                    pe(mm, r=[wn, "x0T"], w=[PS[bank]])
                    dve(lambda e, h=h, bank=bank: e.tensor_copy(out=KT[:, h, tok0:tok0 + n], in_=ps[bank][:, 0:n]),
                        r=[PS[bank]], w=[kname])
                    yield
                if kout is not None:
                    for t in range(ntile):
                        bank = 2 + (t % 2) if onebank is None else onebank

                        def mm(e, wsb=wsb, t=t, bank=bank):
                            ins = None
                            for k in range(8):
                                ins = e.matmul(ps[bank][:, 0:256], lhsT=x0T[:, k, t * 128:(t + 1) * 128], rhs=wsb[:, k, :],
                                               start=(k == 0), stop=(k == 7))
                            return ins
                        pe(mm, r=[wn, "x0T"], w=[PS[bank]])
                        oi = ost_state[0] % 2
                        ost_state[0] += 1
                        dve(lambda e, oi=oi, bank=bank: e.tensor_copy(out=ost[oi][:, :], in_=ps[bank][:, 0:256]),
                            r=[PS[bank]], w=["ost%d" % oi])
                        dma("sp", kout[own_rows + t * 128:own_rows + (t + 1) * 128, hp * 256:(hp + 1) * 256], ost[oi][:, :],
                            ["ost%d" % oi], [], "ost%d" % oi)
                        yield
            for hp in range(2):
                wsb, wn = wload(w_in, 1024 + hp * 256) if pre is None else pre[2 + hp]
                for t in range(ntile):
                    bank = 4 + (t % 2) if onebank is None else onebank

                    def mm(e, wsb=wsb, t=t, bank=bank):
                        ins = None
                        for k in range(8):
                            ins = e.matmul(ps[bank][:, 0:256], lhsT=x0T[:, k, t * 128:(t + 1) * 128], rhs=wsb[:, k, :],
                                           start=(k == 0), stop=(k == 7))
                        return ins
                    pe(mm, r=[wn, "x0T"], w=[PS[bank]])
                    if vout is None:
                        act(lambda e, t=t, bank=bank, hp=hp: e.copy(out=Vt[:, blk0 + t, hp * 256:(hp + 1) * 256], in_=ps[bank][:, 0:256]),
                            r=[PS[bank]], w=[vname])
                        yield
                    else:
                        oi = ost_state[0] % 2
                        ost_state[0] += 1
                        dve(lambda e, oi=oi, bank=bank: e.tensor_copy(out=ost[oi][:, :], in_=ps[bank][:, 0:256]),
                            r=[PS[bank]], w=["ost%d" % oi])
                        act(lambda e, t=t, oi=oi, hp=hp: e.copy(out=Vt[:, blk0 + t, hp * 256:(hp + 1) * 256], in_=ost[oi][:, :]),
                            r=["ost%d" % oi], w=[vname])
                        dma("sp", vout[own_rows + t * 128:own_rows + (t + 1) * 128, hp * 256:(hp + 1) * 256], ost[oi][:, :],
                            ["ost%d" % oi], [], "ost%d" % oi)
                        yield

        def q_conv(*a, **k):
            exhaust(q_conv_g(*a, **k))

        def q_conv_g(ntile, nseg, hist_fn, conv_out_fn, prehist_fn=None):
            n = ntile * 128
            L = n // nseg
            for hp in range(2):
                wsb, wn = wload(w_in, hp * 256)
                for hh in range(2):
                    h = 2 * hp + hh
                    bank = hh

                    def mm(e, wsb=wsb, hh=hh, bank=bank):
                        ins = None
                        for k in range(8):
                            ins = e.matmul(ps[bank][:, 0:n], lhsT=wsb[:, k, hh * 128:(hh + 1) * 128], rhs=x0T[:, k, 0:n],
                                           start=(k == 0), stop=(k == 7))
                        return ins
                    pe(mm, r=[wn, "x0T"], w=[PS[bank]])
                    act(lambda e, h=h, bank=bank: e.copy(out=QT[:, h, 0:n], in_=ps[bank][:, 0:n]), r=[PS[bank]], w=["QT"])
                    yield
            for cpair in range(2):
                wh, whn = wload(w_in, 2560 + cpair * 256)
                wc, wcn = wload(w_in, 2048 + cpair * 256)
                wb, wbn = wload(w_in, 1536 + cpair * 256)
                for ci in range(2):
                    cch = 2 * cpair + ci

                    def proj(bank, wsb, wn, rhs_ap, nn, rname, ci=ci):
                        def mm(e):
                            ins = None
                            for k in range(8):
                                ins = e.matmul(ps[bank][:, 0:nn], lhsT=wsb[:, k, ci * 128:(ci + 1) * 128], rhs=rhs_ap(k),
                                               start=(k == 0), stop=(k == 7))
                            return ins
                        pe(mm, r=[wn, rname], w=[PS[bank]])
                    u3 = wk[1][:, 0:nseg * (L + 2)].rearrange("p (s l) -> p s l", s=nseg)
                    a3 = wk[2][:, 0:n].rearrange("p (s l) -> p s l", s=nseg)
                    hb = 2 if cch % 2 == 0 else 0
                    cb = hb + 1
                    proj(hb, wh, whn, lambda k: x0T[:, k, 0:n], n, "x0T")
                    act(lambda e, hb=hb: e.copy(out=wk[0][:, 0:n], in_=ps[hb][:, 0:n]), r=[PS[hb]], w=["wk0"])
                    proj(cb, wc, wcn, lambda k: x0T[:, k, 0:n], n, "x0T")
                    dve(lambda e, u3=u3, cb=cb: e.tensor_mul(out=u3[:, :, 2:L + 2], in0=ps[cb][:, 0:n].rearrange("p (s l) -> p s l", s=nseg),
                                                             in1=wk[0][:, 0:n].rearrange("p (s l) -> p s l", s=nseg)),
                        r=[PS[cb], "wk0"], w=["wk1"])
                    if prehist_fn is not None and ci == 0:
                        prehist_fn(cpair, wh, whn, wc, wcn)
                    hist_fn(cch, ci, u3, wh, whn, wc, wcn, proj, hb, cb)
                    dve(lambda e, u3=u3, a3=a3, cch=cch: e.tensor_scalar_mul(out=a3, in0=u3[:, :, 0:L], scalar1=convw[:, cch, 0:1]),
                        r=["wk1", "convw"], w=["wk2"])
                    dve(lambda e, u3=u3, a3=a3, cch=cch: e.scalar_tensor_tensor(out=a3, in0=u3[:, :, 1:L + 1], scalar=convw[:, cch, 1:2], in1=a3,
                                                                                  op0=ALU.mult, op1=ALU.add), r=["wk1", "wk2"], w=["wk2"])
                    dve(lambda e, u3=u3, a3=a3, cch=cch: e.scalar_tensor_tensor(out=a3, in0=u3[:, :, 2:L + 2], scalar=convw[:, cch, 2:3], in1=a3,
                                                                                  op0=ALU.mult, op1=ALU.add), r=["wk1", "wk2"], w=["wk2"])
                    proj(hb, wb, wbn, lambda k: x0T[:, k, 0:n], n, "x0T")
                    dve(lambda e, cch=cch, hb=hb: e.tensor_mul(out=catT[:, 4 + cch, 0:n], in0=ps[hb][:, 0:n], in1=wk[2][:, 0:n]),
                        r=[PS[hb], "wk2"], w=["catT"])
                    conv_out_fn(cch, u3, L)
                    yield

        def attn_finish1(n):
            dve(lambda e: e.tensor_copy(out=wk[0][:, 0:n], in_=ps[4][:, 0:n]), r=[PS[4]], w=["wk0"])
            dve(lambda e: e.tensor_copy(out=wk[1][:, 0:n], in_=ps[5][:, 0:n]), r=[PS[5]], w=["wk1"])
            dve(lambda e: e.tensor_copy(out=wk[2][0:64, 0:n], in_=ps[6][0:64, 0:n]), r=[PS[6]], w=["wk2"])

        def attn_finish2(h, n, cc0):
            for m in range(2):
                pe(lambda e, m=m: e.matmul(ps[7][:, 0:n], lhsT=sel[:, m, :], rhs=wk[2][0:64, 0:n], start=True, stop=True),
                   r=["wk2", "sel"], w=[PS[7]])
                act(lambda e: e.activation(out=wk[3][:, 0:n], in_=ps[7][:, 0:n], func=AF.Ln), r=[PS[7]], w=["wk3"])
                act(lambda e: e.activation(out=wk[3][:, 0:n], in_=wk[3][:, 0:n], func=AF.Exp, scale=-1.0), r=["wk3"], w=["wk3"])
                yield
                dve(lambda e, m=m: e.tensor_mul(out=wk[m][:, 0:n], in0=wk[m][:, 0:n], in1=wk[3][:, 0:n]), r=["wk%d" % m, "wk3"], w=["wk%d" % m])
                yield
            dve(lambda e: e.scalar_tensor_tensor(out=wk[0][:, 0:n], in0=wk[1][:, 0:n], scalar=nlam[:, 0:1], in1=wk[0][:, 0:n],
                                                 op0=ALU.mult, op1=ALU.add), r=["wk0", "wk1", "nlam"], w=["wk0"])
            dve(lambda e: e.tensor_mul(out=sqb[:, 0:n], in0=wk[0][:, 0:n], in1=wk[0][:, 0:n]), r=["wk0"], w=["sqb"])
            yield
            pe(lambda e: e.matmul(ps[7][:, 0:n], lhsT=ones[:, :], rhs=sqb[:, 0:n], start=True, stop=True), r=["sqb", "ones"], w=[PS[7]])
            act(lambda e: e.activation(out=wk[3][:, 0:n], in_=ps[7][:, 0:n], func=AF.Ln, bias=epsc[:, :], scale=1.0 / 128.0),
                r=[PS[7]], w=["wk3"])
            act(lambda e: e.activation(out=wk[3][:, 0:n], in_=wk[3][:, 0:n], func=AF.Exp, scale=-0.5), r=["wk3"], w=["wk3"])
            yield
            dve(lambda e: e.scalar_tensor_tensor(out=catT[:, h, cc0:cc0 + n], in0=wk[0][:, 0:n], scalar=gsc[:, 0:1], in1=wk[3][:, 0:n],
                                                 op0=ALU.mult, op1=ALU.mult), r=["wk0", "wk3", "gsc"], w=["catT"])

        def attention(blocks_for_head, ncol, qcol0=0, background=None):
            pt_i = [0]
            pending_fin = None
            for h in range(4):
                blks = blocks_for_head(h)
                nb = len(blks)
                pend = None
                first = [True, True]

                def issue_pv(b, pt, ptn):
                    p0, p1 = b["prange"]
                    c0, c1 = b["c0"], b["c1"]
                    st = first[0]
                    first[0] = False
                    last = b["last"]
                    for m, ob in enumerate((4, 5)):
                        pe(lambda e, b=b, ob=ob, m=m: e.matmul(ps[ob][:, c0:c1], lhsT=b["v"], rhs=pt[p0:p1, m * 512 + c0:m * 512 + c1],
                                                                 start=st, stop=last),
                           r=["%s_%d" % (ptn, m), b["vname"]], w=[PS[ob]])
                    for m in range(2):
                        pe(lambda e, m=m: e.matmul(ps[6][32 * m:32 * m + 32, c0:c1], lhsT=ones[p0:p1, 0:32],
                                                   rhs=pt[p0:p1, m * 512 + c0:m * 512 + c1], start=st, stop=last),
                           r=["%s_%d" % (ptn, m), "ones"], w=[PS[6]])

                for bi, b in enumerate(blks):
                    b["last"] = (bi == nb - 1)
                    sp_ = (bi % 2)
                    c0, c1 = b["c0"], b["c1"]
                    p0, p1 = b["prange"]
                    nbias = len(b["bias"])
                    for m in range(2):
                        kap = b["kt"][m]
                        qap = QT[m * 64:(m + 1) * 64, h, qcol0 + c0:qcol0 + c1]
                        bank = 2 * sp_ + m
                        pe(lambda e, kap=kap, qap=qap, bank=bank, nbias=nbias, p0=p0, p1=p1, c0=c0, c1=c1:
                           e.matmul(ps[bank][p0:p1, c0:c1], lhsT=kap, rhs=qap, start=True, stop=(nbias == 0)),
                           r=[b["kname"], "QT"], w=[PS[bank]])
                        for bj, (bl, br, oc0, oc1) in enumerate(b["bias"]):
                            pe(lambda e, bl=bl, br=br, oc0=oc0, oc1=oc1, bank=bank, lastb=(bj == nbias - 1), p0=p0, p1=p1:
                               e.matmul(ps[bank][p0:p1, oc0:oc1], lhsT=bl, rhs=br, start=False, stop=lastb),
                               r=["Jm", "H01", "HS", "Mf"], w=[PS[bank]])
                    i = pt_i[0] % 3
                    pt_i[0] += 1
                    for m in range(2):
                        act(lambda e, m=m, i=i, bank=2 * sp_ + m, p0=p0, p1=p1, c0=c0, c1=c1: e.activation(out=PT[i][p0:p1, m * 512 + c0:m * 512 + c1], in_=ps[bank][p0:p1, c0:c1],
                                                                                func=AF.Exp, scale=0.125),
                            r=[PS[2 * sp_ + m]], w=["PT%d_%d" % (i, m)])
                    if pend is not None:
                        issue_pv(*pend)
                    pend = (b, PT[i], "PT%d" % i)
                    if bi >= 2 and pending_fin is not None:
                        next(pending_fin, None)
                    if background is not None:
                        background()
                issue_pv(*pend)
                if pending_fin is not None:
                    exhaust(pending_fin)
                attn_finish1(ncol)
                pending_fin = attn_finish2(h, ncol, qcol0)
            exhaust(pending_fin)

        def out_proj_mm(ntile):
            for dq in range(4):
                wsb, wn = wload(w_out, dq * 256)
                for t in range(ntile):
                    bank = 2 * t + dq // 2

                    def mm(e, wsb=wsb, t=t, bank=bank, dq=dq):
                        ins = None
                        for k in range(8):
                            ins = e.matmul(ps[bank][:, (dq % 2) * 256:(dq % 2) * 256 + 256], lhsT=catT[:, k, t * 128:(t + 1) * 128],
                                           rhs=wsb[:, k, :], start=(k == 0), stop=(k == 7))
                        return ins
                    pe(mm, r=[wn, "catT"], w=[PS[bank]])

        def ln1_load(t, src_rows):
            i = x_state[0] % 2
            x_state[0] += 1
            dma("sp", xst[i][:, :], src_rows(t), [], ["xst%d" % i], "xst%d" % i)
            return i

        def ln1_chain(t, src_rows, stat_idx0, xsc_row0, i=None):
            if i is None:
                i = ln1_load(t, src_rows)
            xn_ = "xst%d" % i
            x_ = xst[i]
            sa = st0[:, stat_idx0 + t, :]
            dve(lambda e, sa=sa: e.tensor_scalar(out=stt[:, 16:17], in0=sa[:, 0:1], scalar1=sa[:, 1:2], scalar2=-1.0,
                                                 op0=ALU.mult, op1=ALU.mult), r=["st0"], w=["stt4"])
            act(lambda e, x_=x_, sa=sa: e.activation(out=x_[:, :], in_=x_[:, :], func=AF.Identity, scale=sa[:, 1:2], bias=stt[:, 16:17]),
                r=[xn_, "st0", "stt4"], w=[xn_])
            dve(lambda e, x_=x_: e.tensor_mul(out=x_[:, :], in0=x_[:, :], in1=GB0[:, 0, :]), r=[xn_, "GB0"], w=[xn_])
            dve(lambda e, x_=x_: e.tensor_add(out=x_[:, :], in0=x_[:, :], in1=GB0[:, 1, :]), r=[xn_, "GB0"], w=[xn_])
            for hf in range(2):
                dve(lambda e, x_=x_, hf=hf, t=t: e.scalar_tensor_tensor(out=x_[:, hf * 512:(hf + 1) * 512], in0=x_[:, hf * 512:(hf + 1) * 512],
                                                                        scalar=ALPHA, in1=ps[2 * t + hf][:, :], op0=ALU.mult, op1=ALU.add),
                    r=[xn_, PS[2 * t + hf]], w=[xn_])
            dve(lambda e, x_=x_: e.bn_stats(stt[:, 0:6], x_[:, 0:512]), r=[xn_], w=["stt"])
            dve(lambda e, x_=x_: e.bn_stats(stt[:, 6:12], x_[:, 512:1024]), r=[xn_], w=["stt"])
            dve(lambda e: e.bn_aggr(stt[:, 12:14], stt[:, 0:12]), r=["stt"], w=["stt2"])
            rstd_from_var(stt[:, 13:14], stt[:, 13:14], ["stt2"], ["stt2"])
            dve(lambda e: e.tensor_scalar(out=stt[:, 17:18], in0=stt[:, 12:13], scalar1=stt[:, 13:14], scalar2=-1.0,
                                          op0=ALU.mult, op1=ALU.mult), r=["stt2"], w=["stt5"])
            act(lambda e, x_=x_: e.activation(out=x_[:, :], in_=x_[:, :], func=AF.Identity, scale=stt[:, 13:14], bias=stt[:, 17:18]),
                r=[xn_, "stt2", "stt5"], w=[xn_])
            dma("sp", xsc[xsc_row0 + t * 128:xsc_row0 + (t + 1) * 128, :], x_[:, :], [xn_], ["xsc"], xn_)

        def out_proj_ln1(ntile, src_rows, stat_idx0, xsc_row0):
            out_proj_mm(ntile)
            for t in range(ntile):
                ln1_chain(t, src_rows, stat_idx0, xsc_row0)

        KTALL = ["KT%d" % g for g in range(NG)]
        VALL = ["V%d" % g for g in range(NG)]
        CG = min(4, NCB)
        for s in range(2):
            VG = min(8, NCB)
            for v8 in range(NCB // VG):
                dma("pool", Vt[:, s * SBLK + v8 * VG:s * SBLK + (v8 + 1) * VG, :],
                    cv[s, v8 * VG * 128:(v8 + 1) * VG * 128, :].rearrange("(b p) c -> p b c", p=128), [], ["Vc%d_%d" % (s, v8)], "vc%d_%d" % (s, v8))
            for b4 in range(NCB // CG):
                i = ws_state[0] % NWS
                ws_state[0] += 1
                ck4 = wsl[i][:, :, :].rearrange("p a b -> p (a b)").rearrange("p (g c) -> p g c", c=512)
                dma("pool", ck4[:, 0:CG, :], ck[s, b4 * CG * 128:(b4 + 1) * CG * 128, :].rearrange("(b p) c -> p b c", p=128),
                    [], ["wsl%d" % i], "wsl%d" % i)
                for bb in range(CG):
                    blk = b4 * CG + bb
                    pb = ps_bf(blk % 2)

                    def tr(e, ck4=ck4, pb=pb, bb=bb):
                        ins = None
                        for h in range(4):
                            ins = e.transpose(pb[:, h * 128:(h + 1) * 128], ck4[:, bb, h * 128:(h + 1) * 128], ident[:, :])
                        return ins
                    pe(tr, r=["wsl%d" % i, "ident"], w=[PS[blk % 2]])
                    t0 = s * SOFF + blk * 128
                    dve(lambda e, pb=pb, t0=t0: e.tensor_copy(out=KT[:, :, t0:t0 + 128], in_=pb[:, 0:512].rearrange("p (h t) -> p h t", h=4)),
                        r=[PS[blk % 2]], w=["KTc%d_%d" % (s, blk)])
        S.add("pe", None, r=["Vc%d_%d" % (s, v8) for s in range(2) for v8 in range(NCB // min(8, NCB))] +
              ["KTc%d_%d" % (s, blk) for s in range(2) for blk in range(NCB)], w=["V0", "KT0"])
        S.mark(1)
        ln_tile(xs[:, :], st0[:, SST, :], "st0", 0, x0T, "x0T", 0)
        S.mark(2)
        kv_group(1, SK0, VNEW, "KT0", "V0", own_rows=0, kout=ksn, vout=vsn)

        def hist_sample(cch, ci, u3, wh, whn, wc, wcn, proj, hb, cb):
            dma("sp", uh[:, 0:2, :], cc[:, cch, :, :], [], ["uh"], "uh")
            dve(lambda e, u3=u3: e.tensor_copy(out=u3[:, :, 0:2], in_=uh[:, 0:2, :]), r=["uh"], w=["wk1"])

        def cout_sample(cch, u3, L):
            dve(lambda e, u3=u3: e.tensor_copy(out=uh[:, 2:4, :], in_=u3[:, :, L:L + 2]), r=["wk1"], w=["uh2"])
            dma("sp", csn[:, cch, :, :], uh[:, 2:4, :], ["uh2"], [], "uh2")

        S.mark(3)
        q_conv(1, 2, hist_sample, cout_sample)
        S.mark(4)

        for s in range(2):
            def blocks_sample(h, s=s):
                out = []
                for blk in range(NCB):
                    t0 = s * SOFF + blk * 128
                    bias = []
                    if blk == NCB - 1:
                        bias = [(Jm[:, :], H01[:, h, 128:192], 0, 64)]
                    out.append(dict(kt=(KT[0:64, h, t0:t0 + 128], KT[64:128, h, t0:t0 + 128]), kname="KT0",
                                    v=Vt[:, s * SBLK + blk, h * 128:(h + 1) * 128], vname="V0", c0=0, c1=64, prange=(0, 128), bias=bias))
                t0 = SK0 + s * 64
                p0 = s * 64
                out.append(dict(kt=(KT[0:64, h, t0:t0 + 64], KT[64:128, h, t0:t0 + 64]), kname="KT0",
                                v=Vt[p0:p0 + 64, VNEW, h * 128:(h + 1) * 128], vname="V0", c0=0, c1=64, prange=(p0, p0 + 64),
                                bias=[(Jm[:, 0:64], H01[:, h, 0:64], 0, 64)]))
                return out
            attention(blocks_sample, 64, qcol0=s * 64)
        S.mark(5)
        out_proj_ln1(1, lambda t: xs[:, :], SST, NSLOT * GT)
        S.mark(6)
        S.barrier()

        def prep_steps(pos, onebank):
            own = (pos % 2 == 1)
            j = pos // 2
            pre = [wload(w_in, 512), wload(w_in, 768), wload(w_in, 1024), wload(w_in, 1280)]
            for t in range(4):
                sa = st0[:, j * 4 + t, :] if own else stt[:, 14:16]
                yield from ln_tile_g(xp[pos * GT + t * 128:pos * GT + (t + 1) * 128, :], sa, "st0" if own else "stt3", 0, x0T, "x0T", t * 128)
            yield from kv_group_g(4, pos * GT, pos * 4, "KT%d" % pos, "V%d" % pos,
                                  own_rows=j * GT if own else None, kout=kp if own else None, vout=vp if own else None, onebank=onebank, pre=pre)
            if not own:
                return
            i = x_state[0] % 2
            x_state[0] += 1
            xn_, xb_ = "xst%d" % i, "xnb0"
            dma("sp", xst[i][0:2, :], xh[j, :, :], [], [xn_], xn_)
            dve(lambda e, i=i: e.bn_stats(stt[:, 0:6], xst[i][:, 0:512]), r=[xn_], w=["stt"])
            dve(lambda e, i=i: e.bn_stats(stt[:, 6:12], xst[i][:, 512:1024]), r=[xn_], w=["stt"])
            dve(lambda e: e.bn_aggr(stt[:, 14:16], stt[:, 0:12]), r=["stt"], w=["stt3"])
            rstd_from_var(stt[:, 15:16], stt[:, 15:16], ["stt3"], ["stt3"])
            dve(lambda e, i=i: e.tensor_scalar(out=xnb[i][:, :], in0=xst[i][:, :], scalar1=stt[:, 14:15], scalar2=stt[:, 15:16],
                                               op0=ALU.subtract, op1=ALU.mult), r=[xn_, "stt3"], w=[xb_])
            yield
            yield
            yield
            transpose_affine(xnb[i], xb_, 0, xhT, "xhT", 0, 2)
            yield

        def chain2(a, b):
            yield from a
            yield from b

        def make_bg(gen, ncalls, nsteps):
            state = dict(calls=ncalls, steps=nsteps)

            def bg():
                if state["steps"] <= 0:
                    return
                k = -(-state["steps"] // max(state["calls"], 1))
                state["calls"] -= 1
                for _ in range(k):
                    if next(gen, "END") == "END":
                        state["steps"] = 0
                        return
                    state["steps"] -= 1
            return bg

        exhaust(prep_steps(0, None))
        exhaust(prep_steps(1, None))
        def make_hist(j):
            def prehist(cpair, wh, whn, wc, wcn, j=j):
                for ci in range(2):
                    for ty, (wsb, wn) in enumerate(((wh, whn), (wc, wcn))):
                        col = (ci * 2 + ty) * 2

                        def mm(e, wsb=wsb, ci=ci, col=col):
                            ins = None
                            for k in range(8):
                                ins = e.matmul(ps[1][:, col:col + 2], lhsT=wsb[:, k, ci * 128:(ci + 1) * 128], rhs=xhT[:, k, 0:2],
                                               start=(k == 0), stop=(k == 7))
                            return ins
                        pe(mm, r=[wn, "xhT"], w=[PS[1]])
                v = ps[1][:, 0:8].rearrange("p (c t r) -> p c t r", c=2, t=2)
                dst = uhh[:, 2 * cpair:2 * cpair + 2, :]
                dve(lambda e: e.tensor_copy(out=dst, in_=v[:, :, 0, :]), r=[PS[1]], w=["uhh"])
                dve(lambda e: e.tensor_mul(out=dst, in0=v[:, :, 1, :], in1=dst), r=[PS[1], "uhh"], w=["uhh"])
                dve(lambda e: e.tensor_scalar_mul(out=dst, in0=dst, scalar1=flags[:, j:j + 1]), r=["uhh", "flags"], w=["uhh"])

            def hist_prompt(cch, ci, u3, wh, whn, wc, wcn, proj, hb, cb, j=j):
                dve(lambda e, u3=u3: e.tensor_copy(out=u3[:, 0, 0:2], in_=uhh[:, cch, :]), r=["uhh"], w=["wk1"])

            def cout_prompt(cch, u3, L, j=j):
                if j == NSLOT - 1:
                    dve(lambda e, u3=u3: e.tensor_copy(out=uh[:, 2, :], in_=u3[:, 0, L:L + 2]), r=["wk1"], w=["uh2"])
                    dma("sp", cp[:, cch, :], uh[:, 2, :], ["uh2"], [], "uh2")
            return hist_prompt, cout_prompt, prehist

        def steps(g, k):
            for _ in range(k):
                if next(g, "END") == "END":
                    return

        q_conv(4, 1, *make_hist(0))
        for pos in range(1, NG, 2):
            j = pos // 2
            sidx = j % 2

            def blocks_prompt(h, pos=pos, j=j, sidx=sidx):
                out = []
                for kpos in range(pos + 1):
                    for b in range(4):
                        t0 = kpos * GT + b * 128
                        c0 = 0
                        bias = []
                        if kpos == pos:
                            c0 = b * 128
                            if b < 3:
                                bias = [(Jm[:, :], H01[:, h, :], c0, c0 + 256)]
                            else:
                                bias = [(Jm[:, :], H01[:, h, 0:128], c0, c0 + 128)]
                        elif kpos == pos - 1:
                            bias = [(ident[:, :], Mf[:, sidx, :], 0, 512)]
                            if b == 3:
                                bias.append((Jm[:, :], HS[:, 1 - sidx, h, :], 0, 128))
                        elif kpos == pos - 2 and b == 3:
                            bias = [(Jm[:, :], HS[:, sidx, h, :], 0, 128)]
                        out.append(dict(kt=(KT[0:64, h, t0:t0 + 128], KT[64:128, h, t0:t0 + 128]), kname="KT%d" % kpos,
                                        v=Vt[:, kpos * 4 + b, h * 128:(h + 1) * 128], vname="V%d" % kpos, c0=c0, c1=512,
                                        prange=(0, 128), bias=bias))
                return out
            if j < NSLOT - 1:
                bgen = chain2(prep_steps(pos + 1, 7), prep_steps(pos + 2, 7))
                bgf = make_bg(bgen, 4 * (pos + 1) * 4, 86)
            else:
                bgen, bgf = iter(()), None
            attention(blocks_prompt, 512, background=bgf)
            if dbg_cat is not None:
                dma("sp", dbg_cat[j], catT[:, :, :], ["catT"], [], "dbgc")
                if j == 0:
                    dbg_q = nc.dram_tensor("dbg_q", [128, 4, 512], BF16, kind="ExternalOutput").ap()
                    dbg_k = nc.dram_tensor("dbg_k", [128, 4, 1024], BF16, kind="ExternalOutput").ap()
                    dma("sp", dbg_q, QT[:, :, :], ["QT"], [], "dbgc")
                    dma("sp", dbg_k, KT[:, :, 0:1024], ["KT0", "KT1"], [], "dbgc")
                    dbg_v = nc.dram_tensor("dbg_v", [128, 8, 512], BF16, kind="ExternalOutput").ap()
                    dma("sp", dbg_v, Vt[:, 0:8, :], ["V0", "V1"], [], "dbgc")
            exhaust(bgen)
            srcf = lambda t, pos=pos: xp[pos * GT + t * 128:pos * GT + (t + 1) * 128, :]
            xi0 = ln1_load(0, srcf)
            xi1 = ln1_load(1, srcf)
            out_proj_mm(4)
            if j == NSLOT - 1:
                for c in range(8):
                    dma("pool", W1s[:, :, c * 512:(c + 1) * 512], w1[:, c * 512:(c + 1) * 512].rearrange("(k p) c -> p k c", p=128),
                        [], ["W1_%d" % c] + (KTALL + VALL if c == 0 else []), "W1_%d" % c)
                for c in range(8):
                    dma("pool", W2s[:, c * 4:(c + 1) * 4, :], w2[c * 512:(c + 1) * 512, :].rearrange("(k p) c -> p k c", p=128),
                        [], ["W2_%d" % c], "W2_%d" % c)
            qg = q_conv_g(4, 1, *make_hist(j + 1)) if j + 1 < NSLOT else iter(())
            ln1_chain(0, srcf, j * 4, j * GT, xi0)
            xi2 = ln1_load(2, srcf)
            ln1_chain(1, srcf, j * 4, j * GT, xi1)
            xi3 = ln1_load(3, srcf)
            steps(qg, 3)
            ln1_chain(2, srcf, j * 4, j * GT, xi2)
            steps(qg, 3)
            ln1_chain(3, srcf, j * 4, j * GT, xi3)
            exhaust(qg)

        S.mark(7)
        S.barrier()
        esA.__exit__(None, None, None)
        BAR = S.bar_ids
        GB = sb("GB12", [128, 4, D], F32)
        xres = [sb("xres%d" % i, [128, D], F32) for i in range(2)]
        xb2 = [sb("xb2_0", [128, D], BF16)] * 2
        x1T = sb("x1T", [128, 8, 512], BF16)
        hT = sb("hT", [128, 32, 512], BF16)
        rl = [sb("rl%d" % i, [128, 512], F32) for i in range(2)]
        stb = sb("stb", [128, 16], F32)
        dma("sp", GB[:, :, :], lnB_d[:, 2:6, :], BAR, ["GB12"], "c_GB12")
        W1N = ["W1_%d" % c for c in range(8)]
        W2N = ["W2_%d" % c for c in range(8)]
        xr_state = [0]
        xld = sb("xld", [128, D], F32)

        def grp(g):
            ntile = 4 if g < NSLOT else 1
            return ntile, g * GT

        def prep_load(g, t):
            _, row0 = grp(g)
            dma("sp", xld[:, :], xsc[row0 + t * 128:row0 + (t + 1) * 128, :], ["xsc"], ["xld"], "xld")
            dve(lambda e: e.tensor_copy(out=xb2[0][:, :], in_=xld[:, :]), r=["xld"], w=["xb2_0"])

        def prep_tr(g, t):
            pb = ps_bf(t % 2)

            def tr(e, pb=pb):
                ins = None
                for c in range(8):
                    ins = e.transpose(pb[:, c * 128:(c + 1) * 128], xb2[0][:, c * 128:(c + 1) * 128], ident[:, :])
                return ins
            pe(tr, r=["xb2_0", "ident"], w=[PS[t % 2]])

            def ev(e, pb=pb, t=t):
                ins = None
                for c in range(8):
                    ins = e.activation(out=x1T[:, c, t * 128:(t + 1) * 128], in_=pb[:, c * 128:(c + 1) * 128], func=AF.Identity,
                                       scale=lnT[:, 2, c:c + 1], bias=lnT[:, 3, c:c + 1])
                return ins
            act(ev, r=[PS[t % 2], "lnT"], w=["x1T"])

        for t in range(4):
            prep_load(0, t)
            prep_tr(0, t)
        for g in range(NSLOT + 1):
            ntile, row0 = grp(g)
            n = ntile * 128
            outd = yp if g < NSLOT else ys
            orow0 = row0 if g < NSLOT else 0
            nnext = grp(g + 1)[0] if g < NSLOT else 0
            for fc in range(32):
                bank = 2 + fc % 2

                def mm(e, fc=fc, bank=bank, n=n):
                    ins = None
                    for k in range(8):
                        ins = e.matmul(ps[bank][:, 0:n], lhsT=W1s[:, k, fc * 128:(fc + 1) * 128], rhs=x1T[:, k, 0:n],
                                       start=(k == 0), stop=(k == 7))
                    return ins
                pe(mm, r=[W1N[fc // 4], "x1T"], w=[PS[bank]])
                ri = fc % 2
                act(lambda e, ri=ri, bank=bank, n=n: e.activation(out=rl[ri][:, 0:n], in_=ps[bank][:, 0:n], func=AF.Relu),
                    r=[PS[bank]], w=["rl%d" % ri])
                dve(lambda e, ri=ri, fc=fc, n=n: e.tensor_mul(out=hT[:, fc, 0:n], in0=rl[ri][:, 0:n], in1=rl[ri][:, 0:n]),
                    r=["rl%d" % ri], w=["hT"])
            for t in range(max(ntile, nnext)):
                if t < nnext:
                    prep_load(g + 1, t)
                if t < ntile:
                    i = xr_state[0] % 2
                    xr_state[0] += 1
                    xn_ = "xres%d" % i
                    x_ = xres[i]
                    dma("sp", x_[:, :], xsc[row0 + t * 128:row0 + (t + 1) * 128, :], ["xsc"], [xn_], xn_)
                    dve(lambda e, x_=x_: e.tensor_mul(out=x_[:, :], in0=x_[:, :], in1=GB[:, 0, :]), r=[xn_, "GB12"], w=[xn_])
                    dve(lambda e, x_=x_: e.tensor_add(out=x_[:, :], in0=x_[:, :], in1=GB[:, 1, :]), r=[xn_, "GB12"], w=[xn_])
                    for hf in range(2):
                        bank = 4 + (t % 2) * 2 + hf

                        def mm(e, t=t, hf=hf, bank=bank):
                            ins = None
                            for fc in range(32):
                                ins = e.matmul(ps[bank][:, :], lhsT=hT[:, fc, t * 128:(t + 1) * 128], rhs=W2s[:, fc, hf * 512:(hf + 1) * 512],
                                               start=(fc == 0), stop=(fc == 31))
                            return ins
                        pe(mm, r=W2N + ["hT"], w=[PS[bank]])
                if t < nnext:
                    prep_tr(g + 1, t)
                if t < ntile:
                    for hf in range(2):
                        bank = 4 + (t % 2) * 2 + hf
                        dve(lambda e, x_=x_, hf=hf, bank=bank: e.scalar_tensor_tensor(out=x_[:, hf * 512:(hf + 1) * 512], in0=x_[:, hf * 512:(hf + 1) * 512],
                                                                                      scalar=ALPHA, in1=ps[bank][:, :], op0=ALU.mult, op1=ALU.add),
                            r=[xn_, PS[bank]], w=[xn_])
                    dve(lambda e, x_=x_: e.bn_stats(stb[:, 0:6], x_[:, 0:512]), r=[xn_], w=["stb"])
                    dve(lambda e, x_=x_: e.bn_stats(stb[:, 6:12], x_[:, 512:1024]), r=[xn_], w=["stb"])
                    dve(lambda e: e.bn_aggr(stb[:, 12:14], stb[:, 0:12]), r=["stb"], w=["stb2"])
                    rstd_from_var(stb[:, 13:14], stb[:, 13:14], ["stb2"], ["stb2"])
                    dve(lambda e, x_=x_: e.tensor_scalar(out=x_[:, :], in0=x_[:, :], scalar1=stb[:, 12:13], scalar2=stb[:, 13:14],
                                                         op0=ALU.subtract, op1=ALU.mult), r=[xn_, "stb2"], w=[xn_])
                    dve(lambda e, x_=x_: e.tensor_mul(out=x_[:, :], in0=x_[:, :], in1=GB[:, 2, :]), r=[xn_, "GB12"], w=[xn_])
                    dve(lambda e, x_=x_: e.tensor_add(out=x_[:, :], in0=x_[:, :], in1=GB[:, 3, :]), r=[xn_, "GB12"], w=[xn_])
                    dma("pool", outd[orow0 + t * 128:orow0 + (t + 1) * 128, :], x_[:, :], [xn_], [], "xo%d" % i)

        S.emit(nc, es)
    return nc


_NC_CACHE = {}
_RUNNER = [None]


def _get_nc(NG, PAST):
    if (NG, PAST) not in _NC_CACHE:
        _NC_CACHE[(NG, PAST)] = build_nc(NG, PAST)
    return _NC_CACHE[(NG, PAST)]


def kernel(x_prompt, x_sample, cache_k, cache_v, cache_conv, ln0_g, ln0_b, rel_bias, w_in, conv_w,
           lambda_q1, lambda_k1, lambda_q2, lambda_k2, subln_g, w_out, ln1_g, ln1_b,
           w_ff1, w_ff2, ln2_g, ln2_b):
    f = lambda a: np.ascontiguousarray(np.asarray(a, dtype=np.float32))
    x_prompt, x_sample, cache_k, cache_v, cache_conv = map(f, (x_prompt, x_sample, cache_k, cache_v, cache_conv))
    NB, SEQ = x_prompt.shape[0], x_prompt.shape[1]
    NG = SEQ // GT
    NSLOT = NG // 2
    PAST = cache_k.shape[2]
    NCORES = 2 * NB
    assert x_sample.shape[0] == 2 * NCORES and x_sample.shape[1] == DEC
    vecs = [f(v).reshape(-1) for v in (ln0_g, ln0_b, ln1_g, ln1_b, ln2_g, ln2_b)]
    lnT = np.ascontiguousarray(np.stack([v.reshape(8, 128).T for v in vecs], axis=1))
    lnB = np.ascontiguousarray(np.broadcast_to(np.stack(vecs, 0)[None], (128, 6, D)))
    cw = f(conv_w)[0]
    convw = np.ascontiguousarray(cw.reshape(3, 4, 128).transpose(2, 1, 0))
    subg = f(subln_g).reshape(128, 1)
    lam = np.stack([f(lambda_q1)[0], f(lambda_k1)[0], f(lambda_q2)[0], f(lambda_k2)[0]], 0)
    lamv = np.ascontiguousarray(np.broadcast_to(lam[None], (128, 4, 64)))
    rb = f(rel_bias)
    bt = _bucket_table()
    oh = np.zeros((32, 384), np.float32)
    oh[bt, np.arange(384)] = 1.0
    oh[15, :] -= 1.0
    w_in_, w_out_, w1_, w2_ = f(w_in)[0], f(w_out)[0], f(w_ff1)[0], f(w_ff2)[0]

    in_maps = []
    orders = [_order(0, NG), _order(1, NG)]
    for c in range(NCORES):
        b, half = c // 2, c % 2
        order = orders[half]
        xb = x_prompt[b].reshape(NG, GT, D)
        xp = np.ascontiguousarray(xb[order].reshape(SEQ, D))
        xh = np.zeros((NSLOT, 2, D), np.float32)
        flags = np.zeros((128, 16), np.float32)
        for j in range(NSLOT):
            g = order[2 * j + 1]
            if g > 0:
                xh[j] = x_prompt[b, g * GT - 2:g * GT]
                flags[:, j] = 1.0
        for s in range(2):
            sa = 1.0 if (s + half) % 2 == 0 else 0.0
            flags[:, 8 + s] = sa
            flags[:, 10 + s] = sa
        xs = np.ascontiguousarray(x_sample[2 * c:2 * c + 2].reshape(128, D))
        ck = np.ascontiguousarray(cache_k[0, 2 * c:2 * c + 2].reshape(2, PAST, 512))
        cv = np.ascontiguousarray(cache_v[0, 2 * c:2 * c + 2].reshape(2, PAST, 512))
        ccv = cache_conv[0, 2 * c:2 * c + 2]
        cc = np.ascontiguousarray(ccv.reshape(2, 2, 4, 128).transpose(3, 2, 0, 1))
        in_maps.append(dict(xp=xp, xh=xh, xs=xs, ck=ck, cv=cv, cc=cc, w_in=w_in_, w_out=w_out_, w1=w1_, w2=w2_,
                            lnT=lnT, lnB=lnB, convw=convw, subg=subg, lamv=lamv, rb=rb, oh=oh, flags=flags))

    nc = _get_nc(NG, PAST)
    if _RUNNER[0] is not None:
        R = _RUNNER[0](nc, in_maps)
    else:
        R = run_bass_kernel_spmd(nc, in_maps, core_ids=list(range(NCORES))).results
    if os.environ.get("KDBG"):
        _RUNNER.append(R)

    y_prompt = np.zeros((NB, SEQ, D), np.float32)
    y_sample = np.zeros((2 * NCORES, DEC, D), np.float32)
    nk_p = np.zeros((1, NB, SEQ, 4, 128), np.float32)
    nv_p = np.zeros((1, NB, SEQ, 4, 128), np.float32)
    nc_p = np.zeros((1, NB, 2, 512), np.float32)
    nk_s = np.zeros((1, 2 * NCORES, DEC, 4, 128), np.float32)
    nv_s = np.zeros((1, 2 * NCORES, DEC, 4, 128), np.float32)
    nc_s = np.zeros((1, 2 * NCORES, 2, 512), np.float32)
    for c in range(NCORES):
        b, half = c // 2, c % 2
        order = orders[half]
        r = R[c]
        for j in range(NSLOT):
            g = order[2 * j + 1]
            y_prompt[b, g * GT:(g + 1) * GT] = r["yp"][j * GT:(j + 1) * GT]
            nk_p[0, b, g * GT:(g + 1) * GT] = r["kp"][j * GT:(j + 1) * GT].reshape(GT, 4, 128)
            nv_p[0, b, g * GT:(g + 1) * GT] = r["vp"][j * GT:(j + 1) * GT].reshape(GT, 4, 128)
        if order[NG - 1] == NG - 1:
            nc_p[0, b] = r["cp"].transpose(2, 1, 0).reshape(2, 512)
        y_sample[2 * c:2 * c + 2] = r["ys"].reshape(2, DEC, D)
        nk_s[0, 2 * c:2 * c + 2] = r["ksn"].reshape(2, DEC, 4, 128)
        nv_s[0, 2 * c:2 * c + 2] = r["vsn"].reshape(2, DEC, 4, 128)
        nc_s[0, 2 * c:2 * c + 2] = r["csn"].transpose(2, 3, 1, 0).reshape(2, 2, 512)
    return (y_prompt, y_sample, nk_p, nv_p, nc_p, nk_s, nv_s, nc_s)
```
